# Optimizing a Trainium2 kernel written in Bass

```python
import jax
import jax.numpy as jnp
from jax import lax
import numpy as np

D_MODEL = 1024
BATCH = 16
SEQ = 4096
DEPTH = 4

GRID_W = 64
CTX_LEN = 256
N_MIXERS = 3
N_ATTN_LAYERS = (DEPTH + 2) // N_MIXERS
N_RWKV_LAYERS = (DEPTH + 1) // N_MIXERS
N_POOL_LAYERS = DEPTH // N_MIXERS
N_MOD = 6
EPS = 1e-6
N_HEADS = 16
N_KV_HEADS = 4
HEAD_DIM = D_MODEL // N_HEADS
GQA_REP = N_HEADS // N_KV_HEADS
Q_WIDTH = N_HEADS * HEAD_DIM
KV_WIDTH = N_KV_HEADS * HEAD_DIM
QKV_WIDTH = Q_WIDTH + 2 * KV_WIDTH
ROPE_THETA = 10000.0
Q_BLOCK = 128
RWKV_HEAD = 64
RWKV_HEADS = D_MODEL // RWKV_HEAD
DECAY_LORA = 64
ICLR_LORA = 64
GATE_LORA = 160
GN_EPS = RWKV_HEAD * 1e-5
N_DIRS = 2
POOL_WINDOWS = (2, 4, 8, 16)
POOL_GROUP = D_MODEL // len(POOL_WINDOWS)
D_FF = 4 * D_MODEL

kernel_name = "hybrid_attn_rwkv7_pool_dit"


def _rmsnorm(x, g):
    xf = x.astype(jnp.float32)
    y = xf * lax.rsqrt(jnp.mean(xf * xf, axis=-1, keepdims=True) + EPS)
    return (y * g.astype(jnp.float32)).astype(x.dtype)


def _axial_rope(n_tokens):
    rows = n_tokens // GRID_W
    n_freq = HEAD_DIM // 4
    inv = ROPE_THETA ** (-jnp.arange(n_freq, dtype=jnp.float32) / n_freq)
    ang_r = jnp.arange(rows, dtype=jnp.float32)[:, None] * inv
    ang_c = jnp.arange(GRID_W, dtype=jnp.float32)[:, None] * inv
    ang = jnp.concatenate([
        jnp.broadcast_to(ang_r[:, None, :], (rows, GRID_W, n_freq)),
        jnp.broadcast_to(ang_c[None, :, :], (rows, GRID_W, n_freq))], axis=-1).reshape(rows * GRID_W, 2 * n_freq)
    return jnp.cos(ang), jnp.sin(ang)


def _rope(x, cos, sin):
    half = HEAD_DIM // 2
    xf = x.astype(jnp.float32)
    x1, x2 = xf[..., :half], xf[..., half:]
    c = cos[None, :, None, :]
    s = sin[None, :, None, :]
    return jnp.concatenate([x1 * c - x2 * s, x1 * s + x2 * c], axis=-1).astype(x.dtype)


def _attn_project(h, w_qkv, q_gain, k_gain):
    B, T, _ = h.shape
    qkv = h @ w_qkv
    q = qkv[..., :Q_WIDTH].reshape(B, T, N_HEADS, HEAD_DIM)
    k = qkv[..., Q_WIDTH:Q_WIDTH + KV_WIDTH].reshape(B, T, N_KV_HEADS, HEAD_DIM)
    v = qkv[..., Q_WIDTH + KV_WIDTH:].reshape(B, T, N_KV_HEADS, HEAD_DIM)
    return _rmsnorm(q, q_gain), _rmsnorm(k, k_gain), v


def attention_mixer(h_lat, h_ctx, w_qkv, q_gain, k_gain, w_o, with_ctx_out):
    B, S, _ = h_lat.shape
    L = h_ctx.shape[1]
    scale = HEAD_DIM ** -0.5
    cos, sin = _axial_rope(S)
    q_l, k_l, v_l = _attn_project(h_lat, w_qkv, q_gain, k_gain)
    q_l = _rope(q_l, cos, sin)
    k_l = _rope(k_l, cos, sin)
    q_c, k_c, v_c = _attn_project(h_ctx, w_qkv, q_gain, k_gain)
    n_blk = S // Q_BLOCK
    q_blocks = jnp.moveaxis(q_l.reshape(B, n_blk, Q_BLOCK, N_KV_HEADS, GQA_REP, HEAD_DIM), 1, 0)

    def attend_block(qb):
        s = jnp.concatenate([
            jnp.einsum("bqgrd,bsgd->bgrqs", qb, k_l),
            jnp.einsum("bqgrd,bcgd->bgrqc", qb, k_c)], axis=-1).astype(jnp.float32) * scale
        p = jax.nn.softmax(s, axis=-1).astype(v_l.dtype)
        return (jnp.einsum("bgrqs,bsgd->bqgrd", p[..., :S], v_l)
                + jnp.einsum("bgrqc,bcgd->bqgrd", p[..., S:], v_c))

    o = lax.map(attend_block, q_blocks)
    out_l = jnp.moveaxis(o, 0, 1).reshape(B, S, Q_WIDTH) @ w_o
    out_c = None
    if with_ctx_out:
        qc = q_c.reshape(B, L, N_KV_HEADS, GQA_REP, HEAD_DIM)
        s = jnp.einsum("bqgrd,bcgd->bgrqc", qc, k_c).astype(jnp.float32) * scale
        p = jax.nn.softmax(s, axis=-1).astype(v_c.dtype)
        out_c = jnp.einsum("bgrqc,bcgd->bqgrd", p, v_c).reshape(B, L, Q_WIDTH) @ w_o
    return out_l, out_c


def _centred_shift(h):
    zero = jnp.zeros_like(h[:, :1])
    prev = jnp.concatenate([zero, h[:, :-1]], axis=1)
    nxt = jnp.concatenate([h[:, 1:], zero], axis=1)
    return 0.5 * (prev + nxt) - h


def _heads(t):
    return t.reshape(t.shape[0], t.shape[1], RWKV_HEADS, RWKV_HEAD)


def _rwkv_prepare(h, mu, w_rkv, w0, w1, w2, a0, a1, a2, g1, g2, k_k, k_a):
    xx = _centred_shift(h)
    xr, xw, xk, xv, xa, xg = [h + xx * mu[m] for m in range(6)]
    r = xr @ w_rkv[0]
    k = xk @ w_rkv[1]
    v = xv @ w_rkv[2]
    g = jax.nn.sigmoid(xg @ g1) @ g2
    kk = _heads((k * k_k).astype(jnp.float32))
    kk = kk / jnp.maximum(jnp.sqrt(jnp.sum(kk * kk, axis=-1, keepdims=True)), 1e-12)
    decays, keys, iclrs = [], [], []
    for d in range(N_DIRS):
        w_log = -jax.nn.softplus(-(w0[d] + jnp.tanh(xw @ w1[d]) @ w2[d])) - 0.5
        decays.append(_heads(jnp.exp(-jnp.exp(w_log.astype(jnp.float32)))))
        a = jax.nn.sigmoid(a0[d] + (xa @ a1[d]) @ a2[d])
        keys.append(_heads(k * (1.0 + (a - 1.0) * k_a)))
        iclrs.append(_heads(a))
    return _heads(r), _heads(v), kk, g, decays, keys, iclrs


def _wkv_scan(state0, r, decay, k, v, kk, a, reverse):
    xs = tuple(jnp.moveaxis(t.astype(jnp.float32), 1, 0) for t in (r, decay, k, v, kk, a))

    def step(state, inp):
        r_t, w_t, k_t, v_t, kk_t, a_t = inp
        sa = jnp.einsum("bhvk,bhk->bhv", state, kk_t)
        state = (state * w_t[:, :, None, :]
                 - sa[..., None] * (kk_t * a_t)[:, :, None, :]
                 + v_t[..., None] * k_t[:, :, None, :])
        return state, jnp.einsum("bhvk,bhk->bhv", state, r_t)

    state, ys = lax.scan(step, state0, xs, reverse=reverse)
    return state, jnp.moveaxis(ys, 0, 1)


def _rwkv_finish(y, r, v, keys, r_k, g, ln_g, ln_b, w_o):
    B, T = y.shape[:2]
    mean = jnp.mean(y, axis=-1, keepdims=True)
    var = jnp.mean(jnp.square(y - mean), axis=-1, keepdims=True)
    y = ((y - mean) * lax.rsqrt(var + GN_EPS)).reshape(B, T, D_MODEL)
    y = y * ln_g.astype(jnp.float32) + ln_b.astype(jnp.float32)
    k_sum = (keys[0] + keys[1]).astype(jnp.float32)
    rk = r_k.reshape(RWKV_HEADS, RWKV_HEAD).astype(jnp.float32)
    bonus = jnp.sum(r.astype(jnp.float32) * k_sum * rk, axis=-1, keepdims=True) * v.astype(jnp.float32)
    y = y + bonus.reshape(B, T, D_MODEL)
    return (y.astype(g.dtype) * g) @ w_o


def rwkv_mixer(h_lat, h_ctx, mu, w_rkv, w0, w1, w2, a0, a1, a2, g1, g2, k_k, k_a, r_k, ln_g, ln_b, w_o,
               with_ctx_out):
    prep = (mu, w_rkv, w0, w1, w2, a0, a1, a2, g1, g2, k_k, k_a)
    r_l, v_l, kk_l, g_l, dec_l, key_l, a_l = _rwkv_prepare(h_lat, *prep)
    r_c, v_c, kk_c, g_c, dec_c, key_c, a_c = _rwkv_prepare(h_ctx, *prep)
    B = h_lat.shape[0]
    zero = jnp.zeros((B, RWKV_HEADS, RWKV_HEAD, RWKV_HEAD), jnp.float32)
    ys_l, ys_c = [], []
    for d in range(N_DIRS):
        rev = d == 1
        s_ctx, yc = _wkv_scan(zero, r_c, dec_c[d], key_c[d], v_c, kk_c, a_c[d], rev)
        _, yl = _wkv_scan(s_ctx, r_l, dec_l[d], key_l[d], v_l, kk_l, a_l[d], rev)
        ys_l.append(yl)
        ys_c.append(yc)
    out_l = _rwkv_finish(ys_l[0] + ys_l[1], r_l, v_l, key_l, r_k, g_l, ln_g, ln_b, w_o)
    out_c = None
    if with_ctx_out:
        out_c = _rwkv_finish(ys_c[0] + ys_c[1], r_c, v_c, key_c, r_k, g_c, ln_g, ln_b, w_o)
    return out_l, out_c


def pool_mixer(h, w_group, scale):
    B, T, _ = h.shape
    hf = h.astype(jnp.float32)
    cs = jnp.concatenate([jnp.zeros((B, 1, D_MODEL), jnp.float32), jnp.cumsum(hf, axis=1)], axis=1)
    t = jnp.arange(T)
    outs = []
    for gi, win in enumerate(POOL_WINDOWS):
        sl = slice(gi * POOL_GROUP, (gi + 1) * POOL_GROUP)
        lo = jnp.clip(t - win // 2, 0, T)
        hi = jnp.clip(t + win // 2, 0, T)
        cs_g = cs[..., sl]
        sums = jnp.take(cs_g, hi, axis=1) - jnp.take(cs_g, lo, axis=1)
        cnt = (hi - lo).astype(jnp.float32)[None, :, None]
        pooled = (sums / cnt - hf[..., sl]).astype(h.dtype)
        outs.append(jnp.einsum("btc,cd->btd", pooled, w_group[gi]))
    return jnp.concatenate(outs, axis=-1) * scale


def _sq_relu_mlp(h, w_in, w_out):
    u = jax.nn.relu(h @ w_in)
    return (u * u) @ w_out


def setup_inputs(seed: int = 0) -> dict:
    key = jax.random.key(seed)
    ks = iter(jax.random.split(key, 40))

    def nrm(shape, s):
        return jax.random.normal(next(ks), shape, jnp.float32) * s

    def unif(shape, lo, hi):
        return jax.random.uniform(next(ks), shape, jnp.float32, lo, hi)

    D = D_MODEL
    return {
        "x": nrm((BATCH, SEQ, D), 1.0),
        "c": nrm((BATCH, D), 1.0),
        "ctx": nrm((BATCH, CTX_LEN, D), 1.0),
        "c_ctx": nrm((D,), 1.0),
        "w_mod": nrm((DEPTH, D, N_MOD * D), 0.5 * D ** -0.5),
        "b_mod": nrm((DEPTH, N_MOD * D), 0.02),
        "norm1_g": 1.0 + nrm((DEPTH, D), 0.02),
        "norm2_g": 1.0 + nrm((DEPTH, D), 0.02),
        "mlp_w_in": nrm((DEPTH, D, D_FF), D ** -0.5),
        "mlp_w_out": nrm((DEPTH, D_FF, D), D_FF ** -0.5),
        "attn_w_qkv": nrm((N_ATTN_LAYERS, D, QKV_WIDTH), D ** -0.5),
        "attn_q_gain": 1.0 + nrm((N_ATTN_LAYERS, HEAD_DIM), 0.02),
        "attn_k_gain": 1.0 + nrm((N_ATTN_LAYERS, HEAD_DIM), 0.02),
        "attn_w_o": nrm((N_ATTN_LAYERS, Q_WIDTH, D), Q_WIDTH ** -0.5),
        "rwkv_mu": unif((N_RWKV_LAYERS, 6, D), 0.0, 1.0),
        "rwkv_w_rkv": nrm((N_RWKV_LAYERS, 3, D, D), D ** -0.5),
        "rwkv_w0": unif((N_RWKV_LAYERS, N_DIRS, D), -5.0, 0.0),
        "rwkv_w1": nrm((N_RWKV_LAYERS, N_DIRS, D, DECAY_LORA), D ** -0.5),
        "rwkv_w2": nrm((N_RWKV_LAYERS, N_DIRS, DECAY_LORA, D), 0.1 * DECAY_LORA ** -0.5),
        "rwkv_a0": nrm((N_RWKV_LAYERS, N_DIRS, D), 0.5),
        "rwkv_a1": nrm((N_RWKV_LAYERS, N_DIRS, D, ICLR_LORA), D ** -0.5),
        "rwkv_a2": nrm((N_RWKV_LAYERS, N_DIRS, ICLR_LORA, D), 0.5 * ICLR_LORA ** -0.5),
        "rwkv_g1": nrm((N_RWKV_LAYERS, D, GATE_LORA), D ** -0.5),
        "rwkv_g2": nrm((N_RWKV_LAYERS, GATE_LORA, D), GATE_LORA ** -0.5),
        "rwkv_k_k": 0.85 + nrm((N_RWKV_LAYERS, D), 0.05),
        "rwkv_k_a": 1.0 + nrm((N_RWKV_LAYERS, D), 0.05),
        "rwkv_r_k": nrm((N_RWKV_LAYERS, D), 0.1),
        "rwkv_ln_g": 1.0 + nrm((N_RWKV_LAYERS, D), 0.02),
        "rwkv_ln_b": nrm((N_RWKV_LAYERS, D), 0.02),
        "rwkv_w_o": nrm((N_RWKV_LAYERS, D, D), D ** -0.5),
        "pool_w": nrm((N_POOL_LAYERS, len(POOL_WINDOWS), POOL_GROUP, POOL_GROUP), POOL_GROUP ** -0.5),
        "pool_scale": unif((N_POOL_LAYERS, D), 0.5, 1.5),
    }


def reference(x, c, ctx, c_ctx, w_mod, b_mod, norm1_g, norm2_g, mlp_w_in, mlp_w_out,
              attn_w_qkv, attn_q_gain, attn_k_gain, attn_w_o,
              rwkv_mu, rwkv_w_rkv, rwkv_w0, rwkv_w1, rwkv_w2, rwkv_a0, rwkv_a1, rwkv_a2,
              rwkv_g1, rwkv_g2, rwkv_k_k, rwkv_k_a, rwkv_r_k, rwkv_ln_g, rwkv_ln_b, rwkv_w_o,
              pool_w, pool_scale):
    B = x.shape[0]
    silu_c = jax.nn.silu(c)
    silu_cc = jax.nn.silu(c_ctx)
    for i in range(DEPTH):
        last = i == DEPTH - 1
        j = i // N_MIXERS
        mod_l = (silu_c @ w_mod[i] + b_mod[i]).reshape(B, N_MOD, 1, D_MODEL)
        mod_c = (silu_cc @ w_mod[i] + b_mod[i]).reshape(N_MOD, 1, 1, D_MODEL)
        h_l = _rmsnorm(x, norm1_g[i]) * (1.0 + mod_l[:, 1]) + mod_l[:, 0]
        h_c = _rmsnorm(ctx, norm1_g[i]) * (1.0 + mod_c[1]) + mod_c[0]
        kind = i % N_MIXERS
        if kind == 0:
            y_l, y_c = attention_mixer(h_l, h_c, attn_w_qkv[j], attn_q_gain[j], attn_k_gain[j], attn_w_o[j],
                                       not last)
        elif kind == 1:
            y_l, y_c = rwkv_mixer(h_l, h_c, rwkv_mu[j], rwkv_w_rkv[j], rwkv_w0[j], rwkv_w1[j], rwkv_w2[j],
                                  rwkv_a0[j], rwkv_a1[j], rwkv_a2[j], rwkv_g1[j], rwkv_g2[j], rwkv_k_k[j],
                                  rwkv_k_a[j], rwkv_r_k[j], rwkv_ln_g[j], rwkv_ln_b[j], rwkv_w_o[j], not last)
        else:
            y_l = pool_mixer(h_l, pool_w[j], pool_scale[j])
            y_c = None if last else pool_mixer(h_c, pool_w[j], pool_scale[j])
        x = x + mod_l[:, 2] * y_l
        h2 = _rmsnorm(x, norm2_g[i]) * (1.0 + mod_l[:, 4]) + mod_l[:, 3]
        x = x + mod_l[:, 5] * _sq_relu_mlp(h2, mlp_w_in[i], mlp_w_out[i])
        if not last:
            ctx = ctx + mod_c[2] * y_c
            h2c = _rmsnorm(ctx, norm2_g[i]) * (1.0 + mod_c[4]) + mod_c[3]
            ctx = ctx + mod_c[5] * _sq_relu_mlp(h2c, mlp_w_in[i], mlp_w_out[i])
    return x
```

```python
from contextlib import ExitStack
import numpy as np
import concourse.bass as bass
import concourse.mybir as mybir
from concourse.bass_utils import run_bass_kernel_spmd

F32 = mybir.dt.float32
BF16 = mybir.dt.bfloat16
AF = mybir.ActivationFunctionType
ALU = mybir.AluOpType
AX = mybir.AxisListType

NCORES = 8
NB = 2
LAT = 4096
CTXL = 256
SEG = LAT + CTXL
NTOK = NB * SEG
D = 1024
KC = 8
DFF = 4096
NSC = SEG // 128
EPS = 1e-6
DEPTH = 4
DEC_C = float(np.exp(-0.5))


class Prog:
    N_DMA_SLOTS = 8
    EPOCH = 30000

    def __init__(self, nc, es):
        self.nc = nc
        self.es = es
        self.eng = {"pe": nc.tensor, "act": nc.scalar, "dve": nc.vector, "pool": nc.gpsimd, "sp": nc.sync}
        self.sem = {}
        self.cnt = {}
        self.cur = {}
        self.epoch = {}
        for e in ["pe", "act", "dve", "pool"]:
            self.epoch[e] = 0
            self._new_epoch(e)
        self.dma_slots = {}
        for q in ["sp", "pool", "act"]:
            lst = []
            for i in range(self.N_DMA_SLOTS):
                name = "d_%s%d" % (q, i)
                self.sem[name] = es.enter_context(nc.semaphore(name))
                self.cnt[name] = 0
                lst.append(name)
            self.dma_slots[q] = lst
        self.dma_rr = {"sp": 0, "pool": 0, "act": 0}
        self.waited = {}
        self.last_w = {}
        self.readers = {}
        self.n_inst = 0
        self.n_wait = 0
        self.ps_tiles = [es.enter_context(nc.psum_tensor("psb%d" % i, [128, 512], F32)) for i in range(8)]
        self.ps_rr = 0

    def _new_epoch(self, e):
        name = "%s#%d" % (e, self.epoch[e])
        self.epoch[e] += 1
        self.sem[name] = self.es.enter_context(self.nc.semaphore("s_" + name.replace("#", "_")))
        self.cnt[name] = 0
        self.cur[e] = name

    def next_ps(self):
        i = self.ps_rr % 8
        self.ps_rr += 1
        return self.ps_tiles[i], ("ps", i)

    def _deps(self, me, reads, writes):
        deps = {}

        def add(t, s):
            if deps.get(t, 0) < s:
                deps[t] = s

        for r in reads:
            if r in self.last_w:
                add(*self.last_w[r])
        for w in writes:
            if w in self.last_w:
                add(*self.last_w[w])
            for t, s in self.readers.get(w, {}).items():
                if t.split("#")[0] == me:
                    continue
                add(t, s)
        return deps

    def _wait(self, me, deps):
        e = self.eng[me]
        for t, s in deps.items():
            if me == "pe" and t.split("#")[0] == "pe":
                continue
            if self.waited.get((me, t), 0) >= s:
                continue
            e.wait_ge(self.sem[t], s)
            self.waited[(me, t)] = s
            self.n_wait += 1

    def _commit(self, tok, reads, writes):
        t, s = tok
        for r in reads:
            d = self.readers.setdefault(r, {})
            if d.get(t, 0) < s:
                d[t] = s
        for w in writes:
            self.last_w[w] = tok
            self.readers[w] = {}

    def op(self, me, fn, reads=(), writes=()):
        self._wait(me, self._deps(me, reads, writes))
        inst = fn(self.eng[me])
        name = self.cur[me]
        self.cnt[name] += 1
        inst.then_inc(self.sem[name], 1)
        self._commit((name, self.cnt[name]), reads, writes)
        if self.cnt[name] >= self.EPOCH:
            self._new_epoch(me)
        self.n_inst += 1
        return inst

    def dma(self, q, out, in_, reads=(), writes=(), **kw):
        deps = self._deps("dma", reads, writes)
        slot = self.dma_slots[q][self.dma_rr[q] % self.N_DMA_SLOTS]
        self.dma_rr[q] += 1
        if self.cnt[slot] > 0 and deps.get(slot, 0) < self.cnt[slot]:
            deps[slot] = self.cnt[slot]
        if self.cnt[slot] >= 16 * 1800:
            raise RuntimeError("dma slot counter too large")
        self._wait(q, deps)
        inst = self.eng[q].dma_start(out=out, in_=in_, **kw)
        self.cnt[slot] += 16
        inst.then_inc(self.sem[slot], 16)
        self._commit((slot, self.cnt[slot]), reads, writes)
        self.n_inst += 1
        return inst

    def barrier(self):
        for me in ["sp", "pool", "act", "dve", "pe"]:
            deps = {t: c for t, c in self.cnt.items() if c > 0}
            if me == "pe":
                deps = {t: c for t, c in deps.items() if t.split("#")[0] != "pe"}
            self._wait(me, deps)
        self.last_w = {}
        self.readers = {}

    def mm(self, out, lhsT, rhs, start, stop, reads, writes):
        return self.op("pe", lambda e: e.matmul(out, lhsT=lhsT, rhs=rhs, start=start, stop=stop), reads, writes)

    def tr(self, out, in_, ident, reads, writes):
        return self.op("pe", lambda e: e.transpose(out=out, in_=in_, identity=ident), reads, writes)

    def act(self, out, in_, func, reads, writes, **kw):
        return self.op("act", lambda e: e.activation(out=out, in_=in_, func=func, **kw), reads, writes)

    def tt(self, eng, out, in0, in1, op, reads, writes):
        return self.op(eng, lambda e: e.tensor_tensor(out=out, in0=in0, in1=in1, op=op), reads, writes)

    def ts(self, eng, out, in0, s1, s2, op0, op1, reads, writes):
        if op1 is None:
            return self.op(eng, lambda e: e.tensor_scalar(out=out, in0=in0, scalar1=s1, scalar2=None, op0=op0),
                           reads, writes)
        return self.op(eng, lambda e: e.tensor_scalar(out=out, in0=in0, scalar1=s1, scalar2=s2, op0=op0, op1=op1),
                       reads, writes)

    def stt(self, out, in0, scalar, in1, op0, op1, reads, writes):
        return self.op("dve", lambda e: e.scalar_tensor_tensor(out=out, in0=in0, scalar=scalar, in1=in1,
                                                               op0=op0, op1=op1), reads, writes)

    def copy(self, eng, out, in_, reads, writes):
        if eng == "act":
            return self.act(out, in_, AF.Copy, reads, writes)
        return self.op(eng, lambda e: e.tensor_copy(out=out, in_=in_), reads, writes)


class Stage:
    _n = [0]

    def __init__(self, P):
        self.P = P
        self.es = ExitStack()
        Stage._n[0] += 1
        self.tag = "s%d_" % Stage._n[0]

    def __enter__(self):
        self.es.__enter__()
        return self

    def sb(self, name, shape, dt=F32):
        return self.es.enter_context(self.P.nc.sbuf_tensor(self.tag + name, list(shape), dt))

    def __exit__(self, *a):
        self.P.barrier()
        return self.es.__exit__(*a)


def token_blocks(with_ctx=True):
    out = []
    for b in range(NB):
        if with_ctx:
            out.append((b * SEG, CTXL, b, True, 0))
        for j in range(LAT // 512):
            out.append((b * SEG + CTXL + j * 512, 512, b, False, j * 512))
    return out


VO = {}
_o = 0
for _n, _w in [("bmod", 4 * 48), ("g1", 32), ("g2", 32), ("qg", 2), ("kg", 2), ("pscale", 8), ("mu", 48),
               ("w0", 16), ("a0", 16), ("r_k", 8), ("ln_g", 8), ("ln_b", 8)]:
    VO[_n] = _o
    _o += _w
NV = _o


def emit_norm(P, G, x_t, xkey, nt, L, which, n, h_t, hkey, tmp, sq, rstd, sqkeys=None):
    sc = G["sc1"] if which == 1 else G["sc2"]
    shm = 0 if which == 1 else 3
    sqk = sqkeys if sqkeys is not None else ["sq"] * KC
    P.act(sq[:, 0:KC, :nt], x_t[:, :, :nt], AF.Square, [xkey], list(set(sqk)))
    ps, pk = P.next_ps()
    for c in range(KC):
        P.mm(ps[:, :nt], G["ones_b"][:], sq[:, c, :nt], c == 0, c == KC - 1, [sqk[c]], [pk])
    if rstd is None:
        rs_t, rs_k = P.next_ps()
        P.act(rs_t[:, :nt], ps[:, :nt], AF.Ln, [pk], [rs_k], bias=G["eps_t"][:, 0:1], scale=1.0 / D)
        P.act(rs_t[:, :nt], rs_t[:, :nt], AF.Exp, [rs_k], [rs_k], scale=-0.5)
        for c in range(KC):
            tp, tk = P.next_ps()
            if tp is rs_t:
                tp, tk = P.next_ps()
            P.tt("dve", tp[:, :nt], x_t[:, c, :nt], rs_t[:, :nt], ALU.mult, [xkey, rs_k], [tk])
            P.act(h_t[:, c, :nt], tp[:, :nt], AF.Identity, [tk], [hkey],
                  scale=sc[L][:, c, n:n + 1], bias=G["mod"][L][:, shm * 8 + c, n:n + 1])
        return
    P.act(rstd[:, :nt], ps[:, :nt], AF.Ln, [pk], ["rstd"], bias=G["eps_t"][:, 0:1], scale=1.0 / D)
    P.act(rstd[:, :nt], rstd[:, :nt], AF.Exp, ["rstd"], ["rstd"], scale=-0.5)
    for c in range(KC):
        tk = ("ntmp", c % 2)
        P.tt("dve" if c % 2 == 0 else "pool", tmp[c % 2][:, :nt], x_t[:, c, :nt], rstd[:, :nt], ALU.mult,
             [xkey, "rstd"], [tk])
        P.act(h_t[:, c, :nt], tmp[c % 2][:, :nt], AF.Identity, [tk], [hkey],
              scale=sc[L][:, c, n:n + 1], bias=G["mod"][L][:, shm * 8 + c, n:n + 1])


def load_w(P, w_sb, w_dram, kcs, key, q="pool"):
    wv = w_dram.rearrange("(kc p) o -> p kc o", p=128)
    for kc in range(kcs):
        P.dma(q, w_sb[:, kc, :], wv[:, kc, :], [], [key])


def mod_steps(P, G, T, S, layers):
    wt = [S.sb("wmod%d" % i, [128, KC, 512]) for i in range(2)]
    sil = G["sil"]
    items = [(L, blk) for L in layers for blk in range(12)]

    def load(i):
        L, blk = items[i]
        wv = T["w_mod"][L].rearrange("(kc p) o -> p kc o", p=128)
        P.dma("sp", wt[i % 2][:], wv[:, :, blk * 512:(blk + 1) * 512], [], [("wmod", i % 2)])

    if items:
        load(0)
        yield
    for i, (L, blk) in enumerate(items):
        if i + 1 < len(items):
            load(i + 1)
        w = wt[i % 2]
        wk = ("wmod", i % 2)
        for mo in range(4):
            j = blk * 4 + mo
            ps, pk = P.next_ps()
            for kc in range(KC):
                P.mm(ps[:, 0:3], w[:, kc, mo * 128:(mo + 1) * 128], sil[:, kc, :], kc == 0, kc == KC - 1,
                     [wk, "sil"], [pk])
            P.ts("dve", G["mod"][L][:, j, :], ps[:, 0:3], G["vec"][:, VO["bmod"] + L * 48 + j:VO["bmod"] + L * 48 + j + 1],
                 None, ALU.add, None, [pk], [("mod", L)])
        if blk == 11:
            for n in range(3):
                P.stt(G["sc1"][L][:, :, n], G["mod"][L][:, 8:16, n], 1.0, G["vec"][:, VO["g1"] + L * 8:VO["g1"] + L * 8 + 8],
                      ALU.add, ALU.mult, [("mod", L)], [("mod", L)])
                P.stt(G["sc2"][L][:, :, n], G["mod"][L][:, 32:40, n], 1.0, G["vec"][:, VO["g2"] + L * 8:VO["g2"] + L * 8 + 8],
                      ALU.add, ALU.mult, [("mod", L)], [("mod", L)])
        yield


def stage_prologue(P, G, T):
    nc = P.nc
    with Stage(P) as S:
        cv = S.sb("cv", [128, KC, 3])
        P.dma("sp", cv[:], T["cvT"], [], ["cv"])
        P.act(G["sil"][:], cv[:], AF.Silu, ["cv"], ["sil"])
        for _ in mod_steps(P, G, T, S, [0]):
            pass
    with Stage(P) as S:
        xtm = [S.sb("xtm%d" % i, [128, 4, D]) for i in range(2)]
        xT = [S.sb("xT%d" % i, [128, KC, 512]) for i in range(2)]
        XTv = T["XT"].rearrange("(c p) t -> p c t", p=128)
        for bi, (tok0, nt, b, is_ctx, pos0) in enumerate(token_blocks()):
            xm = xtm[bi % 2]
            xk = ("xtm", bi % 2)
            src = T["ctx2"][b] if is_ctx else T["x2"][b, pos0:pos0 + nt]
            nts = nt // 128
            P.dma("sp", xm[:, 0:nts, :], src.rearrange("(ts p) d -> p ts d", p=128), [], [xk])
            xo = xT[bi % 2]
            ok = ("xT", bi % 2)
            for c in range(KC):
                ps, pk = P.next_ps()
                for ts_ in range(nts):
                    P.tr(ps[:, ts_ * 128:(ts_ + 1) * 128], xm[:, ts_, c * 128:(c + 1) * 128], G["ident"][:], [xk], [pk])
                P.copy("act" if c % 2 == 0 else "dve", xo[:, c, :nt], ps[:, :nt], [pk], [ok])
            P.dma("act", XTv[:, :, tok0:tok0 + nt], xo[:, :, :nt], [ok], ["XT"])


def stage_epilogue(P, G, T):
    with Stage(P) as S:
        xT = [S.sb("xT%d" % i, [128, KC, 512]) for i in range(2)]
        yo = [S.sb("yo%d" % i, [128, 4, D]) for i in range(2)]
        XTv = T["XT"].rearrange("(c p) t -> p c t", p=128)
        for bi, (tok0, nt, b, is_ctx, pos0) in enumerate(token_blocks(with_ctx=False)):
            xi = xT[bi % 2]
            xk = ("xT", bi % 2)
            P.dma("sp", xi[:], XTv[:, :, tok0:tok0 + nt], ["XT"], [xk])
            y = yo[bi % 2]
            yk = ("yo", bi % 2)
            for ts_ in range(4):
                for half in range(2):
                    ps, pk = P.next_ps()
                    for cc in range(4):
                        c = half * 4 + cc
                        P.tr(ps[:, cc * 128:(cc + 1) * 128], xi[:, c, ts_ * 128:(ts_ + 1) * 128], G["ident"][:], [xk], [pk])
                    P.copy("act" if half == 0 else "dve", y[:, ts_, half * 512:(half + 1) * 512], ps[:, :], [pk], [yk])
            P.dma("act", T["y"][b, pos0:pos0 + nt].rearrange("(ts p) d -> p ts d", p=128), y[:], [yk], ["yout"])


def stage_attn_qkv(P, G, T, L, j):
    with Stage(P) as S:
        w = S.sb("wqkv", [128, KC, 1536], BF16)
        load_w(P, w, T["attn_w_qkv"][j], KC, "wqkv")
        cosT = S.sb("cosT", [128, LAT])
        sinT = S.sb("sinT", [128, LAT])
        P.dma("sp", cosT[:], T["c_cos"], [], ["cos"])
        P.dma("sp", sinT[:], T["c_sin"], [], ["sin"])
        xs = [S.sb("x%d" % i, [128, KC, 512]) for i in range(2)]
        h = S.sb("h", [128, KC, 512], BF16)
        sq = S.sb("sq", [128, KC, 512], BF16)
        rstd = S.sb("rstd", [128, 512])
        tmp = [S.sb("ntmp%d" % i, [128, 512]) for i in range(2)]
        qs = [S.sb("qs%d" % i, [128, 512]) for i in range(2)]
        q2 = [S.sb("q2%d" % i, [128, 512]) for i in range(2)]
        rs = [S.sb("rs%d" % i, [128, 512]) for i in range(2)]
        qn = [S.sb("qn%d" % i, [128, 512]) for i in range(2)]
        t1 = [S.sb("t1%d" % i, [128, 512]) for i in range(2)]
        t2 = [S.sb("t2%d" % i, [128, 512]) for i in range(2)]
        qo = [S.sb("qo%d" % i, [128, 512], BF16) for i in range(4)]
        vo = [S.sb("vo%d" % i, [128, 256], BF16) for i in range(2)]
        XTv = T["XT"].rearrange("(c p) t -> p c t", p=128)
        blks = token_blocks()
        P.dma("sp", xs[0][:, :, :blks[0][1]], XTv[:, :, blks[0][0]:blks[0][0] + blks[0][1]], ["XT"], [("x", 0)])
        it = 0
        vi = 0
        for bi, (tok0, nt, b, is_ctx, pos0) in enumerate(blks):
            if bi + 1 < len(blks):
                t0n, ntn = blks[bi + 1][0], blks[bi + 1][1]
                P.dma("sp", xs[(bi + 1) % 2][:, :, :ntn], XTv[:, :, t0n:t0n + ntn], ["XT"], [("x", (bi + 1) % 2)])
            n = 2 if is_ctx else b
            emit_norm(P, G, xs[bi % 2], ("x", bi % 2), nt, L, 1, n, h, "h", tmp, sq, rstd)
            def qk_chain(mo, i2, i3):
                isq = mo < 8
                ps, pk = P.next_ps()
                for kc in range(KC):
                    P.mm(ps[:, :nt], w[:, kc, mo * 128:(mo + 1) * 128], h[:, kc, :nt], kc == 0, kc == KC - 1, ["wqkv", "h"], [pk])
                P.copy("act", qs[i2][:, :nt], ps[:, :nt], [pk], [("qs", i2)])
                yield
                P.tt("dve", q2[i2][:, :nt], ps[:, :nt], qs[i2][:, :nt], ALU.mult, [pk, ("qs", i2)], [("q2", i2)])
                yield
                ps2, pk2 = P.next_ps()
                P.mm(ps2[:, :nt], G["bones"][:], q2[i2][:, :nt], True, True, [("q2", i2)], [pk2])
                P.act(rs[i2][:, :nt], ps2[:, :nt], AF.Ln, [pk2], [("rs", i2)], bias=G["eps_t"][:, 0:1], scale=1.0 / 64)
                yield
                P.act(rs[i2][:, :nt], rs[i2][:, :nt], AF.Exp, [("rs", i2)], [("rs", i2)], scale=-0.5)
                yield
                gcol = (VO["qg"] if isq else VO["kg"]) + j
                oq = qo[i3]
                okk = ("qo", i3)
                if is_ctx:
                    P.stt(oq[:, :nt], qs[i2][:, :nt], G["vec"][:, gcol:gcol + 1], rs[i2][:, :nt], ALU.mult, ALU.mult,
                          [("qs", i2), ("rs", i2)], [okk])
                else:
                    P.stt(qn[i2][:, :nt], qs[i2][:, :nt], G["vec"][:, gcol:gcol + 1], rs[i2][:, :nt], ALU.mult, ALU.mult,
                          [("qs", i2), ("rs", i2)], [("qn", i2)])
                    yield
                    ps3, pk3 = P.next_ps()
                    P.mm(ps3[:, :nt], G["rot"][:], qn[i2][:, :nt], True, True, [("qn", i2)], [pk3])
                    P.tt("pool", t1[i2][:, :nt], qn[i2][:, :nt], cosT[:, pos0:pos0 + nt], ALU.mult, [("qn", i2), "cos"], [("t1", i2)])
                    yield
                    P.tt("dve", t2[i2][:, :nt], ps3[:, :nt], sinT[:, pos0:pos0 + nt], ALU.mult, [pk3, "sin"], [("t2", i2)])
                    yield
                    P.tt("pool", oq[:, :nt], t1[i2][:, :nt], t2[i2][:, :nt], ALU.add, [("t1", i2), ("t2", i2)], [okk])
                yield
                if isq:
                    P.dma("sp", T["QT"][mo * 128:(mo + 1) * 128, tok0:tok0 + nt], oq[:, :nt], [okk], ["QT"])
                else:
                    P.dma("sp", T["KTs"][(mo - 8) * 128:(mo - 7) * 128, tok0:tok0 + nt], oq[:, :nt], [okk], ["KTs"])
                yield

            for mo0 in range(0, 10, 2):
                alive = [qk_chain(mo0 + i, i, (it + i) % 4) for i in range(2)]
                it += 2
                while alive:
                    nxt = []
                    for g_ in alive:
                        try:
                            next(g_)
                            nxt.append(g_)
                        except StopIteration:
                            pass
                    alive = nxt
            for ts_ in range(nt // 128):
                ps, pk = P.next_ps()
                for kc in range(KC):
                    P.mm(ps[:, 0:256], h[:, kc, ts_ * 128:(ts_ + 1) * 128], w[:, kc, 1280:1536], kc == 0, kc == KC - 1,
                         ["wqkv", "h"], [pk])
                v = vo[vi % 2]
                vk = ("vo", vi % 2)
                vi += 1
                P.copy("act", v[:, :], ps[:, 0:256], [pk], [vk])
                sc = (tok0 - b * SEG) // 128 + ts_
                P.dma("sp", T["Vs"][b, :, :, sc, :].rearrange("g s d -> s g d"), v[:, :].rearrange("s (g d) -> s g d", g=4),
                      [vk], ["Vs"])


def stage_attn_core(P, G, T, with_ctx_out, bg_layers=()):
    LOOK = 3
    with Stage(P) as S:
        kTs = [[S.sb("kT%d_%d" % (i, hh), [128, SEG], BF16) for hh in range(2)] for i in range(2)]
        for i in range(2):
            P.op("pool", lambda e: e.memset(kTs[i][0][64:128, :], 0.0), [], [("kT", i)])
            P.op("pool", lambda e: e.memset(kTs[i][1][0:64, :], 0.0), [], [("kT", i)])
        Vall = S.sb("Vall", [128, NSC, 4, 128], BF16)
        qt = [S.sb("qt%d" % i, [128, 512], BF16) for i in range(3)]
        pt = [S.sb("pt%d" % i, [128, 512], BF16) for i in range(6)]
        rl = [S.sb("rl%d" % i, [128, 512]) for i in range(2)]
        on = [S.sb("on%d" % i, [64, 512], BF16) for i in range(2)]
        P.op("pool", lambda e: e.memset(Vall[:, :, :, 64:128], 1.0), [], ["Vones"])
        pob = [[(P.ps_tiles[0], ("ps", 0)), (P.ps_tiles[1], ("ps", 1))], [(P.ps_tiles[2], ("ps", 2)), (P.ps_tiles[3], ("ps", 3))]]
        sb_ = [(P.ps_tiles[i], ("ps", i)) for i in range(4, 8)]
        work = []
        for b in range(NB):
            for g in range(4):
                for pair in range(2):
                    for (tok0, nt, bb, is_ctx, pos0) in token_blocks(with_ctx=with_ctx_out):
                        if bb == b:
                            work.append((b, g, pair, tok0, nt, is_ctx))
        kcur = {}
        cnt = {"s": 0, "p": 0, "o": 0, "k": 0}

        def load_q(wi):
            b, g, pair, tok0, nt, is_ctx = work[wi]
            qc = 2 * g + pair
            P.dma("sp", qt[wi % 3][:, :nt], T["QT"][qc * 128:(qc + 1) * 128, tok0:tok0 + nt], ["QT"], [("qt", wi % 3)])

        def load_kv(wi):
            b, g, pair, tok0, nt, is_ctx = work[wi]
            if (b, g) not in kcur:
                ki = cnt["k"] % 2
                cnt["k"] += 1
                kcur[(b, g)] = ki
                for hf in range(2):
                    P.dma("sp", kTs[ki][hf][hf * 64:(hf + 1) * 64, :], T["KTs"][g * 64:(g + 1) * 64, b * SEG:(b + 1) * SEG], ["KTs"], [("kT", ki)])

        bg = mod_steps(P, G, T, S, list(bg_layers)) if bg_layers else None
        load_kv(0)
        load_q(0)
        for wi, (b, g, pair, tok0, nt, is_ctx) in enumerate(work):
            if bg is not None and wi % 3 == 2:
                if next(bg, "done") == "done":
                    bg = None
            if ("v", b) not in kcur:
                kcur[("v", b)] = 1
                for g2 in range(4):
                    P.dma("sp", Vall[:, :, g2, 0:64], T["Vs"][b, g2], ["Vs"], ["Vall"])
            if wi + 1 < len(work):
                load_kv(wi + 1)
                load_q(wi + 1)
            ki = kcur[(b, g)]
            kT = kTs[ki]
            q = qt[wi % 3]
            qk = ("qt", wi % 3)
            qc = 2 * g + pair
            nsc = (CTXL // 128) if is_ctx else NSC
            po = pob[wi % 2]
            items = [(sc, hh) for sc in range(nsc) for hh in range(2)]
            sbank = {}

            def emit_S(i):
                sc, hh = items[i]
                ps, pk = sb_[cnt["s"] % 4]
                cnt["s"] += 1
                P.mm(ps[:, :nt], kT[hh][:, sc * 128:(sc + 1) * 128], q[:, :nt], True, True, [("kT", ki), qk], [pk])
                sbank[i] = (ps, pk)

            for i in range(min(LOOK, len(items))):
                emit_S(i)
            for i, (sc, hh) in enumerate(items):
                if i + LOOK < len(items):
                    emit_S(i + LOOK)
                ps, pk = sbank.pop(i)
                p_ = pt[cnt["p"] % 6]
                ptk = ("pt", cnt["p"] % 6)
                cnt["p"] += 1
                P.act(p_[:, :nt], ps[:, :nt], AF.Exp, [pk], [ptk], scale=0.125)
                P.mm(po[hh][0][:, :nt], Vall[:, sc, g, :], p_[:, :nt], sc == 0, sc == nsc - 1,
                     ["Vall", "Vones", ptk], [po[hh][1]])
            for hh in range(2):
                oi = cnt["o"] % 2
                cnt["o"] += 1
                r_ = rl[oi]
                o_ = on[oi]
                rk, ok = ("rl", oi), ("on", oi)
                P.op("dve", lambda e: e.reciprocal(out=r_[64:128, :nt], in_=po[hh][0][64:128, :nt]), [po[hh][1]], [rk])
                P.tt("dve", o_[:, :nt], po[hh][0][0:64, :nt], r_[64:128, :nt], ALU.mult, [po[hh][1], rk], [ok])
                hq = qc * 2 + hh
                P.dma("sp", T["AT"][hq * 64:(hq + 1) * 64, tok0:tok0 + nt], o_[:, :nt], [ok], ["AT"])
        if bg is not None:
            for _ in bg:
                pass


def stage_mixout(P, G, T, L, w_dram, kind, with_ctx):
    with Stage(P) as S:
        if kind == "full":
            w = S.sb("wo", [128, KC, D], BF16)
            load_w(P, w, w_dram, KC, "wo")
        else:
            w = S.sb("wo", [128, 8, 256], BF16)
            for g in range(4):
                for k2 in range(2):
                    P.dma("pool", w[:, g * 2 + k2, :], w_dram[g, k2 * 128:(k2 + 1) * 128, :], [], ["wo"])
            gs = S.sb("gs", [128, KC, 3])
            for n in range(3):
                P.tt("dve", gs[:, :, n], G["mod"][L][:, 16:24, n], G["vec"][:, VO["pscale"]:VO["pscale"] + 8], ALU.mult, ["mod"], ["gs"])
        xs = [S.sb("x%d" % i, [128, KC, 512]) for i in range(2)]
        at = [S.sb("a%d" % i, [128, KC, 512], BF16) for i in range(2)]
        XTv = T["XT"].rearrange("(c p) t -> p c t", p=128)
        ATv = T["AT"].rearrange("(c p) t -> p c t", p=128)
        for bi, (tok0, nt, b, is_ctx, pos0) in enumerate(token_blocks(with_ctx=with_ctx)):
            x = xs[bi % 2]
            a = at[bi % 2]
            xk, ak = ("x", bi % 2), ("a", bi % 2)
            P.dma("sp", x[:, :, :nt], XTv[:, :, tok0:tok0 + nt], ["XT"], [xk])
            P.dma("sp", a[:, :, :nt], ATv[:, :, tok0:tok0 + nt], ["AT"], [ak])
            n = 2 if is_ctx else b
            for mo in range(KC):
                ps, pk = P.next_ps()
                if kind == "full":
                    for kc in range(KC):
                        P.mm(ps[:, :nt], w[:, kc, mo * 128:(mo + 1) * 128], a[:, kc, :nt], kc == 0, kc == KC - 1, ["wo", ak], [pk])
                    gate = G["mod"][L][:, 16 + mo, n:n + 1]
                    gk = "mod"
                else:
                    g = mo // 2
                    for k2 in range(2):
                        P.mm(ps[:, :nt], w[:, g * 2 + k2, (mo % 2) * 128:(mo % 2 + 1) * 128], a[:, g * 2 + k2, :nt], k2 == 0, k2 == 1,
                             ["wo", ak], [pk])
                    gate = gs[:, mo, n:n + 1]
                    gk = "gs"
                P.stt(x[:, mo, :nt], ps[:, :nt], gate, x[:, mo, :nt], ALU.mult, ALU.add, [pk, xk, gk], [xk])
            P.dma("sp", XTv[:, :, tok0:tok0 + nt], x[:, :, :nt], [xk], ["XT"])


def stage_mix_mlp(P, G, T, L, w_dram, kind, with_ctx):
    with Stage(P) as S:
        if kind == "full":
            w = S.sb("wo", [128, KC, D], BF16)
            load_w(P, w, w_dram, KC, "wo")
        else:
            w = S.sb("wo", [128, 8, 256], BF16)
            for g in range(4):
                for k2 in range(2):
                    P.dma("pool", w[:, g * 2 + k2, :], w_dram[g, k2 * 128:(k2 + 1) * 128, :], [], ["wo"])
            gs = S.sb("gs", [128, KC, 3])
            for n in range(3):
                P.tt("dve", gs[:, :, n], G["mod"][L][:, 16:24, n], G["vec"][:, VO["pscale"]:VO["pscale"] + 8], ALU.mult, ["mod"], ["gs"])
        w1 = S.sb("w1", [128, KC, DFF], BF16)
        w2 = S.sb("w2", [128, 32, D], BF16)
        load_w(P, w1, T["mlp_w_in"][L], KC, "w1")
        load_w(P, w2, T["mlp_w_out"][L], 32, "w2")
        x = S.sb("x", [128, KC, 512])
        h = S.sb("h", [128, KC, 512], BF16)
        a = h
        u = S.sb("u", [128, 32, 512], BF16)
        XTv = T["XT"].rearrange("(c p) t -> p c t", p=128)
        ATv = T["AT"].rearrange("(c p) t -> p c t", p=128)
        ri = 0
        for bi, (tok0, nt, b, is_ctx, pos0) in enumerate(token_blocks(with_ctx=with_ctx)):
            P.dma("sp", x[:, :, :nt], XTv[:, :, tok0:tok0 + nt], ["XT"], ["x"])
            P.dma("sp", a[:, :, :nt], ATv[:, :, tok0:tok0 + nt], ["AT"], ["h"])
            n = 2 if is_ctx else b
            for mo in range(KC):
                ps, pk = P.next_ps()
                if kind == "full":
                    for kc in range(KC):
                        P.mm(ps[:, :nt], w[:, kc, mo * 128:(mo + 1) * 128], a[:, kc, :nt], kc == 0, kc == KC - 1, ["wo", "h"], [pk])
                    gate = G["mod"][L][:, 16 + mo, n:n + 1]
                    gk = "mod"
                else:
                    g = mo // 2
                    for k2 in range(2):
                        P.mm(ps[:, :nt], w[:, g * 2 + k2, (mo % 2) * 128:(mo % 2 + 1) * 128], a[:, g * 2 + k2, :nt], k2 == 0, k2 == 1,
                             ["wo", "h"], [pk])
                    gate = gs[:, mo, n:n + 1]
                    gk = "gs"
                P.stt(x[:, mo, :nt], ps[:, :nt], gate, x[:, mo, :nt], ALU.mult, ALU.add, [pk, "x", gk], ["x"])
            emit_norm(P, G, x, "x", nt, L, 2, n, h, "h", None, u, None, sqkeys=[("u", c) for c in range(KC)])
            for mo in range(32):
                ps, pk = P.next_ps()
                for kc in range(KC):
                    P.mm(ps[:, :nt], w1[:, kc, mo * 128:(mo + 1) * 128], h[:, kc, :nt], kc == 0, kc == KC - 1, ["w1", "h"], [pk])
                P.act(ps[:, :nt], ps[:, :nt], AF.Relu, [pk], [pk])
                P.act(u[:, mo, :nt], ps[:, :nt], AF.Square, [pk], [("u", mo)])
            for mo in range(KC):
                ps, pk = P.next_ps()
                for kc in range(32):
                    P.mm(ps[:, :nt], w2[:, kc, mo * 128:(mo + 1) * 128], u[:, kc, :nt], kc == 0, kc == 31, ["w2", ("u", kc)], [pk])
                P.stt(x[:, mo, :nt], ps[:, :nt], G["mod"][L][:, 40 + mo, n:n + 1], x[:, mo, :nt], ALU.mult, ALU.add,
                      [pk, "x", ("mod", L)], ["x"])
            P.dma("sp", XTv[:, :, tok0:tok0 + nt], x[:, :, :nt], ["x"], ["XT"])


def stage_mlp(P, G, T, L, with_ctx):
    with Stage(P) as S:
        w1 = S.sb("w1", [128, KC, DFF], BF16)
        w2 = S.sb("w2", [128, 32, D], BF16)
        load_w(P, w1, T["mlp_w_in"][L], KC, "w1")
        load_w(P, w2, T["mlp_w_out"][L], 32, "w2")
        x = S.sb("x", [128, KC, 512])
        h = S.sb("h", [128, KC, 512], BF16)
        u = S.sb("u", [128, 32, 512], BF16)
        sq = S.sb("sq", [128, KC, 512], BF16)
        rstd = S.sb("rstd", [128, 512])
        tmp = [S.sb("ntmp%d" % i, [128, 512]) for i in range(2)]
        rr = tmp
        XTv = T["XT"].rearrange("(c p) t -> p c t", p=128)
        ri = 0
        for bi, (tok0, nt, b, is_ctx, pos0) in enumerate(token_blocks(with_ctx=with_ctx)):
            P.dma("sp", x[:, :, :nt], XTv[:, :, tok0:tok0 + nt], ["XT"], ["x"])
            n = 2 if is_ctx else b
            emit_norm(P, G, x, "x", nt, L, 2, n, h, "h", tmp, sq, rstd)
            for mo in range(32):
                ps, pk = P.next_ps()
                for kc in range(KC):
                    P.mm(ps[:, :nt], w1[:, kc, mo * 128:(mo + 1) * 128], h[:, kc, :nt], kc == 0, kc == KC - 1, ["w1", "h"], [pk])
                r_ = rr[ri % 2]
                rk = ("ntmp", ri % 2)
                ri += 1
                P.act(r_[:, :nt], ps[:, :nt], AF.Relu, [pk], [rk])
                P.tt("pool" if mo % 4 != 3 else "dve", u[:, mo, :nt], r_[:, :nt], r_[:, :nt], ALU.mult, [rk], [("u", mo)])
            for mo in range(KC):
                ps, pk = P.next_ps()
                for kc in range(32):
                    P.mm(ps[:, :nt], w2[:, kc, mo * 128:(mo + 1) * 128], u[:, kc, :nt], kc == 0, kc == 31, ["w2", ("u", kc)], [pk])
                P.stt(x[:, mo, :nt], ps[:, :nt], G["mod"][L][:, 40 + mo, n:n + 1], x[:, mo, :nt], ALU.mult, ALU.add,
                      [pk, "x", ("mod", L)], ["x"])
            P.dma("sp", XTv[:, :, tok0:tok0 + nt], x[:, :, :nt], ["x"], ["XT"])


def stage_norm_to_HT(P, G, T, L, with_ctx):
    with Stage(P) as S:
        xs = [S.sb("x%d" % i, [128, KC, 512]) for i in range(2)]
        hs = [S.sb("h%d" % i, [128, KC, 512]) for i in range(2)]
        sq = S.sb("sq", [128, KC, 512], BF16)
        rstd = S.sb("rstd", [128, 512])
        tmp = [S.sb("ntmp%d" % i, [128, 512]) for i in range(2)]
        XTv = T["XT"].rearrange("(c p) t -> p c t", p=128)
        HTv = T["HT"].rearrange("(c p) t -> p c t", p=128)
        for bi, (tok0, nt, b, is_ctx, pos0) in enumerate(token_blocks(with_ctx=with_ctx)):
            x, h = xs[bi % 2], hs[bi % 2]
            xk, hk = ("x", bi % 2), ("h", bi % 2)
            P.dma("sp", x[:, :, :nt], XTv[:, :, tok0:tok0 + nt], ["XT"], [xk])
            emit_norm(P, G, x, xk, nt, L, 1, 2 if is_ctx else b, h, hk, tmp, sq, rstd)
            P.dma("act", HTv[:, :, tok0:tok0 + nt], h[:, :, :nt], [hk], ["HT"])


def stage_pool(P, G, T, with_ctx):
    with Stage(P) as S:
        PADL = 8
        hb = [S.sb("hb%d" % i, [128, LAT + 24]) for i in range(2)]
        d2 = S.sb("d2", [128, LAT + 24])
        d4 = S.sb("d4", [128, LAT + 24])
        rc = S.sb("rc", [128, 4, LAT])
        rcc = S.sb("rcc", [128, 4, CTXL])
        pm = [S.sb("pm%d" % i, [128, LAT]) for i in range(2)]
        po = [S.sb("po%d" % i, [128, LAT], BF16) for i in range(2)]
        P.dma("sp", rc[:], T["c_rcnt"], [], ["rc"])
        P.dma("sp", rcc[:], T["c_rcntc"], [], ["rc"])
        for i in range(2):
            P.op("pool", lambda e: e.memset(hb[i][:], 0.0), [], [("hb", i)])
        it = 0
        segs = []
        for b in range(NB):
            if with_ctx:
                segs.append((b * SEG, CTXL, True))
            segs.append((b * SEG + CTXL, LAT, False))
        for (tok0, Tn, is_ctx) in segs:
            for c in range(KC):
                gi = c // 2
                win = (2, 4, 8, 16)[gi]
                i2 = it % 2
                it += 1
                hbt = hb[i2]
                hk = ("hb", i2)
                if Tn < LAT:
                    P.op("pool", lambda e: e.memset(hbt[:, PADL + Tn:PADL + Tn + 16], 0.0), [], [hk])
                P.dma("sp", hbt[:, PADL:PADL + Tn], T["HT"][c * 128:(c + 1) * 128, tok0:tok0 + Tn], ["HT"], [hk])
                W_ = Tn + 16
                src = hbt
                sk = hk
                cur = 1
                bufs = [(d2, "d2"), (d4, "d4")]
                bi_ = 0
                while cur < win:
                    dst, dk = bufs[bi_ % 2]
                    bi_ += 1
                    P.tt("dve" if cur in (1, 4) else "pool", dst[:, 0:W_ - cur], src[:, 0:W_ - cur], src[:, cur:W_], ALU.add, [sk], [dk])
                    src, sk = dst, dk
                    cur *= 2
                off = PADL - win // 2
                rct = rcc if is_ctx else rc
                pmt, pot = pm[i2], po[i2]
                P.tt("dve", pmt[:, :Tn], src[:, off:off + Tn], rct[:, gi, :Tn], ALU.mult, [sk, "rc"], [("pm", i2)])
                P.tt("pool", pot[:, :Tn], pmt[:, :Tn], hbt[:, PADL:PADL + Tn], ALU.subtract, [("pm", i2), hk], [("po", i2)])
                P.dma("sp", T["AT"][c * 128:(c + 1) * 128, tok0:tok0 + Tn], pot[:, :Tn], [("po", i2)], ["AT"])


def stage_rwkv_prep(P, G, T, with_ctx=True):
    with Stage(P) as S:
        wr = [S.sb("wrkv%d" % i, [128, KC, D], BF16) for i in range(3)]
        for i in range(3):
            load_w(P, wr[i], T["rwkv_w_rkv"][0, i], KC, "wrkv")
        wg1 = S.sb("wg1", [128, KC, 160], BF16)
        load_w(P, wg1, T["rwkv_g1"][0], KC, "wl")
        wg2a = S.sb("wg2a", [128, D], BF16)
        wg2b = S.sb("wg2b", [32, D], BF16)
        P.dma("pool", wg2a[:], T["rwkv_g2"][0, 0:128, :], [], ["wl"])
        P.dma("pool", wg2b[:], T["rwkv_g2"][0, 128:160, :], [], ["wl"])
        w1 = S.sb("w1", [128, KC, 2, 128], BF16)
        for d in range(2):
            for kc in range(KC):
                P.dma("pool", w1[:, kc, d, 0:64], T["rwkv_w1"][0, d, kc * 128:(kc + 1) * 128, :], [], ["wl"])
                P.dma("pool", w1[:, kc, d, 64:128], T["rwkv_a1"][0, d, kc * 128:(kc + 1) * 128, :], [], ["wl"])
        w2 = S.sb("w2", [64, 2, D], BF16)
        a2 = S.sb("a2", [64, 2, D], BF16)
        for d in range(2):
            P.dma("pool", w2[:, d, :], T["rwkv_w2"][0, d], [], ["wl"])
            P.dma("pool", a2[:, d, :], T["rwkv_a2"][0, d], [], ["wl"])
        hb = S.sb("hb", [128, KC, 514])
        xx = S.sb("xx", [128, KC, 512])
        xms = [S.sb("xm%d" % i, [128, 6, KC, 512], BF16) for i in range(1)]
        lo = S.sb("lo", [128, 2, 512], BF16)
        la = S.sb("la", [128, 2, 512], BF16)
        lg = S.sb("lg", [128, 2, 512], BF16)
        ev = [S.sb("ev%d" % i, [128, 512]) for i in range(4)]
        HTv = T["HT"].rearrange("(c p) t -> p c t", p=128)
        ei = 0
        for bi, (tok0, nt, b, is_ctx, pos0) in enumerate(token_blocks(with_ctx=with_ctx)):
            xm = xms[0]
            xb = 0
            seg0 = b * SEG if is_ctx else b * SEG + CTXL
            segn = CTXL if is_ctx else LAT
            first = tok0 == seg0
            last = tok0 + nt == seg0 + segn
            lo_ = 0 if first else -1
            hi_ = 0 if last else 1
            if first:
                P.op("pool", lambda e: e.memset(hb[:, :, 0:1], 0.0), [], ["hb"])
            if last:
                P.op("pool", lambda e: e.memset(hb[:, :, nt + 1:nt + 2], 0.0), [], ["hb"])
            P.dma("sp", hb[:, :, 1 + lo_:1 + nt + hi_], HTv[:, :, tok0 + lo_:tok0 + nt + hi_], ["HT"], ["hb"])
            P.tt("pool", xx[:, :, :nt], hb[:, :, 0:nt], hb[:, :, 2:nt + 2], ALU.add, ["hb"], ["xx"])
            P.stt(xx[:, :, :nt], xx[:, :, :nt], 0.5, hb[:, :, 1:nt + 1], ALU.mult, ALU.subtract, ["xx", "hb"], ["xx"])
            for m in range(6):
                for c in range(KC):
                    P.stt(xm[:, m, c, :nt], xx[:, c, :nt], G["vec"][:, VO["mu"] + m * 8 + c:VO["mu"] + m * 8 + c + 1], hb[:, c, 1:nt + 1],
                          ALU.mult, ALU.add, ["xx", "hb"], [("xm", xb, m)])
            for i, (m, dst) in enumerate([(0, "RT"), (2, "KT"), (3, "VT")]):
                for mo in range(KC):
                    ps, pk = P.next_ps()
                    for kc in range(KC):
                        P.mm(ps[:, :nt], wr[i][:, kc, mo * 128:(mo + 1) * 128], xm[:, m, kc, :nt], kc == 0, kc == KC - 1,
                             ["wrkv", ("xm", xb, m)], [pk])
                    e_ = ev[ei % 4]
                    ek = ("ev", ei % 4)
                    ei += 1
                    P.copy("act" if mo % 2 == 0 else "dve", e_[:, :nt], ps[:, :nt], [pk], [ek])
                    P.dma("act", T[dst][mo * 128:(mo + 1) * 128, tok0:tok0 + nt], e_[:, :nt], [ek], [dst])
            for part, (o0, osz) in enumerate([(0, 128), (128, 32)]):
                ps, pk = P.next_ps()
                for kc in range(KC):
                    P.mm(ps[0:osz, :nt], wg1[:, kc, o0:o0 + osz], xm[:, 5, kc, :nt], kc == 0, kc == KC - 1, ["wl", ("xm", xb, 5)], [pk])
                P.act(lg[0:osz, part, :nt], ps[0:osz, :nt], AF.Sigmoid, [pk], ["lg"])
            for mo in range(KC):
                ps, pk = P.next_ps()
                P.mm(ps[:, :nt], wg2a[:, mo * 128:(mo + 1) * 128], lg[:, 0, :nt], True, False, ["wl", "lg"], [pk])
                P.mm(ps[:, :nt], wg2b[:, mo * 128:(mo + 1) * 128], lg[0:32, 1, :nt], False, True, ["wl", "lg"], [pk])
                e_ = ev[ei % 4]
                ek = ("ev", ei % 4)
                ei += 1
                P.copy("act" if mo % 2 == 0 else "dve", e_[:, :nt], ps[:, :nt], [pk], [ek])
                P.dma("act", T["GT"][mo * 128:(mo + 1) * 128, tok0:tok0 + nt], e_[:, :nt], [ek], ["GT"])
            for d in range(2):
                ps, pk = P.next_ps()
                for kc in range(KC):
                    P.mm(ps[0:64, :nt], w1[:, kc, d, 0:64], xm[:, 1, kc, :nt], kc == 0, kc == KC - 1, ["wl", ("xm", xb, 1)], [pk])
                P.act(lo[0:64, d, :nt], ps[0:64, :nt], AF.Tanh, [pk], ["lo"])
                ps, pk = P.next_ps()
                for kc in range(KC):
                    P.mm(ps[0:64, :nt], w1[:, kc, d, 64:128], xm[:, 4, kc, :nt], kc == 0, kc == KC - 1, ["wl", ("xm", xb, 4)], [pk])
                P.copy("dve", la[0:64, d, :nt], ps[0:64, :nt], [pk], ["la"])
                for (wsb, src, skey, vofs, dst) in [(w2, lo, "lo", VO["w0"], "SIG%d" % d), (a2, la, "la", VO["a0"], "AA%d" % d)]:
                    for mo in range(KC):
                        ps, pk = P.next_ps()
                        P.mm(ps[:, :nt], wsb[:, d, mo * 128:(mo + 1) * 128], src[0:64, d, :nt], True, True, ["wl", skey], [pk])
                        e_ = ev[ei % 4]
                        ek = ("ev", ei % 4)
                        ei += 1
                        P.act(e_[:, :nt], ps[:, :nt], AF.Sigmoid, [pk], [ek], bias=G["vec"][:, vofs + d * 8 + mo:vofs + d * 8 + mo + 1], scale=1.0)
                        P.dma("act", T[dst][mo * 128:(mo + 1) * 128, tok0:tok0 + nt], e_[:, :nt], [ek], [dst])


F32R = mybir.dt.float32r
SCAN_MODE = "f32"
SCAN_BF16 = True
SCAN_GH = 8


def _mo(ap):
    return ap.bitcast(F32R) if SCAN_MODE == "f32r" else ap


def stage_rwkv_scan(P, G, T):
    CH = 128
    NH = 16
    GH = SCAN_GH
    with Stage(P) as S:
        def big(name):
            return S.sb(name, [64, NH, CH])
        _base = [big("%s0" % n) for n in ("r_t", "k_t", "v_t", "s_t", "a_t")]
        ld = [_base, [_base[0], _base[1], big("v_t1"), _base[3], _base[4]]]

        def ldkey(li, ti):
            return ("ld", li if ti == 2 else 0, ti)
        tA, tB, tC, tD, tE, tF, tG = big("tA"), big("tB"), big("tC"), big("tD"), big("tE"), big("tF"), big("tG")
        tH = big("tH")
        MD = BF16 if SCAN_BF16 else F32
        QR = S.sb("QR", [64, NH, 2 * CH], MD)
        Hst = S.sb("Hst", [64, NH, 64])
        if SCAN_BF16:
            HstM = S.sb("HstM", [64, NH, 64], MD)
            kinvM = S.sb("kinvM", [64, NH, CH], MD)
            binvM = S.sb("binvM", [64, NH, CH], MD)
        else:
            HstM = Hst
        kkv = S.sb("kkv", [64, 3, NH])
        P.dma("sp", kkv[:, 0:2, :], T["c_v64"], [], ["kkv"])
        P.ts("dve", kkv[:, 2, :], kkv[:, 1, :], -1.0, 1.0, ALU.mult, ALU.add, ["kkv"], ["kkv"])
        tot = S.sb("tot", [64, NH])
        ones_s = S.sb("ones_s", [64, CH])
        P.op("pool", lambda e: e.memset(ones_s[:], 1.0), [], ["ones_s"])
        mk = S.sb("mk", [128, 2, 2 * CH])
        mkT = S.sb("mkT", [128, 2, CH])
        P.dma("sp", mk[:], T["c_mask"], [], ["mk"])
        P.dma("sp", mkT[:], T["c_maskT"], [], ["mk"])

        def hb(name, shape):
            return [S.sb("%s%d" % (name, i), shape) for i in range(GH)]
        def hbm(name, shape):
            return [S.sb("%s%d" % (name, i), shape, MD) for i in range(GH)]
        vT, kdT, bdT = hbm("vT", [128, 64]), hbm("kdT", [128, 64]), hbm("bdT", [128, 64])
        Nn, ABb, AK = hb("Nn", [128, CH]), hbm("ABb", [128, CH]), hbm("AK", [128, 2 * CH])
        X = [hb("Xa", [128, CH]), hb("Xb", [128, CH])]
        XT_ = [hb("XTa", [128, CH]), hb("XTb", [128, CH])]
        Rr = [hb("Ra", [128, 64]), hb("Rb", [128, 64])]
        negU = hbm("negU", [128, 64])
        yt = hb("yt", [64, CH])
        cdec = DEC_C
        evi = [0]

        def ev_eng():
            evi[0] += 1
            return "dve" if evi[0] % 3 == 0 else "act"

        def bc(ap2):
            return ap2.unsqueeze(2).broadcast_to([64, NH, CH])

        def view(Tn):
            return T[Tn].rearrange("(h k) t -> k h t", k=64)

        seq = []
        for b in range(NB):
            for d in range(2):
                order = [0, 1] + list(range(2, NSC)) if d == 0 else [1, 0] + list(range(NSC - 1, 1, -1))
                for oi_, sc in enumerate(order):
                    seq.append((b, d, sc, oi_ == 0))

        def issue_loads(si):
            b, d, sc, first = seq[si]
            tok0 = b * SEG + sc * CH
            li = si % 2
            for ti, nm in enumerate(["RT", "KT", "VT", "SIG%d" % d, "AA%d" % d]):
                P.dma("sp", ld[li][ti][:], view(nm)[:, :, tok0:tok0 + CH], [nm], [ldkey(li, ti)])

        tI = big("tI")
        wcs = [S.sb("wc%d" % i, [64, NH]) for i in range(2)]
        kiK, biK = "kinvM", "binvM"
        assert SCAN_BF16

        def prep_early(si):
            b, d, sc, first = seq[si]
            li = si % 2
            r_t, k_t, v_t, s_t, a_t = ld[li]
            kr, kk_, kv, ks, ka = [ldkey(li, ti) for ti in range(5)]
            lastc = CH - 1 if d == 0 else 0
            P.tt("pool", tA[:], k_t[:], bc(kkv[:, 0, :]), ALU.mult, [kk_, "kkv"], ["tA"])
            yield
            P.act(tH[:], tA[:], AF.Square, ["tA"], ["tH"])
            yield
            for q4 in range(4):
                ps, pk = P.next_ps()
                P.mm(ps[0:64, :], G["ones_f"][0:64, 0:64], tH[:, q4 * 4:(q4 + 1) * 4, :], True, True, ["tH"], [pk])
                P.act(tH[:, q4 * 4:(q4 + 1) * 4, :], ps[0:64, :], AF.Ln, [pk], ["tH"], bias=G["tiny_t"][0:64, 0:1], scale=1.0)
                yield
            P.act(tH[:], tH[:], AF.Exp, ["tH"], ["tH"], scale=-0.5)
            yield
            P.tt("dve", tA[:], tA[:], tH[:], ALU.mult, ["tA", "tH"], ["tA"])
            P.tt("pool", tC[:], a_t[:], bc(kkv[:, 1, :]), ALU.mult, [ka, "kkv"], ["tC"])
            yield
            P.tt("pool", tC[:], tC[:], bc(kkv[:, 2, :]), ALU.add, ["tC", "kkv"], ["tC"])
            yield
            P.tt("pool", tC[:], tC[:], k_t[:], ALU.mult, ["tC", kk_], ["tC"])
            P.tt("dve", tD[:], tA[:], a_t[:], ALU.mult, ["tA", ka], ["tD"])
            yield
            for h in range(NH):
                P.op("dve", lambda e: e.tensor_tensor_scan(out=tI[:, h, :], data0=ones_s[:], data1=s_t[:, h, :], initial=0.0,
                                                           op0=ALU.mult, op1=ALU.add), ["ones_s", ks], ["tI"])
                if h % 4 == 3:
                    yield
            if d == 1:
                P.copy("pool", tot[:], tI[:, :, CH - 1], ["tI"], ["tot"])
                P.tt("pool", tF[:], s_t[:], tI[:], ALU.subtract, [ks, "tI"], ["tF"])
                yield
                P.tt("pool", tI[:], tF[:], bc(tot[:]), ALU.add, ["tF", "tot"], ["tI"])
                yield
            P.tt("pool", tF[:], tI[:], s_t[:], ALU.subtract, ["tI", ks], ["tF"])
            P.act(tG[:], tI[:], AF.Exp, ["tI"], ["tG"], scale=-cdec)
            yield
            P.act(tF[:], tF[:], AF.Exp, ["tF"], ["tF"], scale=-cdec)
            P.act(tI[:], tI[:], AF.Exp, ["tI"], ["tI"], scale=cdec)
            P.copy("pool", wcs[li][:], tG[:, :, lastc], ["tG"], [("wc", li)])
            yield
            P.tt("dve", tC[:], tC[:], tI[:], ALU.mult, ["tC", "tI"], ["tC"])
            P.tt("pool", tD[:], tD[:], tI[:], ALU.mult, ["tD", "tI"], ["tD"])
            yield

        def prep_late(si):
            b, d, sc, first = seq[si]
            li = si % 2
            r_t = ld[li][0]
            kr = ldkey(li, 0)
            if first:
                P.op("pool", lambda e: e.memset(Hst[:], 0.0), [], [("Hst", h_) for h_ in range(NH)])
                P.op("pool", lambda e: e.memset(HstM[:], 0.0), [], [("HstM", h_) for h_ in range(NH)])
            P.tt("dve", QR[:, :, 0:CH], tA[:], tF[:], ALU.mult, ["tA", "tF"], ["QR"])
            P.tt("pool", QR[:, :, CH:2 * CH], r_t[:], tG[:], ALU.mult, [kr, "tG"], ["QR"])
            P.copy("act", binvM[:], tD[:], ["tD"], ["binvM"])
            P.copy("act", kinvM[:], tC[:], ["tC"], ["kinvM"])
            P.tt("dve", tB[:], tC[:], bc(wcs[li][:]), ALU.mult, ["tC", ("wc", li)], ["tB"])
            P.tt("pool", tE[:], tD[:], bc(wcs[li][:]), ALU.mult, ["tD", ("wc", li)], ["tE"])

        def run_all(gens):
            alive = list(gens)
            while alive:
                nxt = []
                for g_ in alive:
                    try:
                        next(g_)
                        nxt.append(g_)
                    except StopIteration:
                        pass
                alive = nxt

        issue_loads(0)
        run_all([prep_early(0)])
        for si, (b, d, sc, first) in enumerate(seq):
            prep_late(si)
            if si + 1 < len(seq):
                issue_loads(si + 1)
                pe_gen = prep_early(si + 1)
            else:
                pe_gen = None
            li = si % 2
            v_t = ld[li][2]
            kv = ldkey(li, 2)
            tok0 = b * SEG + sc * CH
            lastc = CH - 1 if d == 0 else 0
            kdec, bdec = tB, tE
            wc_t, wck = wcs[li], ("wc", li)

            def head(h, i2, d=d, tok0=tok0, v_t=v_t, kv=kv, kdec=kdec, bdec=bdec, wc_t=wc_t, wck=wck):
                def K(n):
                    return (n, i2)
                for (src, skey, dst, dkey) in [(v_t, kv, vT, "vT"), (kdec, "tB", kdT, "kdT"), (bdec, "tE", bdT, "bdT")]:
                    ps, pk = P.next_ps()
                    P.tr(ps[:, 0:64], src[:, h, :], G["ident"][0:64, 0:64], [skey], [pk])
                    P.copy(ev_eng(), dst[i2][:], ps[:, 0:64], [pk], [K(dkey)])
                yield
                ps, pk = P.next_ps()
                P.mm(ps[:, 0:2 * CH], binvM[:, h, :], QR[:, h, :], True, True, [biK, "QR"], [pk])
                P.tt("dve", Nn[i2][:], ps[:, 0:CH], mk[:, d, 0:CH], ALU.mult, [pk, "mk"], [K("Nn")])
                P.tt("dve", ABb[i2][:], ps[:, CH:2 * CH], mk[:, d, CH:2 * CH], ALU.mult, [pk, "mk"], [K("ABb")])
                ps, pk = P.next_ps()
                P.mm(ps[:, 0:2 * CH], kinvM[:, h, :], QR[:, h, :], True, True, [kiK, "QR"], [pk])
                P.tt("dve", AK[i2][:], ps[:, 0:2 * CH], mk[:, d, :], ALU.mult, [pk, "mk"], [K("AK")])
                ps, pk = P.next_ps()
                P.mm(ps[:, 0:CH], QR[:, h, 0:CH], binvM[:, h, :], True, True, [biK, "QR"], [pk])
                P.tt("dve", XT_[0][i2][:], ps[:, 0:CH], mkT[:, d, :], ALU.mult, [pk, "mk"], [K("XT0")])
                yield
                ps, pk = P.next_ps()
                P.mm(ps[:, 0:64], QR[:, h, 0:CH], HstM[:, h, :], True, False, ["QR", ("HstM", h)], [pk])
                P.mm(ps[:, 0:64], AK[i2][:, 0:CH], vT[i2][:], False, True, [K("AK"), K("vT")], [pk])
                P.copy(ev_eng(), Rr[0][i2][:], ps[:, 0:64], [pk], [K("R0")])
                yield
                ps, pk = P.next_ps()
                P.mm(ps[:, 0:64], Nn[i2][:], Rr[0][i2][:], True, True, [K("Nn"), K("R0")], [pk])
                P.tt("dve", Rr[1][i2][:], Rr[0][i2][:], ps[:, 0:64], ALU.subtract, [pk, K("R0")], [K("R1")])
                rc = 1
                Xc, XTc = Nn[i2][:], XT_[0][i2][:]
                xck, xtck = K("Nn"), K("XT0")
                for step in range(1, 7):
                    nx = step % 2
                    lastst = step == 6
                    ps, pk = P.next_ps()
                    P.mm(ps[:, 0:CH], XTc, Xc, True, True, [xck, xtck], [pk])
                    P.copy(ev_eng(), X[nx][i2][:], ps[:, 0:CH], [pk], [K("X%d" % nx)])
                    yield
                    if not lastst:
                        ps, pk = P.next_ps()
                        P.tr(ps[:, 0:CH], X[nx][i2][:], G["ident"][:], [K("X%d" % nx)], [pk])
                        P.copy(ev_eng(), XT_[nx][i2][:], ps[:, 0:CH], [pk], [K("XT%d" % nx)])
                    ps, pk = P.next_ps()
                    P.mm(ps[:, 0:64], X[nx][i2][:], Rr[rc][i2][:], True, True, [K("X%d" % nx), K("R%d" % rc)], [pk])
                    if not lastst:
                        P.tt("dve", Rr[1 - rc][i2][:], ps[:, 0:64], Rr[rc][i2][:], ALU.add, [pk, K("R%d" % rc)], [K("R%d" % (1 - rc))])
                    else:
                        P.stt(negU[i2][:], ps[:, 0:64], -1.0, Rr[rc][i2][:], ALU.mult, ALU.subtract, [pk, K("R%d" % rc)], [K("negU")])
                    rc = 1 - rc
                    Xc, XTc = X[nx][i2][:], XT_[nx][i2][:]
                    xck, xtck = K("X%d" % nx), K("XT%d" % nx)
                    yield
                ps, pk = P.next_ps()
                P.mm(ps[0:64, 0:CH], HstM[:, h, :], QR[:, h, CH:2 * CH], True, False, ["QR", ("HstM", h)], [pk])
                P.mm(ps[0:64, 0:CH], vT[i2][:], AK[i2][:, CH:2 * CH], False, False, [K("AK"), K("vT")], [pk])
                P.mm(ps[0:64, 0:CH], negU[i2][:], ABb[i2][:], False, True, [K("ABb"), K("negU")], [pk])
                P.copy(ev_eng(), yt[i2][:], ps[0:64, 0:CH], [pk], [K("yt")])
                P.dma("sp", T["YT%d" % d][h * 64:(h + 1) * 64, tok0:tok0 + CH], yt[i2][:], [K("yt")], ["YT%d" % d])
                ps, pk = P.next_ps()
                P.mm(ps[0:64, 0:64], kdT[i2][:], vT[i2][:], True, False, [K("kdT"), K("vT")], [pk])
                P.mm(ps[0:64, 0:64], bdT[i2][:], negU[i2][:], False, True, [K("bdT"), K("negU")], [pk])
                P.stt(Hst[:, h, :], Hst[:, h, :], wc_t[:, h:h + 1], ps[0:64, 0:64], ALU.mult, ALU.add,
                      [("Hst", h), wck, pk], [("Hst", h)])
                if MD is not F32:
                    P.copy("pool", HstM[:, h, :], Hst[:, h, :], [("Hst", h)], [("HstM", h)])
                yield

            for h0 in range(0, NH, GH):
                gens = [head(h0 + i, i) for i in range(GH)]
                if pe_gen is not None:
                    gens.append(pe_gen)
                alive = list(gens)
                while alive:
                    nxt = []
                    for g_ in alive:
                        try:
                            next(g_)
                            if g_ is pe_gen and len(alive) == 1 and h0 + GH < NH:
                                nxt.append(g_)
                                break
                            nxt.append(g_)
                        except StopIteration:
                            if g_ is pe_gen:
                                pe_gen = None
                    if len(nxt) == 1 and nxt[0] is pe_gen and h0 + GH < NH:
                        break
                    alive = nxt
            if pe_gen is not None:
                run_all([pe_gen])


def stage_rwkv_finish(P, G, T, with_ctx=True):
    NI = 3
    with Stage(P) as S:
        def t5(name, dt=F32):
            return [S.sb("%s%d" % (name, i), [128, 512], dt) for i in range(NI)]
        y0, y1, rr, kk_, vv, gg, a0, a1 = t5("y0"), t5("y1"), t5("r"), t5("k"), t5("v"), t5("g"), t5("a0"), t5("a1")
        ysq, mean, var, tmp, bon = t5("ysq"), t5("mean"), t5("var"), t5("tmp"), t5("bon")
        ao = t5("ao", BF16)

        def body(i2, tok0, nt, c):
            def K(n):
                return (n, i2)
            sl = (slice(c * 128, (c + 1) * 128), slice(tok0, tok0 + nt))
            for (tl, nm) in [(y0, "YT0"), (y1, "YT1"), (rr, "RT"), (kk_, "KT"), (vv, "VT"), (gg, "GT"), (a0, "AA0"), (a1, "AA1")]:
                P.dma("sp", tl[i2][:, :nt], T[nm][sl[0], sl[1]], [nm], [K(nm)])
            yield
            y = y0[i2]
            P.tt("pool", y[:, :nt], y0[i2][:, :nt], y1[i2][:, :nt], ALU.add, [K("YT0"), K("YT1")], [K("YT0")])
            P.tt("pool", tmp[i2][:, :nt], a0[i2][:, :nt], a1[i2][:, :nt], ALU.add, [K("AA0"), K("AA1")], [K("tmp")])
            yield
            ps, pk = P.next_ps()
            P.mm(ps[:, :nt], G["bones"][:], y[:, :nt], True, True, [K("YT0")], [pk])
            P.act(mean[i2][:, :nt], ps[:, :nt], AF.Copy, [pk], [K("mean")], scale=1.0 / 64)
            P.ts("dve", tmp[i2][:, :nt], tmp[i2][:, :nt], -2.0, G["vec128_ka"][:, c:c + 1], ALU.add, ALU.mult, [K("tmp")], [K("tmp")])
            yield
            P.tt("dve", y[:, :nt], y[:, :nt], mean[i2][:, :nt], ALU.subtract, [K("YT0"), K("mean")], [K("YT0")])
            P.act(ysq[i2][:, :nt], y[:, :nt], AF.Square, [K("YT0")], [K("ysq")])
            P.stt(tmp[i2][:, :nt], tmp[i2][:, :nt], 2.0, kk_[i2][:, :nt], ALU.add, ALU.mult, [K("tmp"), K("KT")], [K("tmp")])
            yield
            ps, pk = P.next_ps()
            P.mm(ps[:, :nt], G["bones"][:], ysq[i2][:, :nt], True, True, [K("ysq")], [pk])
            P.act(var[i2][:, :nt], ps[:, :nt], AF.Ln, [pk], [K("var")], bias=G["gneps_t"][:, 0:1], scale=1.0 / 64)
            P.stt(tmp[i2][:, :nt], tmp[i2][:, :nt], G["vec"][:, VO["r_k"] + c:VO["r_k"] + c + 1], rr[i2][:, :nt], ALU.mult, ALU.mult,
                  [K("tmp"), K("RT")], [K("tmp")])
            yield
            P.act(var[i2][:, :nt], var[i2][:, :nt], AF.Exp, [K("var")], [K("var")], scale=-0.5)
            ps, pk = P.next_ps()
            P.mm(ps[:, :nt], G["bones"][:], tmp[i2][:, :nt], True, True, [K("tmp")], [pk])
            P.tt("dve", bon[i2][:, :nt], ps[:, :nt], vv[i2][:, :nt], ALU.mult, [pk, K("VT")], [K("bon")])
            yield
            P.tt("dve", y[:, :nt], y[:, :nt], var[i2][:, :nt], ALU.mult, [K("YT0"), K("var")], [K("YT0")])
            yield
            P.act(y[:, :nt], y[:, :nt], AF.Identity, [K("YT0")], [K("YT0")],
                  scale=G["vec"][:, VO["ln_g"] + c:VO["ln_g"] + c + 1], bias=G["vec"][:, VO["ln_b"] + c:VO["ln_b"] + c + 1])
            yield
            P.tt("pool", y[:, :nt], y[:, :nt], bon[i2][:, :nt], ALU.add, [K("YT0"), K("bon")], [K("YT0")])
            yield
            P.tt("pool", ao[i2][:, :nt], y[:, :nt], gg[i2][:, :nt], ALU.mult, [K("YT0"), K("GT")], [K("ao")])
            P.dma("act", T["AT"][sl[0], sl[1]], ao[i2][:, :nt], [K("ao")], ["AT"])
            yield

        its = [(tok0, nt, c) for (tok0, nt, b, is_ctx, pos0) in token_blocks(with_ctx=with_ctx) for c in range(KC)]
        for g0 in range(0, len(its), NI):
            alive = [body(i, *its[g0 + i]) for i in range(min(NI, len(its) - g0))]
            while alive:
                nxt = []
                for g_ in alive:
                    try:
                        next(g_)
                        nxt.append(g_)
                    except StopIteration:
                        pass
                alive = nxt


INPUT_NAMES = ["w_mod", "mlp_w_in", "mlp_w_out", "attn_w_qkv", "attn_w_o", "rwkv_w_rkv", "rwkv_w1", "rwkv_w2",
               "rwkv_a1", "rwkv_a2", "rwkv_g1", "rwkv_g2", "rwkv_w_o", "pool_w"]
INPUT_SHAPES = {"w_mod": [4, 1024, 6144], "mlp_w_in": [4, 1024, 4096], "mlp_w_out": [4, 4096, 1024],
                "attn_w_qkv": [2, 1024, 1536], "attn_w_o": [2, 1024, 1024], "rwkv_w_rkv": [1, 3, 1024, 1024],
                "rwkv_w1": [1, 2, 1024, 64], "rwkv_w2": [1, 2, 64, 1024], "rwkv_a1": [1, 2, 1024, 64],
                "rwkv_a2": [1, 2, 64, 1024], "rwkv_g1": [1, 1024, 160], "rwkv_g2": [1, 160, 1024],
                "rwkv_w_o": [1, 1024, 1024], "pool_w": [1, 4, 256, 256]}
CONST_SHAPES = {"cvT": [128, KC, 3], "c_vec": [128, NV], "c_ident": [128, 128], "c_ones": [128, 128], "c_bones": [128, 128],
                "c_rot": [128, 128], "c_cos": [128, LAT], "c_sin": [128, LAT], "c_mask": [128, 2, 256], "c_maskT": [128, 2, 128],
                "c_rcnt": [128, 4, LAT], "c_rcntc": [128, 4, CTXL], "c_v64": [64, 2, 16], "c_ka128": [128, KC],
                "c_small": [128, 4]}


def build_program(n_layers=DEPTH, dbg=None):
    nc = bass.Bass("TRN2", target_bir_lowering=False)
    T = {}
    T["x2"] = nc.dram_tensor("x2", [NB, LAT, D], F32, kind="ExternalInput").ap()
    T["ctx2"] = nc.dram_tensor("ctx2", [NB, CTXL, D], F32, kind="ExternalInput").ap()
    for n in INPUT_NAMES:
        T[n] = nc.dram_tensor(n, INPUT_SHAPES[n], F32, kind="ExternalInput").ap()
    for n, s in CONST_SHAPES.items():
        T[n] = nc.dram_tensor(n, s, F32, kind="ExternalInput").ap()
    T["y"] = nc.dram_tensor("y", [NB, LAT, D], F32, kind="ExternalOutput").ap()

    def scr(name, shape, dt=F32):
        kind = "ExternalOutput" if (dbg and name in dbg) else "Internal"
        T[name] = nc.dram_tensor(name, shape, dt, kind=kind).ap()

    scr("XT", [D, NTOK])
    scr("HT", [D, NTOK])
    scr("QT", [D, NTOK], BF16)
    scr("KTs", [256, NTOK], BF16)
    scr("Vs", [NB, 4, 128, NSC, 64], BF16)
    scr("AT", [D, NTOK], BF16)
    for n in ["RT", "KT", "VT", "GT", "SIG0", "SIG1", "AA0", "AA1", "YT0", "YT1"]:
        scr(n, [D, NTOK])

    with ExitStack() as es:
        P = Prog(nc, es)
        G = {}

        def gsb(name, shape, dt=F32):
            return es.enter_context(nc.sbuf_tensor(name, list(shape), dt))

        G["ident"] = gsb("ident", [128, 128])
        G["ones_f"] = gsb("ones_f", [128, 128])
        G["bones"] = gsb("bones", [128, 128])
        G["ones_b"] = gsb("ones_b", [128, 128], BF16)
        G["rot"] = gsb("rot", [128, 128])
        G["vec"] = gsb("vec", [128, NV])
        G["vec128_ka"] = gsb("ka128", [128, KC])
        small = gsb("small", [128, 4])
        G["eps_t"] = small[:, 0:1]
        G["tiny_t"] = small[:, 1:2]
        G["gneps_t"] = small[:, 2:3]
        G["sil"] = gsb("sil", [128, KC, 3])
        G["mod"] = [gsb("mod%d" % L, [128, 48, 3]) for L in range(DEPTH)]
        G["sc1"] = [gsb("sc1_%d" % L, [128, KC, 3]) for L in range(DEPTH)]
        G["sc2"] = [gsb("sc2_%d" % L, [128, KC, 3]) for L in range(DEPTH)]
        for (t, n) in [("ident", "c_ident"), ("ones_f", "c_ones"), ("bones", "c_bones"), ("rot", "c_rot"), ("vec", "c_vec"),
                       ("vec128_ka", "c_ka128")]:
            P.dma("sp", G[t][:], T[n], [], ["consts"])
        P.dma("sp", small[:], T["c_small"], [], ["consts"])
        P.dma("pool", G["ones_b"][:], T["c_ones"], [], ["consts"])
        P.barrier()

        stage_prologue(P, G, T)
        for L in range(n_layers):
            last = L == DEPTH - 1
            kind = L % 3
            j = L // 3
            wc = not last
            if kind == 0:
                stage_attn_qkv(P, G, T, L, j)
                stage_attn_core(P, G, T, wc, bg_layers=(list(range(1, n_layers)) if L == 0 else ()))
                stage_mix_mlp(P, G, T, L, T["attn_w_o"][j], "full", wc)
            elif kind == 1:
                stage_norm_to_HT(P, G, T, L, True)
                stage_rwkv_prep(P, G, T)
                stage_rwkv_scan(P, G, T)
                stage_rwkv_finish(P, G, T, wc)
                stage_mix_mlp(P, G, T, L, T["rwkv_w_o"][j], "full", wc)
            else:
                stage_norm_to_HT(P, G, T, L, wc)
                stage_pool(P, G, T, wc)
                stage_mix_mlp(P, G, T, L, T["pool_w"][j], "pool", wc)
        stage_epilogue(P, G, T)
        P.barrier()
        print("program: %d instructions, %d waits" % (P.n_inst, P.n_wait))
    return nc


def _pm(v, nch=KC):
    return np.ascontiguousarray(np.asarray(v, np.float32).reshape(nch, 128).T)


def host_consts(inp):
    c = {}
    vec = np.zeros((128, NV), np.float32)
    for L in range(DEPTH):
        vec[:, VO["bmod"] + L * 48:VO["bmod"] + (L + 1) * 48] = _pm(inp["b_mod"][L], 48)
        vec[:, VO["g1"] + L * 8:VO["g1"] + (L + 1) * 8] = _pm(inp["norm1_g"][L])
        vec[:, VO["g2"] + L * 8:VO["g2"] + (L + 1) * 8] = _pm(inp["norm2_g"][L])
    for j in range(2):
        vec[:, VO["qg"] + j] = np.tile(inp["attn_q_gain"][j], 2)
        vec[:, VO["kg"] + j] = np.tile(inp["attn_k_gain"][j], 2)
    vec[:, VO["pscale"]:VO["pscale"] + 8] = _pm(inp["pool_scale"][0])
    for m in range(6):
        vec[:, VO["mu"] + m * 8:VO["mu"] + (m + 1) * 8] = _pm(inp["rwkv_mu"][0, m])
    for d in range(2):
        vec[:, VO["w0"] + d * 8:VO["w0"] + (d + 1) * 8] = _pm(inp["rwkv_w0"][0, d])
        vec[:, VO["a0"] + d * 8:VO["a0"] + (d + 1) * 8] = _pm(inp["rwkv_a0"][0, d])
    vec[:, VO["r_k"]:VO["r_k"] + 8] = _pm(inp["rwkv_r_k"][0])
    vec[:, VO["ln_g"]:VO["ln_g"] + 8] = _pm(inp["rwkv_ln_g"][0])
    vec[:, VO["ln_b"]:VO["ln_b"] + 8] = _pm(inp["rwkv_ln_b"][0])
    c["c_vec"] = vec
    c["c_ka128"] = _pm(inp["rwkv_k_a"][0])
    v64 = np.zeros((64, 2, 16), np.float32)
    v64[:, 0, :] = np.asarray(inp["rwkv_k_k"][0], np.float32).reshape(16, 64).T
    v64[:, 1, :] = np.asarray(inp["rwkv_k_a"][0], np.float32).reshape(16, 64).T
    c["c_v64"] = v64
    c["c_ident"] = np.eye(128, dtype=np.float32)
    c["c_ones"] = np.ones((128, 128), np.float32)
    bo = np.zeros((128, 128), np.float32)
    bo[:64, :64] = 1.0
    bo[64:, 64:] = 1.0
    c["c_bones"] = bo
    rot = np.zeros((128, 128), np.float32)
    for hh in range(2):
        for jj in range(32):
            rot[hh * 64 + jj + 32, hh * 64 + jj] = -1.0
            rot[hh * 64 + jj, hh * 64 + jj + 32] = 1.0
    c["c_rot"] = rot
    nf = 16
    inv = (10000.0 ** (-np.arange(nf, dtype=np.float32) / nf)).astype(np.float32)
    t = np.arange(LAT)
    rows = (t // 64).astype(np.float32)
    cols = (t % 64).astype(np.float32)
    ang = np.concatenate([rows[:, None] * inv[None, :], cols[:, None] * inv[None, :]], axis=1).astype(np.float32)
    cosf = np.cos(ang).astype(np.float32).T
    sinf = np.sin(ang).astype(np.float32).T
    c["c_cos"] = np.ascontiguousarray(np.tile(cosf, (4, 1)))
    c["c_sin"] = np.ascontiguousarray(np.tile(sinf, (4, 1)))
    s = np.arange(128)[:, None]
    tt_ = np.arange(128)[None, :]
    mk = np.zeros((128, 2, 256), np.float32)
    mk[:, 0, 0:128] = (tt_ > s)
    mk[:, 0, 128:256] = (tt_ >= s)
    mk[:, 1, 0:128] = (tt_ < s)
    mk[:, 1, 128:256] = (tt_ <= s)
    c["c_mask"] = mk
    mkT = np.zeros((128, 2, 128), np.float32)
    mkT[:, 0, :] = (tt_ < s)
    mkT[:, 1, :] = (tt_ > s)
    c["c_maskT"] = mkT

    def rcnt(Tn):
        out = np.zeros((4, Tn), np.float32)
        tpos = np.arange(Tn)
        for gi, win in enumerate((2, 4, 8, 16)):
            lo = np.clip(tpos - win // 2, 0, Tn)
            hi = np.clip(tpos + win // 2, 0, Tn)
            out[gi] = 1.0 / (hi - lo).astype(np.float32)
        return out
    c["c_rcnt"] = np.ascontiguousarray(np.broadcast_to(rcnt(LAT)[None], (128, 4, LAT))).astype(np.float32)
    c["c_rcntc"] = np.ascontiguousarray(np.broadcast_to(rcnt(CTXL)[None], (128, 4, CTXL))).astype(np.float32)
    sm = np.zeros((128, 4), np.float32)
    sm[:, 0] = EPS
    sm[:, 1] = 1e-30
    sm[:, 2] = 64 * 1e-5
    c["c_small"] = sm
    return c


def make_in_maps(inp, cores):
    consts = host_consts(inp)
    shared = {n: np.ascontiguousarray(np.asarray(inp[n], np.float32)) for n in INPUT_NAMES}
    maps = []
    for i in cores:
        m = dict(shared)
        m.update(consts)
        m["x2"] = np.ascontiguousarray(inp["x"][NB * i:NB * (i + 1)], dtype=np.float32)
        m["ctx2"] = np.ascontiguousarray(inp["ctx"][NB * i:NB * (i + 1)], dtype=np.float32)
        cv = np.stack([inp["c"][NB * i], inp["c"][NB * i + 1], inp["c_ctx"]], axis=0).astype(np.float32)
        m["cvT"] = np.ascontiguousarray(cv.reshape(3, KC, 128).transpose(2, 1, 0))
        maps.append(m)
    return maps


_NC_CACHE = {}


def kernel(**inputs):
    inp = {k: np.asarray(v) for k, v in inputs.items()}
    if "nc" not in _NC_CACHE:
        _NC_CACHE["nc"] = build_program()
    nc = _NC_CACHE["nc"]
    in_maps = make_in_maps(inp, list(range(NCORES)))
    res = run_bass_kernel_spmd(nc, in_maps, core_ids=list(range(NCORES)))
    out = np.concatenate([np.asarray(r["y"]) for r in res.results], axis=0)
    return out.astype(np.float32)
```

```python
from contextlib import ExitStack
import numpy as np
import concourse.bass as bass
import concourse.mybir as mybir
from concourse.bass_utils import run_bass_kernel_spmd

F32 = mybir.dt.float32
BF16 = mybir.dt.bfloat16
AF = mybir.ActivationFunctionType
ALU = mybir.AluOpType
AX = mybir.AxisListType

NCORES = 8
NB = 2
LAT = 4096
CTXL = 256
SEG = LAT + CTXL
NTOK = NB * SEG
D = 1024
KC = 8
DFF = 4096
NSC = SEG // 128
EPS = 1e-6
DEPTH = 4
DEC_C = float(np.exp(-0.5))


class Prog:
    N_DMA_SLOTS = 8
    EPOCH = 30000

    def __init__(self, nc, es):
        self.nc = nc
        self.es = es
        self.eng = {"pe": nc.tensor, "act": nc.scalar, "dve": nc.vector, "pool": nc.gpsimd, "sp": nc.sync}
        self.sem = {}
        self.cnt = {}
        self.cur = {}
        self.epoch = {}
        for e in ["pe", "act", "dve", "pool"]:
            self.epoch[e] = 0
            self._new_epoch(e)
        self.dma_slots = {}
        for q in ["sp", "pool", "act"]:
            lst = []
            for i in range(self.N_DMA_SLOTS):
                name = "d_%s%d" % (q, i)
                self.sem[name] = es.enter_context(nc.semaphore(name))
                self.cnt[name] = 0
                lst.append(name)
            self.dma_slots[q] = lst
        self.dma_rr = {"sp": 0, "pool": 0, "act": 0}
        self.waited = {}
        self.last_w = {}
        self.readers = {}
        self.n_inst = 0
        self.n_wait = 0
        self.ps_tiles = [es.enter_context(nc.psum_tensor("psb%d" % i, [128, 512], F32)) for i in range(8)]
        self.ps_rr = 0

    def _new_epoch(self, e):
        name = "%s#%d" % (e, self.epoch[e])
        self.epoch[e] += 1
        self.sem[name] = self.es.enter_context(self.nc.semaphore("s_" + name.replace("#", "_")))
        self.cnt[name] = 0
        self.cur[e] = name

    def next_ps(self):
        i = self.ps_rr % 8
        self.ps_rr += 1
        return self.ps_tiles[i], ("ps", i)

    def _deps(self, me, reads, writes):
        deps = {}

        def add(t, s):
            if deps.get(t, 0) < s:
                deps[t] = s

        for r in reads:
            if r in self.last_w:
                add(*self.last_w[r])
        for w in writes:
            if w in self.last_w:
                add(*self.last_w[w])
            for t, s in self.readers.get(w, {}).items():
                if t.split("#")[0] == me:
                    continue
                add(t, s)
        return deps

    def _wait(self, me, deps):
        e = self.eng[me]
        for t, s in deps.items():
            if me == "pe" and t.split("#")[0] == "pe":
                continue
            if self.waited.get((me, t), 0) >= s:
                continue
            e.wait_ge(self.sem[t], s)
            self.waited[(me, t)] = s
            self.n_wait += 1

    def _commit(self, tok, reads, writes):
        t, s = tok
        for r in reads:
            d = self.readers.setdefault(r, {})
            if d.get(t, 0) < s:
                d[t] = s
        for w in writes:
            self.last_w[w] = tok
            self.readers[w] = {}

    def op(self, me, fn, reads=(), writes=()):
        self._wait(me, self._deps(me, reads, writes))
        inst = fn(self.eng[me])
        name = self.cur[me]
        self.cnt[name] += 1
        inst.then_inc(self.sem[name], 1)
        self._commit((name, self.cnt[name]), reads, writes)
        if self.cnt[name] >= self.EPOCH:
            self._new_epoch(me)
        self.n_inst += 1
        return inst

    def dma(self, q, out, in_, reads=(), writes=(), **kw):
        deps = self._deps("dma", reads, writes)
        slot = self.dma_slots[q][self.dma_rr[q] % self.N_DMA_SLOTS]
        self.dma_rr[q] += 1
        if self.cnt[slot] > 0 and deps.get(slot, 0) < self.cnt[slot]:
            deps[slot] = self.cnt[slot]
        if self.cnt[slot] >= 16 * 1800:
            raise RuntimeError("dma slot counter too large")
        self._wait(q, deps)
        inst = self.eng[q].dma_start(out=out, in_=in_, **kw)
        self.cnt[slot] += 16
        inst.then_inc(self.sem[slot], 16)
        self._commit((slot, self.cnt[slot]), reads, writes)
        self.n_inst += 1
        return inst

    def barrier(self):
        for me in ["sp", "pool", "act", "dve", "pe"]:
            deps = {t: c for t, c in self.cnt.items() if c > 0}
            if me == "pe":
                deps = {t: c for t, c in deps.items() if t.split("#")[0] != "pe"}
            self._wait(me, deps)
        self.last_w = {}
        self.readers = {}

    def mm(self, out, lhsT, rhs, start, stop, reads, writes):
        return self.op("pe", lambda e: e.matmul(out, lhsT=lhsT, rhs=rhs, start=start, stop=stop), reads, writes)

    def tr(self, out, in_, ident, reads, writes):
        return self.op("pe", lambda e: e.transpose(out=out, in_=in_, identity=ident), reads, writes)

    def act(self, out, in_, func, reads, writes, **kw):
        return self.op("act", lambda e: e.activation(out=out, in_=in_, func=func, **kw), reads, writes)

    def tt(self, eng, out, in0, in1, op, reads, writes):
        return self.op(eng, lambda e: e.tensor_tensor(out=out, in0=in0, in1=in1, op=op), reads, writes)

    def ts(self, eng, out, in0, s1, s2, op0, op1, reads, writes):
        if op1 is None:
            return self.op(eng, lambda e: e.tensor_scalar(out=out, in0=in0, scalar1=s1, scalar2=None, op0=op0),
                           reads, writes)
        return self.op(eng, lambda e: e.tensor_scalar(out=out, in0=in0, scalar1=s1, scalar2=s2, op0=op0, op1=op1),
                       reads, writes)

    def stt(self, out, in0, scalar, in1, op0, op1, reads, writes):
        return self.op("dve", lambda e: e.scalar_tensor_tensor(out=out, in0=in0, scalar=scalar, in1=in1,
                                                               op0=op0, op1=op1), reads, writes)

    def copy(self, eng, out, in_, reads, writes):
        if eng == "act":
            return self.act(out, in_, AF.Copy, reads, writes)
        return self.op(eng, lambda e: e.tensor_copy(out=out, in_=in_), reads, writes)


class Stage:
    _n = [0]

    def __init__(self, P):
        self.P = P
        self.es = ExitStack()
        Stage._n[0] += 1
        self.tag = "s%d_" % Stage._n[0]

    def __enter__(self):
        self.es.__enter__()
        return self

    def sb(self, name, shape, dt=F32):
        return self.es.enter_context(self.P.nc.sbuf_tensor(self.tag + name, list(shape), dt))

    def __exit__(self, *a):
        self.P.barrier()
        return self.es.__exit__(*a)


def token_blocks(with_ctx=True):
    out = []
    for b in range(NB):
        if with_ctx:
            out.append((b * SEG, CTXL, b, True, 0))
        for j in range(LAT // 512):
            out.append((b * SEG + CTXL + j * 512, 512, b, False, j * 512))
    return out


VO = {}
_o = 0
for _n, _w in [("bmod", 4 * 48), ("g1", 32), ("g2", 32), ("qg", 2), ("kg", 2), ("pscale", 8), ("mu", 48),
               ("w0", 16), ("a0", 16), ("r_k", 8), ("ln_g", 8), ("ln_b", 8)]:
    VO[_n] = _o
    _o += _w
NV = _o


def emit_norm(P, G, x_t, xkey, nt, L, which, n, h_t, hkey, tmp, sq, rstd, sqkeys=None):
    sc = G["sc1"] if which == 1 else G["sc2"]
    shm = 0 if which == 1 else 3
    sqk = sqkeys if sqkeys is not None else ["sq"] * KC
    P.act(sq[:, 0:KC, :nt], x_t[:, :, :nt], AF.Square, [xkey], list(set(sqk)))
    ps, pk = P.next_ps()
    for c in range(KC):
        P.mm(ps[:, :nt], G["ones_b"][:], sq[:, c, :nt], c == 0, c == KC - 1, [sqk[c]], [pk])
    if rstd is None:
        rs_t, rs_k = P.next_ps()
        P.act(rs_t[:, :nt], ps[:, :nt], AF.Ln, [pk], [rs_k], bias=G["eps_t"][:, 0:1], scale=1.0 / D)
        P.act(rs_t[:, :nt], rs_t[:, :nt], AF.Exp, [rs_k], [rs_k], scale=-0.5)
        for c in range(KC):
            tp, tk = P.next_ps()
            if tp is rs_t:
                tp, tk = P.next_ps()
            P.tt("dve", tp[:, :nt], x_t[:, c, :nt], rs_t[:, :nt], ALU.mult, [xkey, rs_k], [tk])
            P.act(h_t[:, c, :nt], tp[:, :nt], AF.Identity, [tk], [hkey],
                  scale=sc[L][:, c, n:n + 1], bias=G["mod"][L][:, shm * 8 + c, n:n + 1])
        return
    P.act(rstd[:, :nt], ps[:, :nt], AF.Ln, [pk], ["rstd"], bias=G["eps_t"][:, 0:1], scale=1.0 / D)
    P.act(rstd[:, :nt], rstd[:, :nt], AF.Exp, ["rstd"], ["rstd"], scale=-0.5)
    for c in range(KC):
        tk = ("ntmp", c % 2)
        P.tt("dve" if c % 2 == 0 else "pool", tmp[c % 2][:, :nt], x_t[:, c, :nt], rstd[:, :nt], ALU.mult,
             [xkey, "rstd"], [tk])
        P.act(h_t[:, c, :nt], tmp[c % 2][:, :nt], AF.Identity, [tk], [hkey],
              scale=sc[L][:, c, n:n + 1], bias=G["mod"][L][:, shm * 8 + c, n:n + 1])


def load_w(P, w_sb, w_dram, kcs, key, q="pool"):
    wv = w_dram.rearrange("(kc p) o -> p kc o", p=128)
    for kc in range(kcs):
        P.dma(q, w_sb[:, kc, :], wv[:, kc, :], [], [key])


def mod_steps(P, G, T, S, layers):
    wt = [S.sb("wmod%d" % i, [128, KC, 512]) for i in range(2)]
    sil = G["sil"]
    items = [(L, blk) for L in layers for blk in range(12)]

    def load(i):
        L, blk = items[i]
        wv = T["w_mod"][L].rearrange("(kc p) o -> p kc o", p=128)
        P.dma("sp", wt[i % 2][:], wv[:, :, blk * 512:(blk + 1) * 512], [], [("wmod", i % 2)])

    if items:
        load(0)
        yield
    for i, (L, blk) in enumerate(items):
        if i + 1 < len(items):
            load(i + 1)
        w = wt[i % 2]
        wk = ("wmod", i % 2)
        for mo in range(4):
            j = blk * 4 + mo
            ps, pk = P.next_ps()
            for kc in range(KC):
                P.mm(ps[:, 0:3], w[:, kc, mo * 128:(mo + 1) * 128], sil[:, kc, :], kc == 0, kc == KC - 1,
                     [wk, "sil"], [pk])
            P.ts("dve", G["mod"][L][:, j, :], ps[:, 0:3], G["vec"][:, VO["bmod"] + L * 48 + j:VO["bmod"] + L * 48 + j + 1],
                 None, ALU.add, None, [pk], [("mod", L)])
        if blk == 11:
            for n in range(3):
                P.stt(G["sc1"][L][:, :, n], G["mod"][L][:, 8:16, n], 1.0, G["vec"][:, VO["g1"] + L * 8:VO["g1"] + L * 8 + 8],
                      ALU.add, ALU.mult, [("mod", L)], [("mod", L)])
                P.stt(G["sc2"][L][:, :, n], G["mod"][L][:, 32:40, n], 1.0, G["vec"][:, VO["g2"] + L * 8:VO["g2"] + L * 8 + 8],
                      ALU.add, ALU.mult, [("mod", L)], [("mod", L)])
        yield


def stage_prologue(P, G, T):
    nc = P.nc
    with Stage(P) as S:
        cv = S.sb("cv", [128, KC, 3])
        P.dma("sp", cv[:], T["cvT"], [], ["cv"])
        P.act(G["sil"][:], cv[:], AF.Silu, ["cv"], ["sil"])
        for _ in mod_steps(P, G, T, S, [0]):
            pass
    with Stage(P) as S:
        xtm = [S.sb("xtm%d" % i, [128, 4, D]) for i in range(2)]
        xT = [S.sb("xT%d" % i, [128, KC, 512]) for i in range(2)]
        XTv = T["XT"].rearrange("(c p) t -> p c t", p=128)
        for bi, (tok0, nt, b, is_ctx, pos0) in enumerate(token_blocks()):
            xm = xtm[bi % 2]
            xk = ("xtm", bi % 2)
            src = T["ctx2"][b] if is_ctx else T["x2"][b, pos0:pos0 + nt]
            nts = nt // 128
            P.dma("sp", xm[:, 0:nts, :], src.rearrange("(ts p) d -> p ts d", p=128), [], [xk])
            xo = xT[bi % 2]
            ok = ("xT", bi % 2)
            for c in range(KC):
                ps, pk = P.next_ps()
                for ts_ in range(nts):
                    P.tr(ps[:, ts_ * 128:(ts_ + 1) * 128], xm[:, ts_, c * 128:(c + 1) * 128], G["ident"][:], [xk], [pk])
                P.copy("act" if c % 2 == 0 else "dve", xo[:, c, :nt], ps[:, :nt], [pk], [ok])
            P.dma("act", XTv[:, :, tok0:tok0 + nt], xo[:, :, :nt], [ok], ["XT"])


def stage_epilogue(P, G, T):
    with Stage(P) as S:
        xT = [S.sb("xT%d" % i, [128, KC, 512]) for i in range(2)]
        yo = [S.sb("yo%d" % i, [128, 4, D]) for i in range(2)]
        XTv = T["XT"].rearrange("(c p) t -> p c t", p=128)
        for bi, (tok0, nt, b, is_ctx, pos0) in enumerate(token_blocks(with_ctx=False)):
            xi = xT[bi % 2]
            xk = ("xT", bi % 2)
            P.dma("sp", xi[:], XTv[:, :, tok0:tok0 + nt], ["XT"], [xk])
            y = yo[bi % 2]
            yk = ("yo", bi % 2)
            for ts_ in range(4):
                for half in range(2):
                    ps, pk = P.next_ps()
                    for cc in range(4):
                        c = half * 4 + cc
                        P.tr(ps[:, cc * 128:(cc + 1) * 128], xi[:, c, ts_ * 128:(ts_ + 1) * 128], G["ident"][:], [xk], [pk])
                    P.copy("act" if half == 0 else "dve", y[:, ts_, half * 512:(half + 1) * 512], ps[:, :], [pk], [yk])
            P.dma("act", T["y"][b, pos0:pos0 + nt].rearrange("(ts p) d -> p ts d", p=128), y[:], [yk], ["yout"])


def stage_attn_qkv(P, G, T, L, j):
    with Stage(P) as S:
        w = S.sb("wqkv", [128, KC, 1536], BF16)
        load_w(P, w, T["attn_w_qkv"][j], KC, "wqkv")
        cosT = S.sb("cosT", [128, LAT])
        sinT = S.sb("sinT", [128, LAT])
        P.dma("sp", cosT[:], T["c_cos"], [], ["cos"])
        P.dma("sp", sinT[:], T["c_sin"], [], ["sin"])
        xs = [S.sb("x%d" % i, [128, KC, 512]) for i in range(2)]
        h = S.sb("h", [128, KC, 512], BF16)
        sq = S.sb("sq", [128, KC, 512], BF16)
        rstd = S.sb("rstd", [128, 512])
        tmp = [S.sb("ntmp%d" % i, [128, 512]) for i in range(2)]
        qs = [S.sb("qs%d" % i, [128, 512]) for i in range(2)]
        q2 = [S.sb("q2%d" % i, [128, 512]) for i in range(2)]
        rs = [S.sb("rs%d" % i, [128, 512]) for i in range(2)]
        qn = [S.sb("qn%d" % i, [128, 512]) for i in range(2)]
        t1 = [S.sb("t1%d" % i, [128, 512]) for i in range(2)]
        t2 = [S.sb("t2%d" % i, [128, 512]) for i in range(2)]
        qo = [S.sb("qo%d" % i, [128, 512], BF16) for i in range(4)]
        vo = [S.sb("vo%d" % i, [128, 256], BF16) for i in range(2)]
        XTv = T["XT"].rearrange("(c p) t -> p c t", p=128)
        blks = token_blocks()
        P.dma("sp", xs[0][:, :, :blks[0][1]], XTv[:, :, blks[0][0]:blks[0][0] + blks[0][1]], ["XT"], [("x", 0)])
        it = 0
        vi = 0
        for bi, (tok0, nt, b, is_ctx, pos0) in enumerate(blks):
            if bi + 1 < len(blks):
                t0n, ntn = blks[bi + 1][0], blks[bi + 1][1]
                P.dma("sp", xs[(bi + 1) % 2][:, :, :ntn], XTv[:, :, t0n:t0n + ntn], ["XT"], [("x", (bi + 1) % 2)])
            n = 2 if is_ctx else b
            emit_norm(P, G, xs[bi % 2], ("x", bi % 2), nt, L, 1, n, h, "h", tmp, sq, rstd)
            def qk_chain(mo, i2, i3):
                isq = mo < 8
                ps, pk = P.next_ps()
                for kc in range(KC):
                    P.mm(ps[:, :nt], w[:, kc, mo * 128:(mo + 1) * 128], h[:, kc, :nt], kc == 0, kc == KC - 1, ["wqkv", "h"], [pk])
                P.copy("act", qs[i2][:, :nt], ps[:, :nt], [pk], [("qs", i2)])
                yield
                P.tt("dve", q2[i2][:, :nt], ps[:, :nt], qs[i2][:, :nt], ALU.mult, [pk, ("qs", i2)], [("q2", i2)])
                yield
                ps2, pk2 = P.next_ps()
                P.mm(ps2[:, :nt], G["bones"][:], q2[i2][:, :nt], True, True, [("q2", i2)], [pk2])
                P.act(rs[i2][:, :nt], ps2[:, :nt], AF.Ln, [pk2], [("rs", i2)], bias=G["eps_t"][:, 0:1], scale=1.0 / 64)
                yield
                P.act(rs[i2][:, :nt], rs[i2][:, :nt], AF.Exp, [("rs", i2)], [("rs", i2)], scale=-0.5)
                yield
                gcol = (VO["qg"] if isq else VO["kg"]) + j
                oq = qo[i3]
                okk = ("qo", i3)
                if is_ctx:
                    P.stt(oq[:, :nt], qs[i2][:, :nt], G["vec"][:, gcol:gcol + 1], rs[i2][:, :nt], ALU.mult, ALU.mult,
                          [("qs", i2), ("rs", i2)], [okk])
                else:
                    P.stt(qn[i2][:, :nt], qs[i2][:, :nt], G["vec"][:, gcol:gcol + 1], rs[i2][:, :nt], ALU.mult, ALU.mult,
                          [("qs", i2), ("rs", i2)], [("qn", i2)])
                    yield
                    ps3, pk3 = P.next_ps()
                    P.mm(ps3[:, :nt], G["rot"][:], qn[i2][:, :nt], True, True, [("qn", i2)], [pk3])
                    P.tt("pool", t1[i2][:, :nt], qn[i2][:, :nt], cosT[:, pos0:pos0 + nt], ALU.mult, [("qn", i2), "cos"], [("t1", i2)])
                    yield
                    P.tt("dve", t2[i2][:, :nt], ps3[:, :nt], sinT[:, pos0:pos0 + nt], ALU.mult, [pk3, "sin"], [("t2", i2)])
                    yield
                    P.tt("pool", oq[:, :nt], t1[i2][:, :nt], t2[i2][:, :nt], ALU.add, [("t1", i2), ("t2", i2)], [okk])
                yield
                if isq:
                    P.dma("sp", T["QT"][mo * 128:(mo + 1) * 128, tok0:tok0 + nt], oq[:, :nt], [okk], ["QT"])
                else:
                    P.dma("sp", T["KTs"][(mo - 8) * 128:(mo - 7) * 128, tok0:tok0 + nt], oq[:, :nt], [okk], ["KTs"])
                yield

            for mo0 in range(0, 10, 2):
                alive = [qk_chain(mo0 + i, i, (it + i) % 4) for i in range(2)]
                it += 2
                while alive:
                    nxt = []
                    for g_ in alive:
                        try:
                            next(g_)
                            nxt.append(g_)
                        except StopIteration:
                            pass
                    alive = nxt
            for ts_ in range(nt // 128):
                ps, pk = P.next_ps()
                for kc in range(KC):
                    P.mm(ps[:, 0:256], h[:, kc, ts_ * 128:(ts_ + 1) * 128], w[:, kc, 1280:1536], kc == 0, kc == KC - 1,
                         ["wqkv", "h"], [pk])
                v = vo[vi % 2]
                vk = ("vo", vi % 2)
                vi += 1
                P.copy("act", v[:, :], ps[:, 0:256], [pk], [vk])
                sc = (tok0 - b * SEG) // 128 + ts_
                P.dma("sp", T["Vs"][b, :, :, sc, :].rearrange("g s d -> s g d"), v[:, :].rearrange("s (g d) -> s g d", g=4),
                      [vk], ["Vs"])


def stage_attn_core(P, G, T, with_ctx_out, bg_layers=()):
    LOOK = 3
    with Stage(P) as S:
        kTs = [[S.sb("kT%d_%d" % (i, hh), [128, SEG], BF16) for hh in range(2)] for i in range(2)]
        for i in range(2):
            P.op("pool", lambda e: e.memset(kTs[i][0][64:128, :], 0.0), [], [("kT", i)])
            P.op("pool", lambda e: e.memset(kTs[i][1][0:64, :], 0.0), [], [("kT", i)])
        Vall = S.sb("Vall", [128, NSC, 4, 128], BF16)
        qt = [S.sb("qt%d" % i, [128, 512], BF16) for i in range(3)]
        pt = [S.sb("pt%d" % i, [128, 512], BF16) for i in range(6)]
        rl = [S.sb("rl%d" % i, [128, 512]) for i in range(2)]
        on = [S.sb("on%d" % i, [64, 512], BF16) for i in range(2)]
        P.op("pool", lambda e: e.memset(Vall[:, :, :, 64:128], 1.0), [], ["Vones"])
        pob = [[(P.ps_tiles[0], ("ps", 0)), (P.ps_tiles[1], ("ps", 1))], [(P.ps_tiles[2], ("ps", 2)), (P.ps_tiles[3], ("ps", 3))]]
        sb_ = [(P.ps_tiles[i], ("ps", i)) for i in range(4, 8)]
        work = []
        for b in range(NB):
            for g in range(4):
                for pair in range(2):
                    for (tok0, nt, bb, is_ctx, pos0) in token_blocks(with_ctx=with_ctx_out):
                        if bb == b:
                            work.append((b, g, pair, tok0, nt, is_ctx))
        kcur = {}
        cnt = {"s": 0, "p": 0, "o": 0, "k": 0}

        def load_q(wi):
            b, g, pair, tok0, nt, is_ctx = work[wi]
            qc = 2 * g + pair
            P.dma("sp", qt[wi % 3][:, :nt], T["QT"][qc * 128:(qc + 1) * 128, tok0:tok0 + nt], ["QT"], [("qt", wi % 3)])

        def load_kv(wi):
            b, g, pair, tok0, nt, is_ctx = work[wi]
            if (b, g) not in kcur:
                ki = cnt["k"] % 2
                cnt["k"] += 1
                kcur[(b, g)] = ki
                for hf in range(2):
                    P.dma("sp", kTs[ki][hf][hf * 64:(hf + 1) * 64, :], T["KTs"][g * 64:(g + 1) * 64, b * SEG:(b + 1) * SEG], ["KTs"], [("kT", ki)])

        bg = mod_steps(P, G, T, S, list(bg_layers)) if bg_layers else None
        load_kv(0)
        load_q(0)
        for wi, (b, g, pair, tok0, nt, is_ctx) in enumerate(work):
            if bg is not None and wi % 3 == 2:
                if next(bg, "done") == "done":
                    bg = None
            if ("v", b) not in kcur:
                kcur[("v", b)] = 1
                for g2 in range(4):
                    P.dma("sp", Vall[:, :, g2, 0:64], T["Vs"][b, g2], ["Vs"], ["Vall"])
            if wi + 1 < len(work):
                load_kv(wi + 1)
                load_q(wi + 1)
            ki = kcur[(b, g)]
            kT = kTs[ki]
            q = qt[wi % 3]
            qk = ("qt", wi % 3)
            qc = 2 * g + pair
            nsc = (CTXL // 128) if is_ctx else NSC
            po = pob[wi % 2]
            items = [(sc, hh) for sc in range(nsc) for hh in range(2)]
            sbank = {}

            def emit_S(i):
                sc, hh = items[i]
                ps, pk = sb_[cnt["s"] % 4]
                cnt["s"] += 1
                P.mm(ps[:, :nt], kT[hh][:, sc * 128:(sc + 1) * 128], q[:, :nt], True, True, [("kT", ki), qk], [pk])
                sbank[i] = (ps, pk)

            for i in range(min(LOOK, len(items))):
                emit_S(i)
            for i, (sc, hh) in enumerate(items):
                if i + LOOK < len(items):
                    emit_S(i + LOOK)
                ps, pk = sbank.pop(i)
                p_ = pt[cnt["p"] % 6]
                ptk = ("pt", cnt["p"] % 6)
                cnt["p"] += 1
                P.act(p_[:, :nt], ps[:, :nt], AF.Exp, [pk], [ptk], scale=0.125)
                P.mm(po[hh][0][:, :nt], Vall[:, sc, g, :], p_[:, :nt], sc == 0, sc == nsc - 1,
                     ["Vall", "Vones", ptk], [po[hh][1]])
            for hh in range(2):
                oi = cnt["o"] % 2
                cnt["o"] += 1
                r_ = rl[oi]
                o_ = on[oi]
                rk, ok = ("rl", oi), ("on", oi)
                P.op("dve", lambda e: e.reciprocal(out=r_[64:128, :nt], in_=po[hh][0][64:128, :nt]), [po[hh][1]], [rk])
                P.tt("dve", o_[:, :nt], po[hh][0][0:64, :nt], r_[64:128, :nt], ALU.mult, [po[hh][1], rk], [ok])
                hq = qc * 2 + hh
                P.dma("sp", T["AT"][hq * 64:(hq + 1) * 64, tok0:tok0 + nt], o_[:, :nt], [ok], ["AT"])
        if bg is not None:
            for _ in bg:
                pass


def stage_mixout(P, G, T, L, w_dram, kind, with_ctx):
    with Stage(P) as S:
        if kind == "full":
            w = S.sb("wo", [128, KC, D], BF16)
            load_w(P, w, w_dram, KC, "wo")
        else:
            w = S.sb("wo", [128, 8, 256], BF16)
            for g in range(4):
                for k2 in range(2):
                    P.dma("pool", w[:, g * 2 + k2, :], w_dram[g, k2 * 128:(k2 + 1) * 128, :], [], ["wo"])
            gs = S.sb("gs", [128, KC, 3])
            for n in range(3):
                P.tt("dve", gs[:, :, n], G["mod"][L][:, 16:24, n], G["vec"][:, VO["pscale"]:VO["pscale"] + 8], ALU.mult, ["mod"], ["gs"])
        xs = [S.sb("x%d" % i, [128, KC, 512]) for i in range(2)]
        at = [S.sb("a%d" % i, [128, KC, 512], BF16) for i in range(2)]
        XTv = T["XT"].rearrange("(c p) t -> p c t", p=128)
        ATv = T["AT"].rearrange("(c p) t -> p c t", p=128)
        for bi, (tok0, nt, b, is_ctx, pos0) in enumerate(token_blocks(with_ctx=with_ctx)):
            x = xs[bi % 2]
            a = at[bi % 2]
            xk, ak = ("x", bi % 2), ("a", bi % 2)
            P.dma("sp", x[:, :, :nt], XTv[:, :, tok0:tok0 + nt], ["XT"], [xk])
            P.dma("sp", a[:, :, :nt], ATv[:, :, tok0:tok0 + nt], ["AT"], [ak])
            n = 2 if is_ctx else b
            for mo in range(KC):
                ps, pk = P.next_ps()
                if kind == "full":
                    for kc in range(KC):
                        P.mm(ps[:, :nt], w[:, kc, mo * 128:(mo + 1) * 128], a[:, kc, :nt], kc == 0, kc == KC - 1, ["wo", ak], [pk])
                    gate = G["mod"][L][:, 16 + mo, n:n + 1]
                    gk = "mod"
                else:
                    g = mo // 2
                    for k2 in range(2):
                        P.mm(ps[:, :nt], w[:, g * 2 + k2, (mo % 2) * 128:(mo % 2 + 1) * 128], a[:, g * 2 + k2, :nt], k2 == 0, k2 == 1,
                             ["wo", ak], [pk])
                    gate = gs[:, mo, n:n + 1]
                    gk = "gs"
                P.stt(x[:, mo, :nt], ps[:, :nt], gate, x[:, mo, :nt], ALU.mult, ALU.add, [pk, xk, gk], [xk])
            P.dma("sp", XTv[:, :, tok0:tok0 + nt], x[:, :, :nt], [xk], ["XT"])


def stage_mix_mlp(P, G, T, L, w_dram, kind, with_ctx):
    with Stage(P) as S:
        if kind == "full":
            w = S.sb("wo", [128, KC, D], BF16)
            load_w(P, w, w_dram, KC, "wo")
        else:
            w = S.sb("wo", [128, 8, 256], BF16)
            for g in range(4):
                for k2 in range(2):
                    P.dma("pool", w[:, g * 2 + k2, :], w_dram[g, k2 * 128:(k2 + 1) * 128, :], [], ["wo"])
            gs = S.sb("gs", [128, KC, 3])
            for n in range(3):
                P.tt("dve", gs[:, :, n], G["mod"][L][:, 16:24, n], G["vec"][:, VO["pscale"]:VO["pscale"] + 8], ALU.mult, ["mod"], ["gs"])
        w1 = S.sb("w1", [128, KC, DFF], BF16)
        w2 = S.sb("w2", [128, 32, D], BF16)
        load_w(P, w1, T["mlp_w_in"][L], KC, "w1")
        load_w(P, w2, T["mlp_w_out"][L], 32, "w2")
        x = S.sb("x", [128, KC, 512])
        h = S.sb("h", [128, KC, 512], BF16)
        a = h
        u = S.sb("u", [128, 32, 512], BF16)
        XTv = T["XT"].rearrange("(c p) t -> p c t", p=128)
        ATv = T["AT"].rearrange("(c p) t -> p c t", p=128)
        ri = 0
        for bi, (tok0, nt, b, is_ctx, pos0) in enumerate(token_blocks(with_ctx=with_ctx)):
            P.dma("sp", x[:, :, :nt], XTv[:, :, tok0:tok0 + nt], ["XT"], ["x"])
            P.dma("sp", a[:, :, :nt], ATv[:, :, tok0:tok0 + nt], ["AT"], ["h"])
            n = 2 if is_ctx else b
            for mo in range(KC):
                ps, pk = P.next_ps()
                if kind == "full":
                    for kc in range(KC):
                        P.mm(ps[:, :nt], w[:, kc, mo * 128:(mo + 1) * 128], a[:, kc, :nt], kc == 0, kc == KC - 1, ["wo", "h"], [pk])
                    gate = G["mod"][L][:, 16 + mo, n:n + 1]
                    gk = "mod"
                else:
                    g = mo // 2
                    for k2 in range(2):
                        P.mm(ps[:, :nt], w[:, g * 2 + k2, (mo % 2) * 128:(mo % 2 + 1) * 128], a[:, g * 2 + k2, :nt], k2 == 0, k2 == 1,
                             ["wo", "h"], [pk])
                    gate = gs[:, mo, n:n + 1]
                    gk = "gs"
                P.stt(x[:, mo, :nt], ps[:, :nt], gate, x[:, mo, :nt], ALU.mult, ALU.add, [pk, "x", gk], ["x"])
            emit_norm(P, G, x, "x", nt, L, 2, n, h, "h", None, u, None, sqkeys=[("u", c) for c in range(KC)])
            for mo in range(32):
                ps, pk = P.next_ps()
                for kc in range(KC):
                    P.mm(ps[:, :nt], w1[:, kc, mo * 128:(mo + 1) * 128], h[:, kc, :nt], kc == 0, kc == KC - 1, ["w1", "h"], [pk])
                P.act(ps[:, :nt], ps[:, :nt], AF.Relu, [pk], [pk])
                P.act(u[:, mo, :nt], ps[:, :nt], AF.Square, [pk], [("u", mo)])
            for mo in range(KC):
                ps, pk = P.next_ps()
                for kc in range(32):
                    P.mm(ps[:, :nt], w2[:, kc, mo * 128:(mo + 1) * 128], u[:, kc, :nt], kc == 0, kc == 31, ["w2", ("u", kc)], [pk])
                P.stt(x[:, mo, :nt], ps[:, :nt], G["mod"][L][:, 40 + mo, n:n + 1], x[:, mo, :nt], ALU.mult, ALU.add,
                      [pk, "x", ("mod", L)], ["x"])
            P.dma("sp", XTv[:, :, tok0:tok0 + nt], x[:, :, :nt], ["x"], ["XT"])


def stage_mlp(P, G, T, L, with_ctx):
    with Stage(P) as S:
        w1 = S.sb("w1", [128, KC, DFF], BF16)
        w2 = S.sb("w2", [128, 32, D], BF16)
        load_w(P, w1, T["mlp_w_in"][L], KC, "w1")
        load_w(P, w2, T["mlp_w_out"][L], 32, "w2")
        x = S.sb("x", [128, KC, 512])
        h = S.sb("h", [128, KC, 512], BF16)
        u = S.sb("u", [128, 32, 512], BF16)
        sq = S.sb("sq", [128, KC, 512], BF16)
        rstd = S.sb("rstd", [128, 512])
        tmp = [S.sb("ntmp%d" % i, [128, 512]) for i in range(2)]
        rr = tmp
        XTv = T["XT"].rearrange("(c p) t -> p c t", p=128)
        ri = 0
        for bi, (tok0, nt, b, is_ctx, pos0) in enumerate(token_blocks(with_ctx=with_ctx)):
            P.dma("sp", x[:, :, :nt], XTv[:, :, tok0:tok0 + nt], ["XT"], ["x"])
            n = 2 if is_ctx else b
            emit_norm(P, G, x, "x", nt, L, 2, n, h, "h", tmp, sq, rstd)
            for mo in range(32):
                ps, pk = P.next_ps()
                for kc in range(KC):
                    P.mm(ps[:, :nt], w1[:, kc, mo * 128:(mo + 1) * 128], h[:, kc, :nt], kc == 0, kc == KC - 1, ["w1", "h"], [pk])
                r_ = rr[ri % 2]
                rk = ("ntmp", ri % 2)
                ri += 1
                P.act(r_[:, :nt], ps[:, :nt], AF.Relu, [pk], [rk])
                P.tt("pool" if mo % 4 != 3 else "dve", u[:, mo, :nt], r_[:, :nt], r_[:, :nt], ALU.mult, [rk], [("u", mo)])
            for mo in range(KC):
                ps, pk = P.next_ps()
                for kc in range(32):
                    P.mm(ps[:, :nt], w2[:, kc, mo * 128:(mo + 1) * 128], u[:, kc, :nt], kc == 0, kc == 31, ["w2", ("u", kc)], [pk])
                P.stt(x[:, mo, :nt], ps[:, :nt], G["mod"][L][:, 40 + mo, n:n + 1], x[:, mo, :nt], ALU.mult, ALU.add,
                      [pk, "x", ("mod", L)], ["x"])
            P.dma("sp", XTv[:, :, tok0:tok0 + nt], x[:, :, :nt], ["x"], ["XT"])


def stage_norm_to_HT(P, G, T, L, with_ctx):
    with Stage(P) as S:
        xs = [S.sb("x%d" % i, [128, KC, 512]) for i in range(2)]
        hs = [S.sb("h%d" % i, [128, KC, 512]) for i in range(2)]
        sq = S.sb("sq", [128, KC, 512], BF16)
        rstd = S.sb("rstd", [128, 512])
        tmp = [S.sb("ntmp%d" % i, [128, 512]) for i in range(2)]
        XTv = T["XT"].rearrange("(c p) t -> p c t", p=128)
        HTv = T["HT"].rearrange("(c p) t -> p c t", p=128)
        for bi, (tok0, nt, b, is_ctx, pos0) in enumerate(token_blocks(with_ctx=with_ctx)):
            x, h = xs[bi % 2], hs[bi % 2]
            xk, hk = ("x", bi % 2), ("h", bi % 2)
            P.dma("sp", x[:, :, :nt], XTv[:, :, tok0:tok0 + nt], ["XT"], [xk])
            emit_norm(P, G, x, xk, nt, L, 1, 2 if is_ctx else b, h, hk, tmp, sq, rstd)
            P.dma("act", HTv[:, :, tok0:tok0 + nt], h[:, :, :nt], [hk], ["HT"])


def stage_pool(P, G, T, with_ctx):
    with Stage(P) as S:
        PADL = 8
        hb = [S.sb("hb%d" % i, [128, LAT + 24]) for i in range(2)]
        d2 = S.sb("d2", [128, LAT + 24])
        d4 = S.sb("d4", [128, LAT + 24])
        rc = S.sb("rc", [128, 4, LAT])
        rcc = S.sb("rcc", [128, 4, CTXL])
        pm = [S.sb("pm%d" % i, [128, LAT]) for i in range(2)]
        po = [S.sb("po%d" % i, [128, LAT], BF16) for i in range(2)]
        P.dma("sp", rc[:], T["c_rcnt"], [], ["rc"])
        P.dma("sp", rcc[:], T["c_rcntc"], [], ["rc"])
        for i in range(2):
            P.op("pool", lambda e: e.memset(hb[i][:], 0.0), [], [("hb", i)])
        it = 0
        segs = []
        for b in range(NB):
            if with_ctx:
                segs.append((b * SEG, CTXL, True))
            segs.append((b * SEG + CTXL, LAT, False))
        for (tok0, Tn, is_ctx) in segs:
            for c in range(KC):
                gi = c // 2
                win = (2, 4, 8, 16)[gi]
                i2 = it % 2
                it += 1
                hbt = hb[i2]
                hk = ("hb", i2)
                if Tn < LAT:
                    P.op("pool", lambda e: e.memset(hbt[:, PADL + Tn:PADL + Tn + 16], 0.0), [], [hk])
                P.dma("sp", hbt[:, PADL:PADL + Tn], T["HT"][c * 128:(c + 1) * 128, tok0:tok0 + Tn], ["HT"], [hk])
                W_ = Tn + 16
                src = hbt
                sk = hk
                cur = 1
                bufs = [(d2, "d2"), (d4, "d4")]
                bi_ = 0
                while cur < win:
                    dst, dk = bufs[bi_ % 2]
                    bi_ += 1
                    P.tt("dve" if cur in (1, 4) else "pool", dst[:, 0:W_ - cur], src[:, 0:W_ - cur], src[:, cur:W_], ALU.add, [sk], [dk])
                    src, sk = dst, dk
                    cur *= 2
                off = PADL - win // 2
                rct = rcc if is_ctx else rc
                pmt, pot = pm[i2], po[i2]
                P.tt("dve", pmt[:, :Tn], src[:, off:off + Tn], rct[:, gi, :Tn], ALU.mult, [sk, "rc"], [("pm", i2)])
                P.tt("pool", pot[:, :Tn], pmt[:, :Tn], hbt[:, PADL:PADL + Tn], ALU.subtract, [("pm", i2), hk], [("po", i2)])
                P.dma("sp", T["AT"][c * 128:(c + 1) * 128, tok0:tok0 + Tn], pot[:, :Tn], [("po", i2)], ["AT"])


def stage_rwkv_prep(P, G, T, with_ctx=True):
    with Stage(P) as S:
        wr = [S.sb("wrkv%d" % i, [128, KC, D], BF16) for i in range(3)]
        for i in range(3):
            load_w(P, wr[i], T["rwkv_w_rkv"][0, i], KC, "wrkv")
        wg1 = S.sb("wg1", [128, KC, 160], BF16)
        load_w(P, wg1, T["rwkv_g1"][0], KC, "wl")
        wg2a = S.sb("wg2a", [128, D], BF16)
        wg2b = S.sb("wg2b", [32, D], BF16)
        P.dma("pool", wg2a[:], T["rwkv_g2"][0, 0:128, :], [], ["wl"])
        P.dma("pool", wg2b[:], T["rwkv_g2"][0, 128:160, :], [], ["wl"])
        w1 = S.sb("w1", [128, KC, 2, 128], BF16)
        for d in range(2):
            for kc in range(KC):
                P.dma("pool", w1[:, kc, d, 0:64], T["rwkv_w1"][0, d, kc * 128:(kc + 1) * 128, :], [], ["wl"])
                P.dma("pool", w1[:, kc, d, 64:128], T["rwkv_a1"][0, d, kc * 128:(kc + 1) * 128, :], [], ["wl"])
        w2 = S.sb("w2", [64, 2, D], BF16)
        a2 = S.sb("a2", [64, 2, D], BF16)
        for d in range(2):
            P.dma("pool", w2[:, d, :], T["rwkv_w2"][0, d], [], ["wl"])
            P.dma("pool", a2[:, d, :], T["rwkv_a2"][0, d], [], ["wl"])
        hb = S.sb("hb", [128, KC, 514])
        xx = S.sb("xx", [128, KC, 512])
        xms = [S.sb("xm%d" % i, [128, 6, KC, 512], BF16) for i in range(1)]
        lo = S.sb("lo", [128, 2, 512], BF16)
        la = S.sb("la", [128, 2, 512], BF16)
        lg = S.sb("lg", [128, 2, 512], BF16)
        ev = [S.sb("ev%d" % i, [128, 512]) for i in range(4)]
        HTv = T["HT"].rearrange("(c p) t -> p c t", p=128)
        ei = 0
        for bi, (tok0, nt, b, is_ctx, pos0) in enumerate(token_blocks(with_ctx=with_ctx)):
            xm = xms[0]
            xb = 0
            seg0 = b * SEG if is_ctx else b * SEG + CTXL
            segn = CTXL if is_ctx else LAT
            first = tok0 == seg0
            last = tok0 + nt == seg0 + segn
            lo_ = 0 if first else -1
            hi_ = 0 if last else 1
            if first:
                P.op("pool", lambda e: e.memset(hb[:, :, 0:1], 0.0), [], ["hb"])
            if last:
                P.op("pool", lambda e: e.memset(hb[:, :, nt + 1:nt + 2], 0.0), [], ["hb"])
            P.dma("sp", hb[:, :, 1 + lo_:1 + nt + hi_], HTv[:, :, tok0 + lo_:tok0 + nt + hi_], ["HT"], ["hb"])
            P.tt("pool", xx[:, :, :nt], hb[:, :, 0:nt], hb[:, :, 2:nt + 2], ALU.add, ["hb"], ["xx"])
            P.stt(xx[:, :, :nt], xx[:, :, :nt], 0.5, hb[:, :, 1:nt + 1], ALU.mult, ALU.subtract, ["xx", "hb"], ["xx"])
            for m in range(6):
                for c in range(KC):
                    P.stt(xm[:, m, c, :nt], xx[:, c, :nt], G["vec"][:, VO["mu"] + m * 8 + c:VO["mu"] + m * 8 + c + 1], hb[:, c, 1:nt + 1],
                          ALU.mult, ALU.add, ["xx", "hb"], [("xm", xb, m)])
            for i, (m, dst) in enumerate([(0, "RT"), (2, "KT"), (3, "VT")]):
                for mo in range(KC):
                    ps, pk = P.next_ps()
                    for kc in range(KC):
                        P.mm(ps[:, :nt], wr[i][:, kc, mo * 128:(mo + 1) * 128], xm[:, m, kc, :nt], kc == 0, kc == KC - 1,
                             ["wrkv", ("xm", xb, m)], [pk])
                    e_ = ev[ei % 4]
                    ek = ("ev", ei % 4)
                    ei += 1
                    P.copy("act" if mo % 2 == 0 else "dve", e_[:, :nt], ps[:, :nt], [pk], [ek])
                    P.dma("act", T[dst][mo * 128:(mo + 1) * 128, tok0:tok0 + nt], e_[:, :nt], [ek], [dst])
            for part, (o0, osz) in enumerate([(0, 128), (128, 32)]):
                ps, pk = P.next_ps()
                for kc in range(KC):
                    P.mm(ps[0:osz, :nt], wg1[:, kc, o0:o0 + osz], xm[:, 5, kc, :nt], kc == 0, kc == KC - 1, ["wl", ("xm", xb, 5)], [pk])
                P.act(lg[0:osz, part, :nt], ps[0:osz, :nt], AF.Sigmoid, [pk], ["lg"])
            for mo in range(KC):
                ps, pk = P.next_ps()
                P.mm(ps[:, :nt], wg2a[:, mo * 128:(mo + 1) * 128], lg[:, 0, :nt], True, False, ["wl", "lg"], [pk])
                P.mm(ps[:, :nt], wg2b[:, mo * 128:(mo + 1) * 128], lg[0:32, 1, :nt], False, True, ["wl", "lg"], [pk])
                e_ = ev[ei % 4]
                ek = ("ev", ei % 4)
                ei += 1
                P.copy("act" if mo % 2 == 0 else "dve", e_[:, :nt], ps[:, :nt], [pk], [ek])
                P.dma("act", T["GT"][mo * 128:(mo + 1) * 128, tok0:tok0 + nt], e_[:, :nt], [ek], ["GT"])
            for d in range(2):
                ps, pk = P.next_ps()
                for kc in range(KC):
                    P.mm(ps[0:64, :nt], w1[:, kc, d, 0:64], xm[:, 1, kc, :nt], kc == 0, kc == KC - 1, ["wl", ("xm", xb, 1)], [pk])
                P.act(lo[0:64, d, :nt], ps[0:64, :nt], AF.Tanh, [pk], ["lo"])
                ps, pk = P.next_ps()
                for kc in range(KC):
                    P.mm(ps[0:64, :nt], w1[:, kc, d, 64:128], xm[:, 4, kc, :nt], kc == 0, kc == KC - 1, ["wl", ("xm", xb, 4)], [pk])
                P.copy("dve", la[0:64, d, :nt], ps[0:64, :nt], [pk], ["la"])
                for (wsb, src, skey, vofs, dst) in [(w2, lo, "lo", VO["w0"], "SIG%d" % d), (a2, la, "la", VO["a0"], "AA%d" % d)]:
                    for mo in range(KC):
                        ps, pk = P.next_ps()
                        P.mm(ps[:, :nt], wsb[:, d, mo * 128:(mo + 1) * 128], src[0:64, d, :nt], True, True, ["wl", skey], [pk])
                        e_ = ev[ei % 4]
                        ek = ("ev", ei % 4)
                        ei += 1
                        P.act(e_[:, :nt], ps[:, :nt], AF.Sigmoid, [pk], [ek], bias=G["vec"][:, vofs + d * 8 + mo:vofs + d * 8 + mo + 1], scale=1.0)
                        P.dma("act", T[dst][mo * 128:(mo + 1) * 128, tok0:tok0 + nt], e_[:, :nt], [ek], [dst])


F32R = mybir.dt.float32r
SCAN_MODE = "f32"
SCAN_BF16 = True
SCAN_GH = 8


def _mo(ap):
    return ap.bitcast(F32R) if SCAN_MODE == "f32r" else ap


def stage_rwkv_scan(P, G, T):
    CH = 128
    NH = 16
    GH = SCAN_GH
    with Stage(P) as S:
        def big(name):
            return S.sb(name, [64, NH, CH])
        _base = [big("%s0" % n) for n in ("r_t", "k_t", "v_t", "s_t", "a_t")]
        ld = [_base, [_base[0], _base[1], big("v_t1"), _base[3], _base[4]]]

        def ldkey(li, ti):
            return ("ld", li if ti == 2 else 0, ti)
        tA, tB, tC, tD, tE, tF, tG = big("tA"), big("tB"), big("tC"), big("tD"), big("tE"), big("tF"), big("tG")
        tH = big("tH")
        MD = BF16 if SCAN_BF16 else F32
        QR = S.sb("QR", [64, NH, 2 * CH], MD)
        Hst = S.sb("Hst", [64, NH, 64])
        if SCAN_BF16:
            HstM = S.sb("HstM", [64, NH, 64], MD)
            kinvM = S.sb("kinvM", [64, NH, CH], MD)
            binvM = S.sb("binvM", [64, NH, CH], MD)
        else:
            HstM = Hst
        kkv = S.sb("kkv", [64, 3, NH])
        P.dma("sp", kkv[:, 0:2, :], T["c_v64"], [], ["kkv"])
        P.ts("dve", kkv[:, 2, :], kkv[:, 1, :], -1.0, 1.0, ALU.mult, ALU.add, ["kkv"], ["kkv"])
        tot = S.sb("tot", [64, NH])
        ones_s = S.sb("ones_s", [64, CH])
        P.op("pool", lambda e: e.memset(ones_s[:], 1.0), [], ["ones_s"])
        mk = S.sb("mk", [128, 2, 2 * CH])
        mkT = S.sb("mkT", [128, 2, CH])
        P.dma("sp", mk[:], T["c_mask"], [], ["mk"])
        P.dma("sp", mkT[:], T["c_maskT"], [], ["mk"])

        def hb(name, shape):
            return [S.sb("%s%d" % (name, i), shape) for i in range(GH)]
        def hbm(name, shape):
            return [S.sb("%s%d" % (name, i), shape, MD) for i in range(GH)]
        vT, kdT, bdT = hbm("vT", [128, 64]), hbm("kdT", [128, 64]), hbm("bdT", [128, 64])
        Nn, ABb, AK = hb("Nn", [128, CH]), hbm("ABb", [128, CH]), hbm("AK", [128, 2 * CH])
        X = [hb("Xa", [128, CH]), hb("Xb", [128, CH])]
        XT_ = [hb("XTa", [128, CH]), hb("XTb", [128, CH])]
        Pm = [hb("Pa", [128, CH]), hb("Pb", [128, CH])]
        rhs_t, negU = hb("rhs", [128, 64]), hbm("negU", [128, 64])
        yt = hb("yt", [64, CH])
        cdec = DEC_C
        evi = [0]

        def ev_eng():
            evi[0] += 1
            return "act" if evi[0] % 2 == 0 else "dve"

        def bc(ap2):
            return ap2.unsqueeze(2).broadcast_to([64, NH, CH])

        def view(Tn):
            return T[Tn].rearrange("(h k) t -> k h t", k=64)

        seq = []
        for b in range(NB):
            for d in range(2):
                order = [0, 1] + list(range(2, NSC)) if d == 0 else [1, 0] + list(range(NSC - 1, 1, -1))
                for oi_, sc in enumerate(order):
                    seq.append((b, d, sc, oi_ == 0))

        def issue_loads(si):
            b, d, sc, first = seq[si]
            tok0 = b * SEG + sc * CH
            li = si % 2
            for ti, nm in enumerate(["RT", "KT", "VT", "SIG%d" % d, "AA%d" % d]):
                P.dma("sp", ld[li][ti][:], view(nm)[:, :, tok0:tok0 + CH], [nm], [ldkey(li, ti)])

        tI = big("tI")
        wcs = [S.sb("wc%d" % i, [64, NH]) for i in range(2)]
        kiK, biK = "kinvM", "binvM"
        assert SCAN_BF16

        def prep_early(si):
            b, d, sc, first = seq[si]
            li = si % 2
            r_t, k_t, v_t, s_t, a_t = ld[li]
            kr, kk_, kv, ks, ka = [ldkey(li, ti) for ti in range(5)]
            lastc = CH - 1 if d == 0 else 0
            P.tt("pool", tA[:], k_t[:], bc(kkv[:, 0, :]), ALU.mult, [kk_, "kkv"], ["tA"])
            yield
            P.act(tH[:], tA[:], AF.Square, ["tA"], ["tH"])
            yield
            for q4 in range(4):
                ps, pk = P.next_ps()
                P.mm(ps[0:64, :], G["ones_f"][0:64, 0:64], tH[:, q4 * 4:(q4 + 1) * 4, :], True, True, ["tH"], [pk])
                P.act(tH[:, q4 * 4:(q4 + 1) * 4, :], ps[0:64, :], AF.Ln, [pk], ["tH"], bias=G["tiny_t"][0:64, 0:1], scale=1.0)
                yield
            P.act(tH[:], tH[:], AF.Exp, ["tH"], ["tH"], scale=-0.5)
            yield
            P.tt("dve", tA[:], tA[:], tH[:], ALU.mult, ["tA", "tH"], ["tA"])
            P.tt("pool", tC[:], a_t[:], bc(kkv[:, 1, :]), ALU.mult, [ka, "kkv"], ["tC"])
            yield
            P.tt("pool", tC[:], tC[:], bc(kkv[:, 2, :]), ALU.add, ["tC", "kkv"], ["tC"])
            yield
            P.tt("pool", tC[:], tC[:], k_t[:], ALU.mult, ["tC", kk_], ["tC"])
            P.tt("dve", tD[:], tA[:], a_t[:], ALU.mult, ["tA", ka], ["tD"])
            yield
            for h in range(NH):
                P.op("dve", lambda e: e.tensor_tensor_scan(out=tI[:, h, :], data0=ones_s[:], data1=s_t[:, h, :], initial=0.0,
                                                           op0=ALU.mult, op1=ALU.add), ["ones_s", ks], ["tI"])
                if h % 4 == 3:
                    yield
            if d == 1:
                P.copy("pool", tot[:], tI[:, :, CH - 1], ["tI"], ["tot"])
                P.tt("pool", tF[:], s_t[:], tI[:], ALU.subtract, [ks, "tI"], ["tF"])
                yield
                P.tt("pool", tI[:], tF[:], bc(tot[:]), ALU.add, ["tF", "tot"], ["tI"])
                yield
            P.tt("pool", tF[:], tI[:], s_t[:], ALU.subtract, ["tI", ks], ["tF"])
            P.act(tG[:], tI[:], AF.Exp, ["tI"], ["tG"], scale=-cdec)
            yield
            P.act(tF[:], tF[:], AF.Exp, ["tF"], ["tF"], scale=-cdec)
            P.act(tI[:], tI[:], AF.Exp, ["tI"], ["tI"], scale=cdec)
            P.copy("pool", wcs[li][:], tG[:, :, lastc], ["tG"], [("wc", li)])
            yield
            P.tt("dve", tC[:], tC[:], tI[:], ALU.mult, ["tC", "tI"], ["tC"])
            P.tt("pool", tD[:], tD[:], tI[:], ALU.mult, ["tD", "tI"], ["tD"])
            yield

        def prep_late(si):
            b, d, sc, first = seq[si]
            li = si % 2
            r_t = ld[li][0]
            kr = ldkey(li, 0)
            if first:
                P.op("pool", lambda e: e.memset(Hst[:], 0.0), [], [("Hst", h_) for h_ in range(NH)])
                P.op("pool", lambda e: e.memset(HstM[:], 0.0), [], [("HstM", h_) for h_ in range(NH)])
            P.tt("dve", QR[:, :, 0:CH], tA[:], tF[:], ALU.mult, ["tA", "tF"], ["QR"])
            P.tt("pool", QR[:, :, CH:2 * CH], r_t[:], tG[:], ALU.mult, [kr, "tG"], ["QR"])
            P.copy("act", binvM[:], tD[:], ["tD"], ["binvM"])
            P.copy("act", kinvM[:], tC[:], ["tC"], ["kinvM"])
            P.tt("dve", tB[:], tC[:], bc(wcs[li][:]), ALU.mult, ["tC", ("wc", li)], ["tB"])
            P.tt("pool", tE[:], tD[:], bc(wcs[li][:]), ALU.mult, ["tD", ("wc", li)], ["tE"])

        def run_all(gens):
            alive = list(gens)
            while alive:
                nxt = []
                for g_ in alive:
                    try:
                        next(g_)
                        nxt.append(g_)
                    except StopIteration:
                        pass
                alive = nxt

        issue_loads(0)
        run_all([prep_early(0)])
        for si, (b, d, sc, first) in enumerate(seq):
            prep_late(si)
            if si + 1 < len(seq):
                issue_loads(si + 1)
                pe_gen = prep_early(si + 1)
            else:
                pe_gen = None
            li = si % 2
            v_t = ld[li][2]
            kv = ldkey(li, 2)
            tok0 = b * SEG + sc * CH
            lastc = CH - 1 if d == 0 else 0
            kdec, bdec = tB, tE
            wc_t, wck = wcs[li], ("wc", li)

            def head(h, i2, d=d, tok0=tok0, v_t=v_t, kv=kv, kdec=kdec, bdec=bdec, wc_t=wc_t, wck=wck):
                def K(n):
                    return (n, i2)
                for (src, skey, dst, dkey) in [(v_t, kv, vT, "vT"), (kdec, "tB", kdT, "kdT"), (bdec, "tE", bdT, "bdT")]:
                    ps, pk = P.next_ps()
                    P.tr(ps[:, 0:64], src[:, h, :], G["ident"][0:64, 0:64], [skey], [pk])
                    P.copy(ev_eng(), dst[i2][:], ps[:, 0:64], [pk], [K(dkey)])
                yield
                ps, pk = P.next_ps()
                P.mm(ps[:, 0:2 * CH], binvM[:, h, :], QR[:, h, :], True, True, [biK, "QR"], [pk])
                P.tt("dve", Nn[i2][:], ps[:, 0:CH], mk[:, d, 0:CH], ALU.mult, [pk, "mk"], [K("Nn")])
                P.tt("dve", ABb[i2][:], ps[:, CH:2 * CH], mk[:, d, CH:2 * CH], ALU.mult, [pk, "mk"], [K("ABb")])
                ps, pk = P.next_ps()
                P.mm(ps[:, 0:2 * CH], kinvM[:, h, :], QR[:, h, :], True, True, [kiK, "QR"], [pk])
                P.tt("dve", AK[i2][:], ps[:, 0:2 * CH], mk[:, d, :], ALU.mult, [pk, "mk"], [K("AK")])
                ps, pk = P.next_ps()
                P.mm(ps[:, 0:CH], QR[:, h, 0:CH], binvM[:, h, :], True, True, [biK, "QR"], [pk])
                P.tt("dve", XT_[0][i2][:], ps[:, 0:CH], mkT[:, d, :], ALU.mult, [pk, "mk"], [K("XT0")])
                yield
                P.tt("pool", Pm[0][i2][:], G["ident"][:], Nn[i2][:], ALU.subtract, [K("Nn")], [K("P0")])
                Xc, XTc = Nn[i2][:], XT_[0][i2][:]
                xck, xtck = K("Nn"), K("XT0")
                pc = 0
                for step in range(1, 7):
                    nx = step % 2
                    lastst = step == 6
                    ps, pk = P.next_ps()
                    P.mm(ps[:, 0:CH], Xc, XTc, True, True, [xck, xtck], [pk])
                    P.copy(ev_eng(), XT_[nx][i2][:], ps[:, 0:CH], [pk], [K("XT%d" % nx)])
                    yield
                    if not lastst:
                        ps, pk = P.next_ps()
                        P.tr(ps[:, 0:CH], XT_[nx][i2][:], G["ident"][:], [K("XT%d" % nx)], [pk])
                        P.copy(ev_eng(), X[nx][i2][:], ps[:, 0:CH], [pk], [K("X%d" % nx)])
                    ps, pk = P.next_ps()
                    P.mm(ps[:, 0:CH], XT_[nx][i2][:], Pm[pc][i2][:], True, True, [K("XT%d" % nx), K("P%d" % pc)], [pk])
                    P.tt("dve", Pm[1 - pc][i2][:], ps[:, 0:CH], Pm[pc][i2][:], ALU.add, [pk, K("P%d" % pc)], [K("P%d" % (1 - pc))])
                    pc = 1 - pc
                    Xc, XTc = X[nx][i2][:], XT_[nx][i2][:]
                    xck, xtck = K("X%d" % nx), K("XT%d" % nx)
                    yield
                Tt, tk = Pm[pc][i2], K("P%d" % pc)
                ps, pk = P.next_ps()
                P.mm(ps[:, 0:64], QR[:, h, 0:CH], HstM[:, h, :], True, False, ["QR", ("HstM", h)], [pk])
                P.mm(ps[:, 0:64], AK[i2][:, 0:CH], vT[i2][:], False, True, [K("AK"), K("vT")], [pk])
                P.copy(ev_eng(), rhs_t[i2][:], ps[:, 0:64], [pk], [K("rhs")])
                yield
                ps, pk = P.next_ps()
                P.mm(ps[:, 0:64], Tt[:], rhs_t[i2][:], True, True, [tk, K("rhs")], [pk])
                P.act(negU[i2][:], ps[:, 0:64], AF.Copy, [pk], [K("negU")], scale=-1.0)
                yield
                ps, pk = P.next_ps()
                P.mm(ps[0:64, 0:CH], HstM[:, h, :], QR[:, h, CH:2 * CH], True, False, ["QR", ("HstM", h)], [pk])
                P.mm(ps[0:64, 0:CH], vT[i2][:], AK[i2][:, CH:2 * CH], False, False, [K("AK"), K("vT")], [pk])
                P.mm(ps[0:64, 0:CH], negU[i2][:], ABb[i2][:], False, True, [K("ABb"), K("negU")], [pk])
                P.copy(ev_eng(), yt[i2][:], ps[0:64, 0:CH], [pk], [K("yt")])
                P.dma("sp", T["YT%d" % d][h * 64:(h + 1) * 64, tok0:tok0 + CH], yt[i2][:], [K("yt")], ["YT%d" % d])
                ps, pk = P.next_ps()
                P.mm(ps[0:64, 0:64], kdT[i2][:], vT[i2][:], True, False, [K("kdT"), K("vT")], [pk])
                P.mm(ps[0:64, 0:64], bdT[i2][:], negU[i2][:], False, True, [K("bdT"), K("negU")], [pk])
                P.stt(Hst[:, h, :], Hst[:, h, :], wc_t[:, h:h + 1], ps[0:64, 0:64], ALU.mult, ALU.add,
                      [("Hst", h), wck, pk], [("Hst", h)])
                if MD is not F32:
                    P.copy("pool", HstM[:, h, :], Hst[:, h, :], [("Hst", h)], [("HstM", h)])
                yield

            for h0 in range(0, NH, GH):
                gens = [head(h0 + i, i) for i in range(GH)]
                if pe_gen is not None:
                    gens.append(pe_gen)
                alive = list(gens)
                while alive:
                    nxt = []
                    for g_ in alive:
                        try:
                            next(g_)
                            if g_ is pe_gen and len(alive) == 1 and h0 + GH < NH:
                                nxt.append(g_)
                                break
                            nxt.append(g_)
                        except StopIteration:
                            if g_ is pe_gen:
                                pe_gen = None
                    if len(nxt) == 1 and nxt[0] is pe_gen and h0 + GH < NH:
                        break
                    alive = nxt
            if pe_gen is not None:
                run_all([pe_gen])


def stage_rwkv_finish(P, G, T, with_ctx=True):
    NI = 3
    with Stage(P) as S:
        def t5(name, dt=F32):
            return [S.sb("%s%d" % (name, i), [128, 512], dt) for i in range(NI)]
        y0, y1, rr, kk_, vv, gg, a0, a1 = t5("y0"), t5("y1"), t5("r"), t5("k"), t5("v"), t5("g"), t5("a0"), t5("a1")
        ysq, mean, var, tmp, bon = t5("ysq"), t5("mean"), t5("var"), t5("tmp"), t5("bon")
        ao = t5("ao", BF16)

        def body(i2, tok0, nt, c):
            def K(n):
                return (n, i2)
            sl = (slice(c * 128, (c + 1) * 128), slice(tok0, tok0 + nt))
            for (tl, nm) in [(y0, "YT0"), (y1, "YT1"), (rr, "RT"), (kk_, "KT"), (vv, "VT"), (gg, "GT"), (a0, "AA0"), (a1, "AA1")]:
                P.dma("sp", tl[i2][:, :nt], T[nm][sl[0], sl[1]], [nm], [K(nm)])
            yield
            y = y0[i2]
            P.tt("pool", y[:, :nt], y0[i2][:, :nt], y1[i2][:, :nt], ALU.add, [K("YT0"), K("YT1")], [K("YT0")])
            P.tt("pool", tmp[i2][:, :nt], a0[i2][:, :nt], a1[i2][:, :nt], ALU.add, [K("AA0"), K("AA1")], [K("tmp")])
            yield
            ps, pk = P.next_ps()
            P.mm(ps[:, :nt], G["bones"][:], y[:, :nt], True, True, [K("YT0")], [pk])
            P.act(mean[i2][:, :nt], ps[:, :nt], AF.Copy, [pk], [K("mean")], scale=1.0 / 64)
            P.ts("dve", tmp[i2][:, :nt], tmp[i2][:, :nt], -2.0, G["vec128_ka"][:, c:c + 1], ALU.add, ALU.mult, [K("tmp")], [K("tmp")])
            yield
            P.tt("dve", y[:, :nt], y[:, :nt], mean[i2][:, :nt], ALU.subtract, [K("YT0"), K("mean")], [K("YT0")])
            P.act(ysq[i2][:, :nt], y[:, :nt], AF.Square, [K("YT0")], [K("ysq")])
            P.stt(tmp[i2][:, :nt], tmp[i2][:, :nt], 2.0, kk_[i2][:, :nt], ALU.add, ALU.mult, [K("tmp"), K("KT")], [K("tmp")])
            yield
            ps, pk = P.next_ps()
            P.mm(ps[:, :nt], G["bones"][:], ysq[i2][:, :nt], True, True, [K("ysq")], [pk])
            P.act(var[i2][:, :nt], ps[:, :nt], AF.Ln, [pk], [K("var")], bias=G["gneps_t"][:, 0:1], scale=1.0 / 64)
            P.stt(tmp[i2][:, :nt], tmp[i2][:, :nt], G["vec"][:, VO["r_k"] + c:VO["r_k"] + c + 1], rr[i2][:, :nt], ALU.mult, ALU.mult,
                  [K("tmp"), K("RT")], [K("tmp")])
            yield
            P.act(var[i2][:, :nt], var[i2][:, :nt], AF.Exp, [K("var")], [K("var")], scale=-0.5)
            ps, pk = P.next_ps()
            P.mm(ps[:, :nt], G["bones"][:], tmp[i2][:, :nt], True, True, [K("tmp")], [pk])
            P.tt("dve", bon[i2][:, :nt], ps[:, :nt], vv[i2][:, :nt], ALU.mult, [pk, K("VT")], [K("bon")])
            yield
            P.tt("dve", y[:, :nt], y[:, :nt], var[i2][:, :nt], ALU.mult, [K("YT0"), K("var")], [K("YT0")])
            yield
            P.act(y[:, :nt], y[:, :nt], AF.Identity, [K("YT0")], [K("YT0")],
                  scale=G["vec"][:, VO["ln_g"] + c:VO["ln_g"] + c + 1], bias=G["vec"][:, VO["ln_b"] + c:VO["ln_b"] + c + 1])
            yield
            P.tt("pool", y[:, :nt], y[:, :nt], bon[i2][:, :nt], ALU.add, [K("YT0"), K("bon")], [K("YT0")])
            yield
            P.tt("pool", ao[i2][:, :nt], y[:, :nt], gg[i2][:, :nt], ALU.mult, [K("YT0"), K("GT")], [K("ao")])
            P.dma("act", T["AT"][sl[0], sl[1]], ao[i2][:, :nt], [K("ao")], ["AT"])
            yield

        its = [(tok0, nt, c) for (tok0, nt, b, is_ctx, pos0) in token_blocks(with_ctx=with_ctx) for c in range(KC)]
        for g0 in range(0, len(its), NI):
            alive = [body(i, *its[g0 + i]) for i in range(min(NI, len(its) - g0))]
            while alive:
                nxt = []
                for g_ in alive:
                    try:
                        next(g_)
                        nxt.append(g_)
                    except StopIteration:
                        pass
                alive = nxt


INPUT_NAMES = ["w_mod", "mlp_w_in", "mlp_w_out", "attn_w_qkv", "attn_w_o", "rwkv_w_rkv", "rwkv_w1", "rwkv_w2",
               "rwkv_a1", "rwkv_a2", "rwkv_g1", "rwkv_g2", "rwkv_w_o", "pool_w"]
INPUT_SHAPES = {"w_mod": [4, 1024, 6144], "mlp_w_in": [4, 1024, 4096], "mlp_w_out": [4, 4096, 1024],
                "attn_w_qkv": [2, 1024, 1536], "attn_w_o": [2, 1024, 1024], "rwkv_w_rkv": [1, 3, 1024, 1024],
                "rwkv_w1": [1, 2, 1024, 64], "rwkv_w2": [1, 2, 64, 1024], "rwkv_a1": [1, 2, 1024, 64],
                "rwkv_a2": [1, 2, 64, 1024], "rwkv_g1": [1, 1024, 160], "rwkv_g2": [1, 160, 1024],
                "rwkv_w_o": [1, 1024, 1024], "pool_w": [1, 4, 256, 256]}
CONST_SHAPES = {"cvT": [128, KC, 3], "c_vec": [128, NV], "c_ident": [128, 128], "c_ones": [128, 128], "c_bones": [128, 128],
                "c_rot": [128, 128], "c_cos": [128, LAT], "c_sin": [128, LAT], "c_mask": [128, 2, 256], "c_maskT": [128, 2, 128],
                "c_rcnt": [128, 4, LAT], "c_rcntc": [128, 4, CTXL], "c_v64": [64, 2, 16], "c_ka128": [128, KC],
                "c_small": [128, 4]}


def build_program(n_layers=DEPTH, dbg=None):
    nc = bass.Bass("TRN2", target_bir_lowering=False)
    T = {}
    T["x2"] = nc.dram_tensor("x2", [NB, LAT, D], F32, kind="ExternalInput").ap()
    T["ctx2"] = nc.dram_tensor("ctx2", [NB, CTXL, D], F32, kind="ExternalInput").ap()
    for n in INPUT_NAMES:
        T[n] = nc.dram_tensor(n, INPUT_SHAPES[n], F32, kind="ExternalInput").ap()
    for n, s in CONST_SHAPES.items():
        T[n] = nc.dram_tensor(n, s, F32, kind="ExternalInput").ap()
    T["y"] = nc.dram_tensor("y", [NB, LAT, D], F32, kind="ExternalOutput").ap()

    def scr(name, shape, dt=F32):
        kind = "ExternalOutput" if (dbg and name in dbg) else "Internal"
        T[name] = nc.dram_tensor(name, shape, dt, kind=kind).ap()

    scr("XT", [D, NTOK])
    scr("HT", [D, NTOK])
    scr("QT", [D, NTOK], BF16)
    scr("KTs", [256, NTOK], BF16)
    scr("Vs", [NB, 4, 128, NSC, 64], BF16)
    scr("AT", [D, NTOK], BF16)
    for n in ["RT", "KT", "VT", "GT", "SIG0", "SIG1", "AA0", "AA1", "YT0", "YT1"]:
        scr(n, [D, NTOK])

    with ExitStack() as es:
        P = Prog(nc, es)
        G = {}

        def gsb(name, shape, dt=F32):
            return es.enter_context(nc.sbuf_tensor(name, list(shape), dt))

        G["ident"] = gsb("ident", [128, 128])
        G["ones_f"] = gsb("ones_f", [128, 128])
        G["bones"] = gsb("bones", [128, 128])
        G["ones_b"] = gsb("ones_b", [128, 128], BF16)
        G["rot"] = gsb("rot", [128, 128])
        G["vec"] = gsb("vec", [128, NV])
        G["vec128_ka"] = gsb("ka128", [128, KC])
        small = gsb("small", [128, 4])
        G["eps_t"] = small[:, 0:1]
        G["tiny_t"] = small[:, 1:2]
        G["gneps_t"] = small[:, 2:3]
        G["sil"] = gsb("sil", [128, KC, 3])
        G["mod"] = [gsb("mod%d" % L, [128, 48, 3]) for L in range(DEPTH)]
        G["sc1"] = [gsb("sc1_%d" % L, [128, KC, 3]) for L in range(DEPTH)]
        G["sc2"] = [gsb("sc2_%d" % L, [128, KC, 3]) for L in range(DEPTH)]
        for (t, n) in [("ident", "c_ident"), ("ones_f", "c_ones"), ("bones", "c_bones"), ("rot", "c_rot"), ("vec", "c_vec"),
                       ("vec128_ka", "c_ka128")]:
            P.dma("sp", G[t][:], T[n], [], ["consts"])
        P.dma("sp", small[:], T["c_small"], [], ["consts"])
        P.dma("pool", G["ones_b"][:], T["c_ones"], [], ["consts"])
        P.barrier()

        stage_prologue(P, G, T)
        for L in range(n_layers):
            last = L == DEPTH - 1
            kind = L % 3
            j = L // 3
            wc = not last
            if kind == 0:
                stage_attn_qkv(P, G, T, L, j)
                stage_attn_core(P, G, T, wc, bg_layers=(list(range(1, n_layers)) if L == 0 else ()))
                stage_mix_mlp(P, G, T, L, T["attn_w_o"][j], "full", wc)
            elif kind == 1:
                stage_norm_to_HT(P, G, T, L, True)
                stage_rwkv_prep(P, G, T)
                stage_rwkv_scan(P, G, T)
                stage_rwkv_finish(P, G, T, wc)
                stage_mix_mlp(P, G, T, L, T["rwkv_w_o"][j], "full", wc)
            else:
                stage_norm_to_HT(P, G, T, L, wc)
                stage_pool(P, G, T, wc)
                stage_mix_mlp(P, G, T, L, T["pool_w"][j], "pool", wc)
        stage_epilogue(P, G, T)
        P.barrier()
        print("program: %d instructions, %d waits" % (P.n_inst, P.n_wait))
    return nc


def _pm(v, nch=KC):
    return np.ascontiguousarray(np.asarray(v, np.float32).reshape(nch, 128).T)


def host_consts(inp):
    c = {}
    vec = np.zeros((128, NV), np.float32)
    for L in range(DEPTH):
        vec[:, VO["bmod"] + L * 48:VO["bmod"] + (L + 1) * 48] = _pm(inp["b_mod"][L], 48)
        vec[:, VO["g1"] + L * 8:VO["g1"] + (L + 1) * 8] = _pm(inp["norm1_g"][L])
        vec[:, VO["g2"] + L * 8:VO["g2"] + (L + 1) * 8] = _pm(inp["norm2_g"][L])
    for j in range(2):
        vec[:, VO["qg"] + j] = np.tile(inp["attn_q_gain"][j], 2)
        vec[:, VO["kg"] + j] = np.tile(inp["attn_k_gain"][j], 2)
    vec[:, VO["pscale"]:VO["pscale"] + 8] = _pm(inp["pool_scale"][0])
    for m in range(6):
        vec[:, VO["mu"] + m * 8:VO["mu"] + (m + 1) * 8] = _pm(inp["rwkv_mu"][0, m])
    for d in range(2):
        vec[:, VO["w0"] + d * 8:VO["w0"] + (d + 1) * 8] = _pm(inp["rwkv_w0"][0, d])
        vec[:, VO["a0"] + d * 8:VO["a0"] + (d + 1) * 8] = _pm(inp["rwkv_a0"][0, d])
    vec[:, VO["r_k"]:VO["r_k"] + 8] = _pm(inp["rwkv_r_k"][0])
    vec[:, VO["ln_g"]:VO["ln_g"] + 8] = _pm(inp["rwkv_ln_g"][0])
    vec[:, VO["ln_b"]:VO["ln_b"] + 8] = _pm(inp["rwkv_ln_b"][0])
    c["c_vec"] = vec
    c["c_ka128"] = _pm(inp["rwkv_k_a"][0])
    v64 = np.zeros((64, 2, 16), np.float32)
    v64[:, 0, :] = np.asarray(inp["rwkv_k_k"][0], np.float32).reshape(16, 64).T
    v64[:, 1, :] = np.asarray(inp["rwkv_k_a"][0], np.float32).reshape(16, 64).T
    c["c_v64"] = v64
    c["c_ident"] = np.eye(128, dtype=np.float32)
    c["c_ones"] = np.ones((128, 128), np.float32)
    bo = np.zeros((128, 128), np.float32)
    bo[:64, :64] = 1.0
    bo[64:, 64:] = 1.0
    c["c_bones"] = bo
    rot = np.zeros((128, 128), np.float32)
    for hh in range(2):
        for jj in range(32):
            rot[hh * 64 + jj + 32, hh * 64 + jj] = -1.0
            rot[hh * 64 + jj, hh * 64 + jj + 32] = 1.0
    c["c_rot"] = rot
    nf = 16
    inv = (10000.0 ** (-np.arange(nf, dtype=np.float32) / nf)).astype(np.float32)
    t = np.arange(LAT)
    rows = (t // 64).astype(np.float32)
    cols = (t % 64).astype(np.float32)
    ang = np.concatenate([rows[:, None] * inv[None, :], cols[:, None] * inv[None, :]], axis=1).astype(np.float32)
    cosf = np.cos(ang).astype(np.float32).T
    sinf = np.sin(ang).astype(np.float32).T
    c["c_cos"] = np.ascontiguousarray(np.tile(cosf, (4, 1)))
    c["c_sin"] = np.ascontiguousarray(np.tile(sinf, (4, 1)))
    s = np.arange(128)[:, None]
    tt_ = np.arange(128)[None, :]
    mk = np.zeros((128, 2, 256), np.float32)
    mk[:, 0, 0:128] = (tt_ > s)
    mk[:, 0, 128:256] = (tt_ >= s)
    mk[:, 1, 0:128] = (tt_ < s)
    mk[:, 1, 128:256] = (tt_ <= s)
    c["c_mask"] = mk
    mkT = np.zeros((128, 2, 128), np.float32)
    mkT[:, 0, :] = (tt_ < s)
    mkT[:, 1, :] = (tt_ > s)
    c["c_maskT"] = mkT

    def rcnt(Tn):
        out = np.zeros((4, Tn), np.float32)
        tpos = np.arange(Tn)
        for gi, win in enumerate((2, 4, 8, 16)):
            lo = np.clip(tpos - win // 2, 0, Tn)
            hi = np.clip(tpos + win // 2, 0, Tn)
            out[gi] = 1.0 / (hi - lo).astype(np.float32)
        return out
    c["c_rcnt"] = np.ascontiguousarray(np.broadcast_to(rcnt(LAT)[None], (128, 4, LAT))).astype(np.float32)
    c["c_rcntc"] = np.ascontiguousarray(np.broadcast_to(rcnt(CTXL)[None], (128, 4, CTXL))).astype(np.float32)
    sm = np.zeros((128, 4), np.float32)
    sm[:, 0] = EPS
    sm[:, 1] = 1e-30
    sm[:, 2] = 64 * 1e-5
    c["c_small"] = sm
    return c


def make_in_maps(inp, cores):
    consts = host_consts(inp)
    shared = {n: np.ascontiguousarray(np.asarray(inp[n], np.float32)) for n in INPUT_NAMES}
    maps = []
    for i in cores:
        m = dict(shared)
        m.update(consts)
        m["x2"] = np.ascontiguousarray(inp["x"][NB * i:NB * (i + 1)], dtype=np.float32)
        m["ctx2"] = np.ascontiguousarray(inp["ctx"][NB * i:NB * (i + 1)], dtype=np.float32)
        cv = np.stack([inp["c"][NB * i], inp["c"][NB * i + 1], inp["c_ctx"]], axis=0).astype(np.float32)
        m["cvT"] = np.ascontiguousarray(cv.reshape(3, KC, 128).transpose(2, 1, 0))
        maps.append(m)
    return maps


_NC_CACHE = {}


def kernel(**inputs):
    inp = {k: np.asarray(v) for k, v in inputs.items()}
    if "nc" not in _NC_CACHE:
        _NC_CACHE["nc"] = build_program()
    nc = _NC_CACHE["nc"]
    in_maps = make_in_maps(inp, list(range(NCORES)))
    res = run_bass_kernel_spmd(nc, in_maps, core_ids=list(range(NCORES)))
    out = np.concatenate([np.asarray(r["y"]) for r in res.results], axis=0)
    return out.astype(np.float32)
```

```python
from contextlib import ExitStack
import numpy as np
import concourse.bass as bass
import concourse.mybir as mybir
from concourse.bass_utils import run_bass_kernel_spmd

F32 = mybir.dt.float32
BF16 = mybir.dt.bfloat16
AF = mybir.ActivationFunctionType
ALU = mybir.AluOpType
AX = mybir.AxisListType

NCORES = 8
NB = 2
LAT = 4096
CTXL = 256
SEG = LAT + CTXL
NTOK = NB * SEG
D = 1024
KC = 8
DFF = 4096
NSC = SEG // 128
EPS = 1e-6
DEPTH = 4
DEC_C = float(np.exp(-0.5))


class Prog:
    N_DMA_SLOTS = 8
    EPOCH = 30000

    def __init__(self, nc, es):
        self.nc = nc
        self.es = es
        self.eng = {"pe": nc.tensor, "act": nc.scalar, "dve": nc.vector, "pool": nc.gpsimd, "sp": nc.sync}
        self.sem = {}
        self.cnt = {}
        self.cur = {}
        self.epoch = {}
        for e in ["pe", "act", "dve", "pool"]:
            self.epoch[e] = 0
            self._new_epoch(e)
        self.dma_slots = {}
        for q in ["sp", "pool", "act"]:
            lst = []
            for i in range(self.N_DMA_SLOTS):
                name = "d_%s%d" % (q, i)
                self.sem[name] = es.enter_context(nc.semaphore(name))
                self.cnt[name] = 0
                lst.append(name)
            self.dma_slots[q] = lst
        self.dma_rr = {"sp": 0, "pool": 0, "act": 0}
        self.waited = {}
        self.last_w = {}
        self.readers = {}
        self.n_inst = 0
        self.n_wait = 0
        self.ps_tiles = [es.enter_context(nc.psum_tensor("psb%d" % i, [128, 512], F32)) for i in range(8)]
        self.ps_rr = 0

    def _new_epoch(self, e):
        name = "%s#%d" % (e, self.epoch[e])
        self.epoch[e] += 1
        self.sem[name] = self.es.enter_context(self.nc.semaphore("s_" + name.replace("#", "_")))
        self.cnt[name] = 0
        self.cur[e] = name

    def next_ps(self):
        i = self.ps_rr % 8
        self.ps_rr += 1
        return self.ps_tiles[i], ("ps", i)

    def _deps(self, me, reads, writes):
        deps = {}

        def add(t, s):
            if deps.get(t, 0) < s:
                deps[t] = s

        for r in reads:
            if r in self.last_w:
                add(*self.last_w[r])
        for w in writes:
            if w in self.last_w:
                add(*self.last_w[w])
            for t, s in self.readers.get(w, {}).items():
                if t.split("#")[0] == me:
                    continue
                add(t, s)
        return deps

    def _wait(self, me, deps):
        e = self.eng[me]
        for t, s in deps.items():
            if me == "pe" and t.split("#")[0] == "pe":
                continue
            if self.waited.get((me, t), 0) >= s:
                continue
            e.wait_ge(self.sem[t], s)
            self.waited[(me, t)] = s
            self.n_wait += 1

    def _commit(self, tok, reads, writes):
        t, s = tok
        for r in reads:
            d = self.readers.setdefault(r, {})
            if d.get(t, 0) < s:
                d[t] = s
        for w in writes:
            self.last_w[w] = tok
            self.readers[w] = {}

    def op(self, me, fn, reads=(), writes=()):
        self._wait(me, self._deps(me, reads, writes))
        inst = fn(self.eng[me])
        name = self.cur[me]
        self.cnt[name] += 1
        inst.then_inc(self.sem[name], 1)
        self._commit((name, self.cnt[name]), reads, writes)
        if self.cnt[name] >= self.EPOCH:
            self._new_epoch(me)
        self.n_inst += 1
        return inst

    def dma(self, q, out, in_, reads=(), writes=(), **kw):
        deps = self._deps("dma", reads, writes)
        slot = self.dma_slots[q][self.dma_rr[q] % self.N_DMA_SLOTS]
        self.dma_rr[q] += 1
        if self.cnt[slot] > 0 and deps.get(slot, 0) < self.cnt[slot]:
            deps[slot] = self.cnt[slot]
        if self.cnt[slot] >= 16 * 2100:
            raise RuntimeError("dma slot counter too large")
        self._wait(q, deps)
        inst = self.eng[q].dma_start(out=out, in_=in_, **kw)
        self.cnt[slot] += 16
        inst.then_inc(self.sem[slot], 16)
        self._commit((slot, self.cnt[slot]), reads, writes)
        self.n_inst += 1
        return inst

    def barrier(self):
        for me in ["sp", "pool", "act", "dve", "pe"]:
            deps = {t: c for t, c in self.cnt.items() if c > 0}
            if me == "pe":
                deps = {t: c for t, c in deps.items() if t.split("#")[0] != "pe"}
            self._wait(me, deps)
        self.last_w = {}
        self.readers = {}

    def mm(self, out, lhsT, rhs, start, stop, reads, writes):
        return self.op("pe", lambda e: e.matmul(out, lhsT=lhsT, rhs=rhs, start=start, stop=stop), reads, writes)

    def tr(self, out, in_, ident, reads, writes):
        return self.op("pe", lambda e: e.transpose(out=out, in_=in_, identity=ident), reads, writes)

    def act(self, out, in_, func, reads, writes, **kw):
        return self.op("act", lambda e: e.activation(out=out, in_=in_, func=func, **kw), reads, writes)

    def tt(self, eng, out, in0, in1, op, reads, writes):
        return self.op(eng, lambda e: e.tensor_tensor(out=out, in0=in0, in1=in1, op=op), reads, writes)

    def ts(self, eng, out, in0, s1, s2, op0, op1, reads, writes):
        if op1 is None:
            return self.op(eng, lambda e: e.tensor_scalar(out=out, in0=in0, scalar1=s1, scalar2=None, op0=op0),
                           reads, writes)
        return self.op(eng, lambda e: e.tensor_scalar(out=out, in0=in0, scalar1=s1, scalar2=s2, op0=op0, op1=op1),
                       reads, writes)

    def stt(self, out, in0, scalar, in1, op0, op1, reads, writes):
        return self.op("dve", lambda e: e.scalar_tensor_tensor(out=out, in0=in0, scalar=scalar, in1=in1,
                                                               op0=op0, op1=op1), reads, writes)

    def copy(self, eng, out, in_, reads, writes):
        if eng == "act":
            return self.act(out, in_, AF.Copy, reads, writes)
        return self.op(eng, lambda e: e.tensor_copy(out=out, in_=in_), reads, writes)


class Stage:
    _n = [0]

    def __init__(self, P):
        self.P = P
        self.es = ExitStack()
        Stage._n[0] += 1
        self.tag = "s%d_" % Stage._n[0]

    def __enter__(self):
        self.es.__enter__()
        return self

    def sb(self, name, shape, dt=F32):
        return self.es.enter_context(self.P.nc.sbuf_tensor(self.tag + name, list(shape), dt))

    def __exit__(self, *a):
        self.P.barrier()
        return self.es.__exit__(*a)


def token_blocks(with_ctx=True):
    out = []
    for b in range(NB):
        if with_ctx:
            out.append((b * SEG, CTXL, b, True, 0))
        for j in range(LAT // 512):
            out.append((b * SEG + CTXL + j * 512, 512, b, False, j * 512))
    return out


VO = {}
_o = 0
for _n, _w in [("bmod", 4 * 48), ("g1", 32), ("g2", 32), ("qg", 2), ("kg", 2), ("pscale", 8), ("mu", 48),
               ("w0", 16), ("a0", 16), ("r_k", 8), ("ln_g", 8), ("ln_b", 8)]:
    VO[_n] = _o
    _o += _w
NV = _o


def emit_norm(P, G, x_t, xkey, nt, L, which, n, h_t, hkey, tmp, sq, rstd, sqkeys=None):
    sc = G["sc1"] if which == 1 else G["sc2"]
    shm = 0 if which == 1 else 3
    sqk = sqkeys if sqkeys is not None else ["sq"] * KC
    xks = xkey if isinstance(xkey, list) else [xkey] * KC
    P.act(sq[:, 0:KC, :nt], x_t[:, :, :nt], AF.Square, list(set(xks)), list(set(sqk)))
    ps, pk = P.next_ps()
    for c in range(KC):
        P.mm(ps[:, :nt], G["ones_b"][:], sq[:, c, :nt], c == 0, c == KC - 1, [sqk[c]], [pk])
    if rstd is None:
        rs_t, rs_k = P.next_ps()
        P.act(rs_t[:, :nt], ps[:, :nt], AF.Ln, [pk], [rs_k], bias=G["eps_t"][:, 0:1], scale=1.0 / D)
        P.act(rs_t[:, :nt], rs_t[:, :nt], AF.Exp, [rs_k], [rs_k], scale=-0.5)
        for c in range(KC):
            tp, tk = P.next_ps()
            if tp is rs_t:
                tp, tk = P.next_ps()
            P.tt("dve", tp[:, :nt], x_t[:, c, :nt], rs_t[:, :nt], ALU.mult, [xks[c], rs_k], [tk])
            P.act(h_t[:, c, :nt], tp[:, :nt], AF.Identity, [tk], [hkey],
                  scale=sc[L][:, c, n:n + 1], bias=G["mod"][L][:, shm * 8 + c, n:n + 1])
        return
    P.act(rstd[:, :nt], ps[:, :nt], AF.Ln, [pk], ["rstd"], bias=G["eps_t"][:, 0:1], scale=1.0 / D)
    P.act(rstd[:, :nt], rstd[:, :nt], AF.Exp, ["rstd"], ["rstd"], scale=-0.5)
    for c in range(KC):
        tk = ("ntmp", c % 2)
        P.tt("dve" if c % 2 == 0 else "pool", tmp[c % 2][:, :nt], x_t[:, c, :nt], rstd[:, :nt], ALU.mult,
             [xks[c], "rstd"], [tk])
        P.act(h_t[:, c, :nt], tmp[c % 2][:, :nt], AF.Identity, [tk], [hkey],
              scale=sc[L][:, c, n:n + 1], bias=G["mod"][L][:, shm * 8 + c, n:n + 1])


def load_w(P, w_sb, w_dram, kcs, key, q="pool"):
    wv = w_dram.rearrange("(kc p) o -> p kc o", p=128)
    for kc in range(kcs):
        P.dma(q, w_sb[:, kc, :], wv[:, kc, :], [], [key])


def mod_steps(P, G, T, S, layers):
    wt = [S.sb("wmod%d" % i, [128, KC, 512]) for i in range(2)]
    sil = G["sil"]
    items = [(L, blk) for L in layers for blk in range(12)]

    def load(i):
        L, blk = items[i]
        wv = T["w_mod"][L].rearrange("(kc p) o -> p kc o", p=128)
        P.dma("sp", wt[i % 2][:], wv[:, :, blk * 512:(blk + 1) * 512], [], [("wmod", i % 2)])

    if items:
        load(0)
        yield
    for i, (L, blk) in enumerate(items):
        if i + 1 < len(items):
            load(i + 1)
        w = wt[i % 2]
        wk = ("wmod", i % 2)
        for mo in range(4):
            j = blk * 4 + mo
            ps, pk = P.next_ps()
            for kc in range(KC):
                P.mm(ps[:, 0:3], w[:, kc, mo * 128:(mo + 1) * 128], sil[:, kc, :], kc == 0, kc == KC - 1,
                     [wk, "sil"], [pk])
            P.ts("dve", G["mod"][L][:, j, :], ps[:, 0:3], G["vec"][:, VO["bmod"] + L * 48 + j:VO["bmod"] + L * 48 + j + 1],
                 None, ALU.add, None, [pk], [("mod", L)])
        if blk == 11:
            for n in range(3):
                P.stt(G["sc1"][L][:, :, n], G["mod"][L][:, 8:16, n], 1.0, G["vec"][:, VO["g1"] + L * 8:VO["g1"] + L * 8 + 8],
                      ALU.add, ALU.mult, [("mod", L)], [("mod", L)])
                P.stt(G["sc2"][L][:, :, n], G["mod"][L][:, 32:40, n], 1.0, G["vec"][:, VO["g2"] + L * 8:VO["g2"] + L * 8 + 8],
                      ALU.add, ALU.mult, [("mod", L)], [("mod", L)])
        yield


def stage_prologue(P, G, T):
    nc = P.nc
    with Stage(P) as S:
        cv = S.sb("cv", [128, KC, 3])
        P.dma("sp", cv[:], T["cvT"], [], ["cv"])
        P.act(G["sil"][:], cv[:], AF.Silu, ["cv"], ["sil"])
        for _ in mod_steps(P, G, T, S, [0]):
            pass
    with Stage(P) as S:
        xtm = [S.sb("xtm%d" % i, [128, 4, D]) for i in range(2)]
        xT = [S.sb("xT%d" % i, [128, KC, 512]) for i in range(2)]
        XTv = T["XT"].rearrange("(c p) t -> p c t", p=128)
        for bi, (tok0, nt, b, is_ctx, pos0) in enumerate(token_blocks()):
            xm = xtm[bi % 2]
            xk = ("xtm", bi % 2)
            src = T["ctx2"][b] if is_ctx else T["x2"][b, pos0:pos0 + nt]
            nts = nt // 128
            P.dma("sp", xm[:, 0:nts, :], src.rearrange("(ts p) d -> p ts d", p=128), [], [xk])
            xo = xT[bi % 2]
            ok = ("xT", bi % 2)
            for c in range(KC):
                ps, pk = P.next_ps()
                for ts_ in range(nts):
                    P.tr(ps[:, ts_ * 128:(ts_ + 1) * 128], xm[:, ts_, c * 128:(c + 1) * 128], G["ident"][:], [xk], [pk])
                P.copy("act" if c % 2 == 0 else "dve", xo[:, c, :nt], ps[:, :nt], [pk], [ok])
            P.dma("act", XTv[:, :, tok0:tok0 + nt], xo[:, :, :nt], [ok], ["XT"])


def stage_epilogue(P, G, T):
    with Stage(P) as S:
        xT = [S.sb("xT%d" % i, [128, KC, 512]) for i in range(2)]
        yo = [S.sb("yo%d" % i, [128, 4, D]) for i in range(2)]
        XTv = T["XT"].rearrange("(c p) t -> p c t", p=128)
        for bi, (tok0, nt, b, is_ctx, pos0) in enumerate(token_blocks(with_ctx=False)):
            xi = xT[bi % 2]
            xk = ("xT", bi % 2)
            P.dma("sp", xi[:], XTv[:, :, tok0:tok0 + nt], ["XT"], [xk])
            y = yo[bi % 2]
            yk = ("yo", bi % 2)
            for ts_ in range(4):
                for half in range(2):
                    ps, pk = P.next_ps()
                    for cc in range(4):
                        c = half * 4 + cc
                        P.tr(ps[:, cc * 128:(cc + 1) * 128], xi[:, c, ts_ * 128:(ts_ + 1) * 128], G["ident"][:], [xk], [pk])
                    P.copy("act" if half == 0 else "dve", y[:, ts_, half * 512:(half + 1) * 512], ps[:, :], [pk], [yk])
            P.dma("act", T["y"][b, pos0:pos0 + nt].rearrange("(ts p) d -> p ts d", p=128), y[:], [yk], ["yout"])


def stage_attn_qkv(P, G, T, L, j):
    with Stage(P) as S:
        w = S.sb("wqkv", [128, KC, 1536], BF16)
        load_w(P, w, T["attn_w_qkv"][j], KC, "wqkv")
        cosT = S.sb("cosT", [128, LAT])
        sinT = S.sb("sinT", [128, LAT])
        P.dma("sp", cosT[:], T["c_cos"], [], ["cos"])
        P.dma("sp", sinT[:], T["c_sin"], [], ["sin"])
        xs = [S.sb("x%d" % i, [128, KC, 512]) for i in range(2)]
        h = S.sb("h", [128, KC, 512], BF16)
        sq = S.sb("sq", [128, KC, 512], BF16)
        rstd = S.sb("rstd", [128, 512])
        tmp = [S.sb("ntmp%d" % i, [128, 512]) for i in range(2)]
        qs = [S.sb("qs%d" % i, [128, 512]) for i in range(2)]
        q2 = [S.sb("q2%d" % i, [128, 512]) for i in range(2)]
        rs = [S.sb("rs%d" % i, [128, 512]) for i in range(2)]
        qn = [S.sb("qn%d" % i, [128, 512]) for i in range(2)]
        t1 = [S.sb("t1%d" % i, [128, 512]) for i in range(2)]
        t2 = [S.sb("t2%d" % i, [128, 512]) for i in range(2)]
        qo = [S.sb("qo%d" % i, [128, 512], BF16) for i in range(4)]
        vo = [S.sb("vo%d" % i, [128, 256], BF16) for i in range(2)]
        XTv = T["XT"].rearrange("(c p) t -> p c t", p=128)
        blks = token_blocks()
        P.dma("sp", xs[0][:, :, :blks[0][1]], XTv[:, :, blks[0][0]:blks[0][0] + blks[0][1]], ["XT"], [("x", 0)])
        it = 0
        vi = 0
        for bi, (tok0, nt, b, is_ctx, pos0) in enumerate(blks):
            if bi + 1 < len(blks):
                t0n, ntn = blks[bi + 1][0], blks[bi + 1][1]
                P.dma("sp", xs[(bi + 1) % 2][:, :, :ntn], XTv[:, :, t0n:t0n + ntn], ["XT"], [("x", (bi + 1) % 2)])
            n = 2 if is_ctx else b
            emit_norm(P, G, xs[bi % 2], ("x", bi % 2), nt, L, 1, n, h, "h", tmp, sq, rstd)
            def qk_chain(mo, i2, i3):
                isq = mo < 8
                ps, pk = P.next_ps()
                for kc in range(KC):
                    P.mm(ps[:, :nt], w[:, kc, mo * 128:(mo + 1) * 128], h[:, kc, :nt], kc == 0, kc == KC - 1, ["wqkv", "h"], [pk])
                P.copy("act", qs[i2][:, :nt], ps[:, :nt], [pk], [("qs", i2)])
                yield
                P.tt("dve", q2[i2][:, :nt], ps[:, :nt], qs[i2][:, :nt], ALU.mult, [pk, ("qs", i2)], [("q2", i2)])
                yield
                ps2, pk2 = P.next_ps()
                P.mm(ps2[:, :nt], G["bones"][:], q2[i2][:, :nt], True, True, [("q2", i2)], [pk2])
                P.act(rs[i2][:, :nt], ps2[:, :nt], AF.Ln, [pk2], [("rs", i2)], bias=G["eps_t"][:, 0:1], scale=1.0 / 64)
                yield
                P.act(rs[i2][:, :nt], rs[i2][:, :nt], AF.Exp, [("rs", i2)], [("rs", i2)], scale=-0.5)
                yield
                gcol = (VO["qg"] if isq else VO["kg"]) + j
                oq = qo[i3]
                okk = ("qo", i3)
                if is_ctx:
                    P.stt(oq[:, :nt], qs[i2][:, :nt], G["vec"][:, gcol:gcol + 1], rs[i2][:, :nt], ALU.mult, ALU.mult,
                          [("qs", i2), ("rs", i2)], [okk])
                else:
                    P.stt(qn[i2][:, :nt], qs[i2][:, :nt], G["vec"][:, gcol:gcol + 1], rs[i2][:, :nt], ALU.mult, ALU.mult,
                          [("qs", i2), ("rs", i2)], [("qn", i2)])
                    yield
                    ps3, pk3 = P.next_ps()
                    P.mm(ps3[:, :nt], G["rot"][:], qn[i2][:, :nt], True, True, [("qn", i2)], [pk3])
                    P.tt("pool", t1[i2][:, :nt], qn[i2][:, :nt], cosT[:, pos0:pos0 + nt], ALU.mult, [("qn", i2), "cos"], [("t1", i2)])
                    yield
                    P.tt("dve", t2[i2][:, :nt], ps3[:, :nt], sinT[:, pos0:pos0 + nt], ALU.mult, [pk3, "sin"], [("t2", i2)])
                    yield
                    P.tt("pool", oq[:, :nt], t1[i2][:, :nt], t2[i2][:, :nt], ALU.add, [("t1", i2), ("t2", i2)], [okk])
                yield
                if isq:
                    P.dma("sp", T["QT"][mo * 128:(mo + 1) * 128, tok0:tok0 + nt], oq[:, :nt], [okk], ["QT"])
                else:
                    P.dma("sp", T["KTs"][(mo - 8) * 128:(mo - 7) * 128, tok0:tok0 + nt], oq[:, :nt], [okk], ["KTs"])
                yield

            for mo0 in range(0, 10, 2):
                alive = [qk_chain(mo0 + i, i, (it + i) % 4) for i in range(2)]
                it += 2
                while alive:
                    nxt = []
                    for g_ in alive:
                        try:
                            next(g_)
                            nxt.append(g_)
                        except StopIteration:
                            pass
                    alive = nxt
            for ts_ in range(nt // 128):
                ps, pk = P.next_ps()
                for kc in range(KC):
                    P.mm(ps[:, 0:256], h[:, kc, ts_ * 128:(ts_ + 1) * 128], w[:, kc, 1280:1536], kc == 0, kc == KC - 1,
                         ["wqkv", "h"], [pk])
                v = vo[vi % 2]
                vk = ("vo", vi % 2)
                vi += 1
                P.copy("act", v[:, :], ps[:, 0:256], [pk], [vk])
                sc = (tok0 - b * SEG) // 128 + ts_
                P.dma("sp", T["Vs"][b, :, :, sc, :].rearrange("g s d -> s g d"), v[:, :].rearrange("s (g d) -> s g d", g=4),
                      [vk], ["Vs"])


def stage_attn_core(P, G, T, with_ctx_out, bg_layers=()):
    LOOK = 3
    with Stage(P) as S:
        kTs = [[S.sb("kT%d_%d" % (i, hh), [128, SEG], BF16) for hh in range(2)] for i in range(2)]
        for i in range(2):
            P.op("pool", lambda e: e.memset(kTs[i][0][64:128, :], 0.0), [], [("kT", i)])
            P.op("pool", lambda e: e.memset(kTs[i][1][0:64, :], 0.0), [], [("kT", i)])
        Vall = S.sb("Vall", [128, NSC, 4, 128], BF16)
        qt = [S.sb("qt%d" % i, [128, 512], BF16) for i in range(3)]
        pt = [S.sb("pt%d" % i, [128, 512], BF16) for i in range(6)]
        rl = [S.sb("rl%d" % i, [128, 512]) for i in range(2)]
        on = [S.sb("on%d" % i, [64, 512], BF16) for i in range(2)]
        P.op("pool", lambda e: e.memset(Vall[:, :, :, 64:128], 1.0), [], ["Vones"])
        pob = [[(P.ps_tiles[0], ("ps", 0)), (P.ps_tiles[1], ("ps", 1))], [(P.ps_tiles[2], ("ps", 2)), (P.ps_tiles[3], ("ps", 3))]]
        sb_ = [(P.ps_tiles[i], ("ps", i)) for i in range(4, 8)]
        work = []
        for b in range(NB):
            for g in range(4):
                for pair in range(2):
                    for (tok0, nt, bb, is_ctx, pos0) in token_blocks(with_ctx=with_ctx_out):
                        if bb == b:
                            work.append((b, g, pair, tok0, nt, is_ctx))
        kcur = {}
        cnt = {"s": 0, "p": 0, "o": 0, "k": 0}

        def load_q(wi):
            b, g, pair, tok0, nt, is_ctx = work[wi]
            qc = 2 * g + pair
            P.dma("sp", qt[wi % 3][:, :nt], T["QT"][qc * 128:(qc + 1) * 128, tok0:tok0 + nt], ["QT"], [("qt", wi % 3)])

        def load_kv(wi):
            b, g, pair, tok0, nt, is_ctx = work[wi]
            if (b, g) not in kcur:
                ki = cnt["k"] % 2
                cnt["k"] += 1
                kcur[(b, g)] = ki
                for hf in range(2):
                    P.dma("sp", kTs[ki][hf][hf * 64:(hf + 1) * 64, :], T["KTs"][g * 64:(g + 1) * 64, b * SEG:(b + 1) * SEG], ["KTs"], [("kT", ki)])

        bg = mod_steps(P, G, T, S, list(bg_layers)) if bg_layers else None
        load_kv(0)
        load_q(0)
        for wi, (b, g, pair, tok0, nt, is_ctx) in enumerate(work):
            if bg is not None and wi % 3 == 2:
                if next(bg, "done") == "done":
                    bg = None
            if ("v", b) not in kcur:
                kcur[("v", b)] = 1
                for g2 in range(4):
                    P.dma("sp", Vall[:, :, g2, 0:64], T["Vs"][b, g2], ["Vs"], ["Vall"])
            if wi + 1 < len(work):
                load_kv(wi + 1)
                load_q(wi + 1)
            ki = kcur[(b, g)]
            kT = kTs[ki]
            q = qt[wi % 3]
            qk = ("qt", wi % 3)
            qc = 2 * g + pair
            nsc = (CTXL // 128) if is_ctx else NSC
            po = pob[wi % 2]
            items = [(sc, hh) for sc in range(nsc) for hh in range(2)]
            sbank = {}

            def emit_S(i):
                sc, hh = items[i]
                ps, pk = sb_[cnt["s"] % 4]
                cnt["s"] += 1
                P.mm(ps[:, :nt], kT[hh][:, sc * 128:(sc + 1) * 128], q[:, :nt], True, True, [("kT", ki), qk], [pk])
                sbank[i] = (ps, pk)

            for i in range(min(LOOK, len(items))):
                emit_S(i)
            for i, (sc, hh) in enumerate(items):
                if i + LOOK < len(items):
                    emit_S(i + LOOK)
                ps, pk = sbank.pop(i)
                p_ = pt[cnt["p"] % 6]
                ptk = ("pt", cnt["p"] % 6)
                cnt["p"] += 1
                P.act(p_[:, :nt], ps[:, :nt], AF.Exp, [pk], [ptk], scale=0.125)
                P.mm(po[hh][0][:, :nt], Vall[:, sc, g, :], p_[:, :nt], sc == 0, sc == nsc - 1,
                     ["Vall", "Vones", ptk], [po[hh][1]])
            for hh in range(2):
                oi = cnt["o"] % 2
                cnt["o"] += 1
                r_ = rl[oi]
                o_ = on[oi]
                rk, ok = ("rl", oi), ("on", oi)
                P.op("dve", lambda e: e.reciprocal(out=r_[64:128, :nt], in_=po[hh][0][64:128, :nt]), [po[hh][1]], [rk])
                P.tt("dve", o_[:, :nt], po[hh][0][0:64, :nt], r_[64:128, :nt], ALU.mult, [po[hh][1], rk], [ok])
                hq = qc * 2 + hh
                P.dma("sp", T["AT"][hq * 64:(hq + 1) * 64, tok0:tok0 + nt], o_[:, :nt], [ok], ["AT"])
        if bg is not None:
            for _ in bg:
                pass


def stage_mixout(P, G, T, L, w_dram, kind, with_ctx):
    with Stage(P) as S:
        if kind == "full":
            w = S.sb("wo", [128, KC, D], BF16)
            load_w(P, w, w_dram, KC, "wo")
        else:
            w = S.sb("wo", [128, 8, 256], BF16)
            for g in range(4):
                for k2 in range(2):
                    P.dma("pool", w[:, g * 2 + k2, :], w_dram[g, k2 * 128:(k2 + 1) * 128, :], [], ["wo"])
            gs = S.sb("gs", [128, KC, 3])
            for n in range(3):
                P.tt("dve", gs[:, :, n], G["mod"][L][:, 16:24, n], G["vec"][:, VO["pscale"]:VO["pscale"] + 8], ALU.mult, ["mod"], ["gs"])
        xs = [S.sb("x%d" % i, [128, KC, 512]) for i in range(2)]
        at = [S.sb("a%d" % i, [128, KC, 512], BF16) for i in range(2)]
        XTv = T["XT"].rearrange("(c p) t -> p c t", p=128)
        ATv = T["AT"].rearrange("(c p) t -> p c t", p=128)
        for bi, (tok0, nt, b, is_ctx, pos0) in enumerate(token_blocks(with_ctx=with_ctx)):
            x = xs[bi % 2]
            a = at[bi % 2]
            xk, ak = ("x", bi % 2), ("a", bi % 2)
            P.dma("sp", x[:, :, :nt], XTv[:, :, tok0:tok0 + nt], ["XT"], [xk])
            P.dma("sp", a[:, :, :nt], ATv[:, :, tok0:tok0 + nt], ["AT"], [ak])
            n = 2 if is_ctx else b
            for mo in range(KC):
                ps, pk = P.next_ps()
                if kind == "full":
                    for kc in range(KC):
                        P.mm(ps[:, :nt], w[:, kc, mo * 128:(mo + 1) * 128], a[:, kc, :nt], kc == 0, kc == KC - 1, ["wo", ak], [pk])
                    gate = G["mod"][L][:, 16 + mo, n:n + 1]
                    gk = "mod"
                else:
                    g = mo // 2
                    for k2 in range(2):
                        P.mm(ps[:, :nt], w[:, g * 2 + k2, (mo % 2) * 128:(mo % 2 + 1) * 128], a[:, g * 2 + k2, :nt], k2 == 0, k2 == 1,
                             ["wo", ak], [pk])
                    gate = gs[:, mo, n:n + 1]
                    gk = "gs"
                P.stt(x[:, mo, :nt], ps[:, :nt], gate, x[:, mo, :nt], ALU.mult, ALU.add, [pk, xk, gk], [xk])
            P.dma("sp", XTv[:, :, tok0:tok0 + nt], x[:, :, :nt], [xk], ["XT"])


def stage_mix_mlp(P, G, T, L, w_dram, kind, with_ctx):
    with Stage(P) as S:
        if kind == "full":
            w = S.sb("wo", [128, KC, D], BF16)
            load_w(P, w, w_dram, KC, "wo")
        else:
            w = S.sb("wo", [128, 8, 256], BF16)
            for g in range(4):
                for k2 in range(2):
                    P.dma("pool", w[:, g * 2 + k2, :], w_dram[g, k2 * 128:(k2 + 1) * 128, :], [], ["wo"])
            gs = S.sb("gs", [128, KC, 3])
            for n in range(3):
                P.tt("dve", gs[:, :, n], G["mod"][L][:, 16:24, n], G["vec"][:, VO["pscale"]:VO["pscale"] + 8], ALU.mult, ["mod"], ["gs"])
        w1 = S.sb("w1", [128, KC, DFF], BF16)
        w2 = S.sb("w2", [128, 32, D], BF16)
        load_w(P, w1, T["mlp_w_in"][L], KC, "w1")
        load_w(P, w2, T["mlp_w_out"][L], 32, "w2")
        x = S.sb("x", [128, KC, 512])
        h = S.sb("h", [128, KC, 512], BF16)
        a = h
        u = S.sb("u", [128, 32, 512], BF16)
        XTv = T["XT"].rearrange("(c p) t -> p c t", p=128)
        ATv = T["AT"].rearrange("(c p) t -> p c t", p=128)
        ri = 0
        for bi, (tok0, nt, b, is_ctx, pos0) in enumerate(token_blocks(with_ctx=with_ctx)):
            for c in range(KC):
                P.dma("sp", x[:, c, :nt], XTv[:, c, tok0:tok0 + nt], [("XT", c)], [("x", c)])
            P.dma("sp", a[:, :, :nt], ATv[:, :, tok0:tok0 + nt], ["AT"], ["h"])
            n = 2 if is_ctx else b
            for mo in range(KC):
                ps, pk = P.next_ps()
                if kind == "full":
                    for kc in range(KC):
                        P.mm(ps[:, :nt], w[:, kc, mo * 128:(mo + 1) * 128], a[:, kc, :nt], kc == 0, kc == KC - 1, ["wo", "h"], [pk])
                    gate = G["mod"][L][:, 16 + mo, n:n + 1]
                    gk = "mod"
                else:
                    g = mo // 2
                    for k2 in range(2):
                        P.mm(ps[:, :nt], w[:, g * 2 + k2, (mo % 2) * 128:(mo % 2 + 1) * 128], a[:, g * 2 + k2, :nt], k2 == 0, k2 == 1,
                             ["wo", "h"], [pk])
                    gate = gs[:, mo, n:n + 1]
                    gk = "gs"
                P.stt(x[:, mo, :nt], ps[:, :nt], gate, x[:, mo, :nt], ALU.mult, ALU.add, [pk, ("x", mo), gk], [("x", mo)])
            emit_norm(P, G, x, [("x", c) for c in range(KC)], nt, L, 2, n, h, "h", None, u, None, sqkeys=[("u", c) for c in range(KC)])
            for mo in range(32):
                ps, pk = P.next_ps()
                for kc in range(KC):
                    P.mm(ps[:, :nt], w1[:, kc, mo * 128:(mo + 1) * 128], h[:, kc, :nt], kc == 0, kc == KC - 1, ["w1", "h"], [pk])
                P.act(ps[:, :nt], ps[:, :nt], AF.Relu, [pk], [pk])
                P.act(u[:, mo, :nt], ps[:, :nt], AF.Square, [pk], [("u", mo)])
            for mo in range(KC):
                ps, pk = P.next_ps()
                for kc in range(32):
                    P.mm(ps[:, :nt], w2[:, kc, mo * 128:(mo + 1) * 128], u[:, kc, :nt], kc == 0, kc == 31, ["w2", ("u", kc)], [pk])
                P.stt(x[:, mo, :nt], ps[:, :nt], G["mod"][L][:, 40 + mo, n:n + 1], x[:, mo, :nt], ALU.mult, ALU.add,
                      [pk, ("x", mo), ("mod", L)], [("x", mo)])
                P.dma("act", XTv[:, mo, tok0:tok0 + nt], x[:, mo, :nt], [("x", mo)], [("XT", mo)])


def stage_mlp(P, G, T, L, with_ctx):
    with Stage(P) as S:
        w1 = S.sb("w1", [128, KC, DFF], BF16)
        w2 = S.sb("w2", [128, 32, D], BF16)
        load_w(P, w1, T["mlp_w_in"][L], KC, "w1")
        load_w(P, w2, T["mlp_w_out"][L], 32, "w2")
        x = S.sb("x", [128, KC, 512])
        h = S.sb("h", [128, KC, 512], BF16)
        u = S.sb("u", [128, 32, 512], BF16)
        sq = S.sb("sq", [128, KC, 512], BF16)
        rstd = S.sb("rstd", [128, 512])
        tmp = [S.sb("ntmp%d" % i, [128, 512]) for i in range(2)]
        rr = tmp
        XTv = T["XT"].rearrange("(c p) t -> p c t", p=128)
        ri = 0
        for bi, (tok0, nt, b, is_ctx, pos0) in enumerate(token_blocks(with_ctx=with_ctx)):
            P.dma("sp", x[:, :, :nt], XTv[:, :, tok0:tok0 + nt], ["XT"], ["x"])
            n = 2 if is_ctx else b
            emit_norm(P, G, x, "x", nt, L, 2, n, h, "h", tmp, sq, rstd)
            for mo in range(32):
                ps, pk = P.next_ps()
                for kc in range(KC):
                    P.mm(ps[:, :nt], w1[:, kc, mo * 128:(mo + 1) * 128], h[:, kc, :nt], kc == 0, kc == KC - 1, ["w1", "h"], [pk])
                r_ = rr[ri % 2]
                rk = ("ntmp", ri % 2)
                ri += 1
                P.act(r_[:, :nt], ps[:, :nt], AF.Relu, [pk], [rk])
                P.tt("pool" if mo % 4 != 3 else "dve", u[:, mo, :nt], r_[:, :nt], r_[:, :nt], ALU.mult, [rk], [("u", mo)])
            for mo in range(KC):
                ps, pk = P.next_ps()
                for kc in range(32):
                    P.mm(ps[:, :nt], w2[:, kc, mo * 128:(mo + 1) * 128], u[:, kc, :nt], kc == 0, kc == 31, ["w2", ("u", kc)], [pk])
                P.stt(x[:, mo, :nt], ps[:, :nt], G["mod"][L][:, 40 + mo, n:n + 1], x[:, mo, :nt], ALU.mult, ALU.add,
                      [pk, "x", ("mod", L)], ["x"])
            P.dma("sp", XTv[:, :, tok0:tok0 + nt], x[:, :, :nt], ["x"], ["XT"])


def stage_norm_to_HT(P, G, T, L, with_ctx):
    with Stage(P) as S:
        xs = [S.sb("x%d" % i, [128, KC, 512]) for i in range(2)]
        hs = [S.sb("h%d" % i, [128, KC, 512]) for i in range(2)]
        sq = S.sb("sq", [128, KC, 512], BF16)
        rstd = S.sb("rstd", [128, 512])
        tmp = [S.sb("ntmp%d" % i, [128, 512]) for i in range(2)]
        XTv = T["XT"].rearrange("(c p) t -> p c t", p=128)
        HTv = T["HT"].rearrange("(c p) t -> p c t", p=128)
        for bi, (tok0, nt, b, is_ctx, pos0) in enumerate(token_blocks(with_ctx=with_ctx)):
            x, h = xs[bi % 2], hs[bi % 2]
            xk, hk = ("x", bi % 2), ("h", bi % 2)
            P.dma("sp", x[:, :, :nt], XTv[:, :, tok0:tok0 + nt], ["XT"], [xk])
            emit_norm(P, G, x, xk, nt, L, 1, 2 if is_ctx else b, h, hk, tmp, sq, rstd)
            P.dma("act", HTv[:, :, tok0:tok0 + nt], h[:, :, :nt], [hk], ["HT"])


def stage_pool(P, G, T, with_ctx):
    with Stage(P) as S:
        PADL = 8
        hb = [S.sb("hb%d" % i, [128, LAT + 24]) for i in range(2)]
        d2 = S.sb("d2", [128, LAT + 24])
        d4 = S.sb("d4", [128, LAT + 24])
        rc = S.sb("rc", [128, 4, LAT])
        rcc = S.sb("rcc", [128, 4, CTXL])
        pm = [S.sb("pm%d" % i, [128, LAT]) for i in range(2)]
        po = [S.sb("po%d" % i, [128, LAT], BF16) for i in range(2)]
        P.dma("sp", rc[:], T["c_rcnt"], [], ["rc"])
        P.dma("sp", rcc[:], T["c_rcntc"], [], ["rc"])
        for i in range(2):
            P.op("pool", lambda e: e.memset(hb[i][:], 0.0), [], [("hb", i)])
        it = 0
        segs = []
        for b in range(NB):
            if with_ctx:
                segs.append((b * SEG, CTXL, True))
            segs.append((b * SEG + CTXL, LAT, False))
        for (tok0, Tn, is_ctx) in segs:
            for c in range(KC):
                gi = c // 2
                win = (2, 4, 8, 16)[gi]
                i2 = it % 2
                it += 1
                hbt = hb[i2]
                hk = ("hb", i2)
                if Tn < LAT:
                    P.op("pool", lambda e: e.memset(hbt[:, PADL + Tn:PADL + Tn + 16], 0.0), [], [hk])
                P.dma("sp", hbt[:, PADL:PADL + Tn], T["HT"][c * 128:(c + 1) * 128, tok0:tok0 + Tn], ["HT"], [hk])
                W_ = Tn + 16
                src = hbt
                sk = hk
                cur = 1
                bufs = [(d2, "d2"), (d4, "d4")]
                bi_ = 0
                while cur < win:
                    dst, dk = bufs[bi_ % 2]
                    bi_ += 1
                    P.tt("dve" if cur in (1, 4) else "pool", dst[:, 0:W_ - cur], src[:, 0:W_ - cur], src[:, cur:W_], ALU.add, [sk], [dk])
                    src, sk = dst, dk
                    cur *= 2
                off = PADL - win // 2
                rct = rcc if is_ctx else rc
                pmt, pot = pm[i2], po[i2]
                P.tt("dve", pmt[:, :Tn], src[:, off:off + Tn], rct[:, gi, :Tn], ALU.mult, [sk, "rc"], [("pm", i2)])
                P.tt("pool", pot[:, :Tn], pmt[:, :Tn], hbt[:, PADL:PADL + Tn], ALU.subtract, [("pm", i2), hk], [("po", i2)])
                P.dma("sp", T["AT"][c * 128:(c + 1) * 128, tok0:tok0 + Tn], pot[:, :Tn], [("po", i2)], ["AT"])


def stage_rwkv_prep(P, G, T, with_ctx=True):
    with Stage(P) as S:
        wr = [S.sb("wrkv%d" % i, [128, KC, D], BF16) for i in range(3)]
        for i in range(3):
            load_w(P, wr[i], T["rwkv_w_rkv"][0, i], KC, "wrkv")
        wg1 = S.sb("wg1", [128, KC, 160], BF16)
        load_w(P, wg1, T["rwkv_g1"][0], KC, "wl")
        wg2a = S.sb("wg2a", [128, D], BF16)
        wg2b = S.sb("wg2b", [32, D], BF16)
        P.dma("pool", wg2a[:], T["rwkv_g2"][0, 0:128, :], [], ["wl"])
        P.dma("pool", wg2b[:], T["rwkv_g2"][0, 128:160, :], [], ["wl"])
        w1 = S.sb("w1", [128, KC, 2, 128], BF16)
        for d in range(2):
            for kc in range(KC):
                P.dma("pool", w1[:, kc, d, 0:64], T["rwkv_w1"][0, d, kc * 128:(kc + 1) * 128, :], [], ["wl"])
                P.dma("pool", w1[:, kc, d, 64:128], T["rwkv_a1"][0, d, kc * 128:(kc + 1) * 128, :], [], ["wl"])
        w2 = S.sb("w2", [64, 2, D], BF16)
        a2 = S.sb("a2", [64, 2, D], BF16)
        for d in range(2):
            P.dma("pool", w2[:, d, :], T["rwkv_w2"][0, d], [], ["wl"])
            P.dma("pool", a2[:, d, :], T["rwkv_a2"][0, d], [], ["wl"])
        hb = S.sb("hb", [128, KC, 514])
        xx = S.sb("xx", [128, KC, 512])
        xms = [S.sb("xm%d" % i, [128, 6, KC, 512], BF16) for i in range(1)]
        lo = S.sb("lo", [128, 2, 512], BF16)
        la = S.sb("la", [128, 2, 512], BF16)
        lg = S.sb("lg", [128, 2, 512], BF16)
        ev = [S.sb("ev%d" % i, [128, 512]) for i in range(4)]
        HTv = T["HT"].rearrange("(c p) t -> p c t", p=128)
        ei = 0
        for bi, (tok0, nt, b, is_ctx, pos0) in enumerate(token_blocks(with_ctx=with_ctx)):
            xm = xms[0]
            xb = 0
            seg0 = b * SEG if is_ctx else b * SEG + CTXL
            segn = CTXL if is_ctx else LAT
            first = tok0 == seg0
            last = tok0 + nt == seg0 + segn
            lo_ = 0 if first else -1
            hi_ = 0 if last else 1
            if first:
                P.op("pool", lambda e: e.memset(hb[:, :, 0:1], 0.0), [], ["hb"])
            if last:
                P.op("pool", lambda e: e.memset(hb[:, :, nt + 1:nt + 2], 0.0), [], ["hb"])
            P.dma("sp", hb[:, :, 1 + lo_:1 + nt + hi_], HTv[:, :, tok0 + lo_:tok0 + nt + hi_], ["HT"], ["hb"])
            P.tt("pool", xx[:, :, :nt], hb[:, :, 0:nt], hb[:, :, 2:nt + 2], ALU.add, ["hb"], ["xx"])
            P.stt(xx[:, :, :nt], xx[:, :, :nt], 0.5, hb[:, :, 1:nt + 1], ALU.mult, ALU.subtract, ["xx", "hb"], ["xx"])
            for m in range(6):
                for c in range(KC):
                    P.stt(xm[:, m, c, :nt], xx[:, c, :nt], G["vec"][:, VO["mu"] + m * 8 + c:VO["mu"] + m * 8 + c + 1], hb[:, c, 1:nt + 1],
                          ALU.mult, ALU.add, ["xx", "hb"], [("xm", xb, m)])
            for i, (m, dst) in enumerate([(0, "RT"), (2, "KT"), (3, "VT")]):
                for mo in range(KC):
                    ps, pk = P.next_ps()
                    for kc in range(KC):
                        P.mm(ps[:, :nt], wr[i][:, kc, mo * 128:(mo + 1) * 128], xm[:, m, kc, :nt], kc == 0, kc == KC - 1,
                             ["wrkv", ("xm", xb, m)], [pk])
                    e_ = ev[ei % 4]
                    ek = ("ev", ei % 4)
                    ei += 1
                    P.copy("act" if mo % 2 == 0 else "dve", e_[:, :nt], ps[:, :nt], [pk], [ek])
                    P.dma("act", T[dst][mo * 128:(mo + 1) * 128, tok0:tok0 + nt], e_[:, :nt], [ek], [dst])
            for part, (o0, osz) in enumerate([(0, 128), (128, 32)]):
                ps, pk = P.next_ps()
                for kc in range(KC):
                    P.mm(ps[0:osz, :nt], wg1[:, kc, o0:o0 + osz], xm[:, 5, kc, :nt], kc == 0, kc == KC - 1, ["wl", ("xm", xb, 5)], [pk])
                P.act(lg[0:osz, part, :nt], ps[0:osz, :nt], AF.Sigmoid, [pk], ["lg"])
            for mo in range(KC):
                ps, pk = P.next_ps()
                P.mm(ps[:, :nt], wg2a[:, mo * 128:(mo + 1) * 128], lg[:, 0, :nt], True, False, ["wl", "lg"], [pk])
                P.mm(ps[:, :nt], wg2b[:, mo * 128:(mo + 1) * 128], lg[0:32, 1, :nt], False, True, ["wl", "lg"], [pk])
                e_ = ev[ei % 4]
                ek = ("ev", ei % 4)
                ei += 1
                P.copy("act" if mo % 2 == 0 else "dve", e_[:, :nt], ps[:, :nt], [pk], [ek])
                P.dma("act", T["GT"][mo * 128:(mo + 1) * 128, tok0:tok0 + nt], e_[:, :nt], [ek], ["GT"])
            for d in range(2):
                ps, pk = P.next_ps()
                for kc in range(KC):
                    P.mm(ps[0:64, :nt], w1[:, kc, d, 0:64], xm[:, 1, kc, :nt], kc == 0, kc == KC - 1, ["wl", ("xm", xb, 1)], [pk])
                P.act(lo[0:64, d, :nt], ps[0:64, :nt], AF.Tanh, [pk], ["lo"])
                ps, pk = P.next_ps()
                for kc in range(KC):
                    P.mm(ps[0:64, :nt], w1[:, kc, d, 64:128], xm[:, 4, kc, :nt], kc == 0, kc == KC - 1, ["wl", ("xm", xb, 4)], [pk])
                P.copy("dve", la[0:64, d, :nt], ps[0:64, :nt], [pk], ["la"])
                for (wsb, src, skey, vofs, dst) in [(w2, lo, "lo", VO["w0"], "SIG%d" % d), (a2, la, "la", VO["a0"], "AA%d" % d)]:
                    for mo in range(KC):
                        ps, pk = P.next_ps()
                        P.mm(ps[:, :nt], wsb[:, d, mo * 128:(mo + 1) * 128], src[0:64, d, :nt], True, True, ["wl", skey], [pk])
                        e_ = ev[ei % 4]
                        ek = ("ev", ei % 4)
                        ei += 1
                        P.act(e_[:, :nt], ps[:, :nt], AF.Sigmoid, [pk], [ek], bias=G["vec"][:, vofs + d * 8 + mo:vofs + d * 8 + mo + 1], scale=1.0)
                        P.dma("act", T[dst][mo * 128:(mo + 1) * 128, tok0:tok0 + nt], e_[:, :nt], [ek], [dst])


F32R = mybir.dt.float32r
SCAN_MODE = "f32"
SCAN_BF16 = True
SCAN_GH = 8


def _mo(ap):
    return ap.bitcast(F32R) if SCAN_MODE == "f32r" else ap


def stage_rwkv_scan(P, G, T):
    CH = 128
    NH = 16
    GH = SCAN_GH
    with Stage(P) as S:
        def big(name):
            return S.sb(name, [64, NH, CH])
        _base = [big("%s0" % n) for n in ("r_t", "k_t", "v_t", "s_t", "a_t")]
        ld = [_base, [_base[0], _base[1], big("v_t1"), _base[3], _base[4]]]

        def ldkey(li, ti):
            return ("ld", li if ti == 2 else 0, ti)
        tA, tB, tC, tD, tE, tF, tG = big("tA"), big("tB"), big("tC"), big("tD"), big("tE"), big("tF"), big("tG")
        tH = big("tH")
        MD = BF16 if SCAN_BF16 else F32
        QR = S.sb("QR", [64, NH, 2 * CH], MD)
        Hst = S.sb("Hst", [64, NH, 64])
        if SCAN_BF16:
            HstM = S.sb("HstM", [64, NH, 64], MD)
            kinvM = S.sb("kinvM", [64, NH, CH], MD)
            binvM = S.sb("binvM", [64, NH, CH], MD)
        else:
            HstM = Hst
        kkv = S.sb("kkv", [64, 3, NH])
        P.dma("sp", kkv[:, 0:2, :], T["c_v64"], [], ["kkv"])
        P.ts("dve", kkv[:, 2, :], kkv[:, 1, :], -1.0, 1.0, ALU.mult, ALU.add, ["kkv"], ["kkv"])
        tot = S.sb("tot", [64, NH])
        ones_s = S.sb("ones_s", [64, CH])
        P.op("pool", lambda e: e.memset(ones_s[:], 1.0), [], ["ones_s"])
        mk = S.sb("mk", [128, 2, 2 * CH])
        mkT = S.sb("mkT", [128, 2, CH])
        P.dma("sp", mk[:], T["c_mask"], [], ["mk"])
        P.dma("sp", mkT[:], T["c_maskT"], [], ["mk"])

        def hb(name, shape):
            return [S.sb("%s%d" % (name, i), shape) for i in range(GH)]
        def hbm(name, shape):
            return [S.sb("%s%d" % (name, i), shape, MD) for i in range(GH)]
        vT, kdT, bdT = hbm("vT", [128, 64]), hbm("kdT", [128, 64]), hbm("bdT", [128, 64])
        Nn, ABb, AK = hb("Nn", [128, CH]), hbm("ABb", [128, CH]), hbm("AK", [128, 2 * CH])
        X = [hb("Xa", [128, CH]), hb("Xb", [128, CH])]
        XT_ = [hb("XTa", [128, CH]), hb("XTb", [128, CH])]
        Pm = [hb("Pa", [128, CH]), hb("Pb", [128, CH])]
        rhs_t, negU = hb("rhs", [128, 64]), hbm("negU", [128, 64])
        yt = hb("yt", [64, CH])
        cdec = DEC_C
        evi = [0]

        def ev_eng():
            evi[0] += 1
            return "act" if evi[0] % 2 == 0 else "dve"

        def bc(ap2):
            return ap2.unsqueeze(2).broadcast_to([64, NH, CH])

        def view(Tn):
            return T[Tn].rearrange("(h k) t -> k h t", k=64)

        seq = []
        for b in range(NB):
            for d in range(2):
                order = [0, 1] + list(range(2, NSC)) if d == 0 else [1, 0] + list(range(NSC - 1, 1, -1))
                for oi_, sc in enumerate(order):
                    seq.append((b, d, sc, oi_ == 0))

        def issue_loads(si):
            b, d, sc, first = seq[si]
            tok0 = b * SEG + sc * CH
            li = si % 2
            for ti, nm in enumerate(["RT", "KT", "VT", "SIG%d" % d, "AA%d" % d]):
                P.dma("sp", ld[li][ti][:], view(nm)[:, :, tok0:tok0 + CH], [nm], [ldkey(li, ti)])

        tI = big("tI")
        wcs = [S.sb("wc%d" % i, [64, NH]) for i in range(2)]
        kiK, biK = "kinvM", "binvM"
        assert SCAN_BF16

        def prep_early(si):
            b, d, sc, first = seq[si]
            li = si % 2
            r_t, k_t, v_t, s_t, a_t = ld[li]
            kr, kk_, kv, ks, ka = [ldkey(li, ti) for ti in range(5)]
            lastc = CH - 1 if d == 0 else 0
            P.tt("pool", tA[:], k_t[:], bc(kkv[:, 0, :]), ALU.mult, [kk_, "kkv"], ["tA"])
            yield
            P.act(tH[:], tA[:], AF.Square, ["tA"], ["tH"])
            yield
            for q4 in range(4):
                ps, pk = P.next_ps()
                P.mm(ps[0:64, :], G["ones_f"][0:64, 0:64], tH[:, q4 * 4:(q4 + 1) * 4, :], True, True, ["tH"], [pk])
                P.act(tH[:, q4 * 4:(q4 + 1) * 4, :], ps[0:64, :], AF.Ln, [pk], ["tH"], bias=G["tiny_t"][0:64, 0:1], scale=1.0)
                yield
            P.act(tH[:], tH[:], AF.Exp, ["tH"], ["tH"], scale=-0.5)
            yield
            P.tt("dve", tA[:], tA[:], tH[:], ALU.mult, ["tA", "tH"], ["tA"])
            P.tt("pool", tC[:], a_t[:], bc(kkv[:, 1, :]), ALU.mult, [ka, "kkv"], ["tC"])
            yield
            P.tt("pool", tC[:], tC[:], bc(kkv[:, 2, :]), ALU.add, ["tC", "kkv"], ["tC"])
            yield
            P.tt("pool", tC[:], tC[:], k_t[:], ALU.mult, ["tC", kk_], ["tC"])
            P.tt("dve", tD[:], tA[:], a_t[:], ALU.mult, ["tA", ka], ["tD"])
            yield
            for h in range(NH):
                P.op("dve", lambda e: e.tensor_tensor_scan(out=tI[:, h, :], data0=ones_s[:], data1=s_t[:, h, :], initial=0.0,
                                                           op0=ALU.mult, op1=ALU.add), ["ones_s", ks], ["tI"])
                if h % 4 == 3:
                    yield
            if d == 1:
                P.copy("pool", tot[:], tI[:, :, CH - 1], ["tI"], ["tot"])
                P.tt("pool", tF[:], s_t[:], tI[:], ALU.subtract, [ks, "tI"], ["tF"])
                yield
                P.tt("pool", tI[:], tF[:], bc(tot[:]), ALU.add, ["tF", "tot"], ["tI"])
                yield
            P.tt("pool", tF[:], tI[:], s_t[:], ALU.subtract, ["tI", ks], ["tF"])
            P.act(tG[:], tI[:], AF.Exp, ["tI"], ["tG"], scale=-cdec)
            yield
            P.act(tF[:], tF[:], AF.Exp, ["tF"], ["tF"], scale=-cdec)
            P.act(tI[:], tI[:], AF.Exp, ["tI"], ["tI"], scale=cdec)
            P.copy("pool", wcs[li][:], tG[:, :, lastc], ["tG"], [("wc", li)])
            yield
            P.tt("dve", tC[:], tC[:], tI[:], ALU.mult, ["tC", "tI"], ["tC"])
            P.tt("pool", tD[:], tD[:], tI[:], ALU.mult, ["tD", "tI"], ["tD"])
            yield

        def prep_late(si):
            b, d, sc, first = seq[si]
            li = si % 2
            r_t = ld[li][0]
            kr = ldkey(li, 0)
            if first:
                P.op("pool", lambda e: e.memset(Hst[:], 0.0), [], [("Hst", h_) for h_ in range(NH)])
                P.op("pool", lambda e: e.memset(HstM[:], 0.0), [], [("HstM", h_) for h_ in range(NH)])
            P.tt("dve", QR[:, :, 0:CH], tA[:], tF[:], ALU.mult, ["tA", "tF"], ["QR"])
            P.tt("pool", QR[:, :, CH:2 * CH], r_t[:], tG[:], ALU.mult, [kr, "tG"], ["QR"])
            P.copy("act", binvM[:], tD[:], ["tD"], ["binvM"])
            P.copy("act", kinvM[:], tC[:], ["tC"], ["kinvM"])
            P.tt("dve", tB[:], tC[:], bc(wcs[li][:]), ALU.mult, ["tC", ("wc", li)], ["tB"])
            P.tt("pool", tE[:], tD[:], bc(wcs[li][:]), ALU.mult, ["tD", ("wc", li)], ["tE"])

        def run_all(gens):
            alive = list(gens)
            while alive:
                nxt = []
                for g_ in alive:
                    try:
                        next(g_)
                        nxt.append(g_)
                    except StopIteration:
                        pass
                alive = nxt

        issue_loads(0)
        run_all([prep_early(0)])
        for si, (b, d, sc, first) in enumerate(seq):
            prep_late(si)
            if si + 1 < len(seq):
                issue_loads(si + 1)
                pe_gen = prep_early(si + 1)
            else:
                pe_gen = None
            li = si % 2
            v_t = ld[li][2]
            kv = ldkey(li, 2)
            tok0 = b * SEG + sc * CH
            lastc = CH - 1 if d == 0 else 0
            kdec, bdec = tB, tE
            wc_t, wck = wcs[li], ("wc", li)

            def head(h, i2, d=d, tok0=tok0, v_t=v_t, kv=kv, kdec=kdec, bdec=bdec, wc_t=wc_t, wck=wck):
                def K(n):
                    return (n, i2)
                for (src, skey, dst, dkey) in [(v_t, kv, vT, "vT"), (kdec, "tB", kdT, "kdT"), (bdec, "tE", bdT, "bdT")]:
                    ps, pk = P.next_ps()
                    P.tr(ps[:, 0:64], src[:, h, :], G["ident"][0:64, 0:64], [skey], [pk])
                    P.copy(ev_eng(), dst[i2][:], ps[:, 0:64], [pk], [K(dkey)])
                yield
                ps, pk = P.next_ps()
                P.mm(ps[:, 0:2 * CH], binvM[:, h, :], QR[:, h, :], True, True, [biK, "QR"], [pk])
                P.tt("dve", Nn[i2][:], ps[:, 0:CH], mk[:, d, 0:CH], ALU.mult, [pk, "mk"], [K("Nn")])
                P.tt("dve", ABb[i2][:], ps[:, CH:2 * CH], mk[:, d, CH:2 * CH], ALU.mult, [pk, "mk"], [K("ABb")])
                ps, pk = P.next_ps()
                P.mm(ps[:, 0:2 * CH], kinvM[:, h, :], QR[:, h, :], True, True, [kiK, "QR"], [pk])
                P.tt("dve", AK[i2][:], ps[:, 0:2 * CH], mk[:, d, :], ALU.mult, [pk, "mk"], [K("AK")])
                ps, pk = P.next_ps()
                P.mm(ps[:, 0:CH], QR[:, h, 0:CH], binvM[:, h, :], True, True, [biK, "QR"], [pk])
                P.tt("dve", XT_[0][i2][:], ps[:, 0:CH], mkT[:, d, :], ALU.mult, [pk, "mk"], [K("XT0")])
                yield
                P.tt("pool", Pm[0][i2][:], G["ident"][:], Nn[i2][:], ALU.subtract, [K("Nn")], [K("P0")])
                Xc, XTc = Nn[i2][:], XT_[0][i2][:]
                xck, xtck = K("Nn"), K("XT0")
                pc = 0
                for step in range(1, 7):
                    nx = step % 2
                    lastst = step == 6
                    ps, pk = P.next_ps()
                    P.mm(ps[:, 0:CH], Xc, XTc, True, True, [xck, xtck], [pk])
                    P.copy(ev_eng(), XT_[nx][i2][:], ps[:, 0:CH], [pk], [K("XT%d" % nx)])
                    yield
                    if not lastst:
                        ps, pk = P.next_ps()
                        P.tr(ps[:, 0:CH], XT_[nx][i2][:], G["ident"][:], [K("XT%d" % nx)], [pk])
                        P.copy(ev_eng(), X[nx][i2][:], ps[:, 0:CH], [pk], [K("X%d" % nx)])
                    ps, pk = P.next_ps()
                    P.mm(ps[:, 0:CH], XT_[nx][i2][:], Pm[pc][i2][:], True, True, [K("XT%d" % nx), K("P%d" % pc)], [pk])
                    P.tt("dve", Pm[1 - pc][i2][:], ps[:, 0:CH], Pm[pc][i2][:], ALU.add, [pk, K("P%d" % pc)], [K("P%d" % (1 - pc))])
                    pc = 1 - pc
                    Xc, XTc = X[nx][i2][:], XT_[nx][i2][:]
                    xck, xtck = K("X%d" % nx), K("XT%d" % nx)
                    yield
                Tt, tk = Pm[pc][i2], K("P%d" % pc)
                ps, pk = P.next_ps()
                P.mm(ps[:, 0:64], QR[:, h, 0:CH], HstM[:, h, :], True, False, ["QR", ("HstM", h)], [pk])
                P.mm(ps[:, 0:64], AK[i2][:, 0:CH], vT[i2][:], False, True, [K("AK"), K("vT")], [pk])
                P.copy(ev_eng(), rhs_t[i2][:], ps[:, 0:64], [pk], [K("rhs")])
                yield
                ps, pk = P.next_ps()
                P.mm(ps[:, 0:64], Tt[:], rhs_t[i2][:], True, True, [tk, K("rhs")], [pk])
                P.act(negU[i2][:], ps[:, 0:64], AF.Copy, [pk], [K("negU")], scale=-1.0)
                yield
                ps, pk = P.next_ps()
                P.mm(ps[0:64, 0:CH], HstM[:, h, :], QR[:, h, CH:2 * CH], True, False, ["QR", ("HstM", h)], [pk])
                P.mm(ps[0:64, 0:CH], vT[i2][:], AK[i2][:, CH:2 * CH], False, False, [K("AK"), K("vT")], [pk])
                P.mm(ps[0:64, 0:CH], negU[i2][:], ABb[i2][:], False, True, [K("ABb"), K("negU")], [pk])
                P.copy(ev_eng(), yt[i2][:], ps[0:64, 0:CH], [pk], [K("yt")])
                P.dma("sp", T["YT%d" % d][h * 64:(h + 1) * 64, tok0:tok0 + CH], yt[i2][:], [K("yt")], ["YT%d" % d])
                ps, pk = P.next_ps()
                P.mm(ps[0:64, 0:64], kdT[i2][:], vT[i2][:], True, False, [K("kdT"), K("vT")], [pk])
                P.mm(ps[0:64, 0:64], bdT[i2][:], negU[i2][:], False, True, [K("bdT"), K("negU")], [pk])
                P.stt(Hst[:, h, :], Hst[:, h, :], wc_t[:, h:h + 1], ps[0:64, 0:64], ALU.mult, ALU.add,
                      [("Hst", h), wck, pk], [("Hst", h)])
                if MD is not F32:
                    P.copy("pool", HstM[:, h, :], Hst[:, h, :], [("Hst", h)], [("HstM", h)])
                yield

            for h0 in range(0, NH, GH):
                gens = [head(h0 + i, i) for i in range(GH)]
                if pe_gen is not None:
                    gens.append(pe_gen)
                alive = list(gens)
                while alive:
                    nxt = []
                    for g_ in alive:
                        try:
                            next(g_)
                            if g_ is pe_gen and len(alive) == 1 and h0 + GH < NH:
                                nxt.append(g_)
                                break
                            nxt.append(g_)
                        except StopIteration:
                            if g_ is pe_gen:
                                pe_gen = None
                    if len(nxt) == 1 and nxt[0] is pe_gen and h0 + GH < NH:
                        break
                    alive = nxt
            if pe_gen is not None:
                run_all([pe_gen])


def stage_rwkv_finish(P, G, T, with_ctx=True):
    NI = 3
    with Stage(P) as S:
        def t5(name, dt=F32):
            return [S.sb("%s%d" % (name, i), [128, 512], dt) for i in range(NI)]
        y0, y1, rr, kk_, vv, gg, a0, a1 = t5("y0"), t5("y1"), t5("r"), t5("k"), t5("v"), t5("g"), t5("a0"), t5("a1")
        ysq, mean, var, tmp, bon = t5("ysq"), t5("mean"), t5("var"), t5("tmp"), t5("bon")
        ao = t5("ao", BF16)

        def body(i2, tok0, nt, c):
            def K(n):
                return (n, i2)
            sl = (slice(c * 128, (c + 1) * 128), slice(tok0, tok0 + nt))
            for (tl, nm) in [(y0, "YT0"), (y1, "YT1"), (rr, "RT"), (kk_, "KT"), (vv, "VT"), (gg, "GT"), (a0, "AA0"), (a1, "AA1")]:
                P.dma("sp", tl[i2][:, :nt], T[nm][sl[0], sl[1]], [nm], [K(nm)])
            yield
            y = y0[i2]
            P.tt("pool", y[:, :nt], y0[i2][:, :nt], y1[i2][:, :nt], ALU.add, [K("YT0"), K("YT1")], [K("YT0")])
            P.tt("pool", tmp[i2][:, :nt], a0[i2][:, :nt], a1[i2][:, :nt], ALU.add, [K("AA0"), K("AA1")], [K("tmp")])
            yield
            ps, pk = P.next_ps()
            P.mm(ps[:, :nt], G["bones"][:], y[:, :nt], True, True, [K("YT0")], [pk])
            P.act(mean[i2][:, :nt], ps[:, :nt], AF.Copy, [pk], [K("mean")], scale=1.0 / 64)
            P.ts("dve", tmp[i2][:, :nt], tmp[i2][:, :nt], -2.0, G["vec128_ka"][:, c:c + 1], ALU.add, ALU.mult, [K("tmp")], [K("tmp")])
            yield
            P.tt("dve", y[:, :nt], y[:, :nt], mean[i2][:, :nt], ALU.subtract, [K("YT0"), K("mean")], [K("YT0")])
            P.act(ysq[i2][:, :nt], y[:, :nt], AF.Square, [K("YT0")], [K("ysq")])
            P.stt(tmp[i2][:, :nt], tmp[i2][:, :nt], 2.0, kk_[i2][:, :nt], ALU.add, ALU.mult, [K("tmp"), K("KT")], [K("tmp")])
            yield
            ps, pk = P.next_ps()
            P.mm(ps[:, :nt], G["bones"][:], ysq[i2][:, :nt], True, True, [K("ysq")], [pk])
            P.act(var[i2][:, :nt], ps[:, :nt], AF.Ln, [pk], [K("var")], bias=G["gneps_t"][:, 0:1], scale=1.0 / 64)
            P.stt(tmp[i2][:, :nt], tmp[i2][:, :nt], G["vec"][:, VO["r_k"] + c:VO["r_k"] + c + 1], rr[i2][:, :nt], ALU.mult, ALU.mult,
                  [K("tmp"), K("RT")], [K("tmp")])
            yield
            P.act(var[i2][:, :nt], var[i2][:, :nt], AF.Exp, [K("var")], [K("var")], scale=-0.5)
            ps, pk = P.next_ps()
            P.mm(ps[:, :nt], G["bones"][:], tmp[i2][:, :nt], True, True, [K("tmp")], [pk])
            P.tt("dve", bon[i2][:, :nt], ps[:, :nt], vv[i2][:, :nt], ALU.mult, [pk, K("VT")], [K("bon")])
            yield
            P.tt("dve", y[:, :nt], y[:, :nt], var[i2][:, :nt], ALU.mult, [K("YT0"), K("var")], [K("YT0")])
            yield
            P.act(y[:, :nt], y[:, :nt], AF.Identity, [K("YT0")], [K("YT0")],
                  scale=G["vec"][:, VO["ln_g"] + c:VO["ln_g"] + c + 1], bias=G["vec"][:, VO["ln_b"] + c:VO["ln_b"] + c + 1])
            yield
            P.tt("pool", y[:, :nt], y[:, :nt], bon[i2][:, :nt], ALU.add, [K("YT0"), K("bon")], [K("YT0")])
            yield
            P.tt("pool", ao[i2][:, :nt], y[:, :nt], gg[i2][:, :nt], ALU.mult, [K("YT0"), K("GT")], [K("ao")])
            P.dma("act", T["AT"][sl[0], sl[1]], ao[i2][:, :nt], [K("ao")], ["AT"])
            yield

        its = [(tok0, nt, c) for (tok0, nt, b, is_ctx, pos0) in token_blocks(with_ctx=with_ctx) for c in range(KC)]
        for g0 in range(0, len(its), NI):
            alive = [body(i, *its[g0 + i]) for i in range(min(NI, len(its) - g0))]
            while alive:
                nxt = []
                for g_ in alive:
                    try:
                        next(g_)
                        nxt.append(g_)
                    except StopIteration:
                        pass
                alive = nxt


INPUT_NAMES = ["w_mod", "mlp_w_in", "mlp_w_out", "attn_w_qkv", "attn_w_o", "rwkv_w_rkv", "rwkv_w1", "rwkv_w2",
               "rwkv_a1", "rwkv_a2", "rwkv_g1", "rwkv_g2", "rwkv_w_o", "pool_w"]
INPUT_SHAPES = {"w_mod": [4, 1024, 6144], "mlp_w_in": [4, 1024, 4096], "mlp_w_out": [4, 4096, 1024],
                "attn_w_qkv": [2, 1024, 1536], "attn_w_o": [2, 1024, 1024], "rwkv_w_rkv": [1, 3, 1024, 1024],
                "rwkv_w1": [1, 2, 1024, 64], "rwkv_w2": [1, 2, 64, 1024], "rwkv_a1": [1, 2, 1024, 64],
                "rwkv_a2": [1, 2, 64, 1024], "rwkv_g1": [1, 1024, 160], "rwkv_g2": [1, 160, 1024],
                "rwkv_w_o": [1, 1024, 1024], "pool_w": [1, 4, 256, 256]}
CONST_SHAPES = {"cvT": [128, KC, 3], "c_vec": [128, NV], "c_ident": [128, 128], "c_ones": [128, 128], "c_bones": [128, 128],
                "c_rot": [128, 128], "c_cos": [128, LAT], "c_sin": [128, LAT], "c_mask": [128, 2, 256], "c_maskT": [128, 2, 128],
                "c_rcnt": [128, 4, LAT], "c_rcntc": [128, 4, CTXL], "c_v64": [64, 2, 16], "c_ka128": [128, KC],
                "c_small": [128, 4]}


def build_program(n_layers=DEPTH, dbg=None):
    nc = bass.Bass("TRN2", target_bir_lowering=False)
    T = {}
    T["x2"] = nc.dram_tensor("x2", [NB, LAT, D], F32, kind="ExternalInput").ap()
    T["ctx2"] = nc.dram_tensor("ctx2", [NB, CTXL, D], F32, kind="ExternalInput").ap()
    for n in INPUT_NAMES:
        T[n] = nc.dram_tensor(n, INPUT_SHAPES[n], F32, kind="ExternalInput").ap()
    for n, s in CONST_SHAPES.items():
        T[n] = nc.dram_tensor(n, s, F32, kind="ExternalInput").ap()
    T["y"] = nc.dram_tensor("y", [NB, LAT, D], F32, kind="ExternalOutput").ap()

    def scr(name, shape, dt=F32):
        kind = "ExternalOutput" if (dbg and name in dbg) else "Internal"
        T[name] = nc.dram_tensor(name, shape, dt, kind=kind).ap()

    scr("XT", [D, NTOK])
    scr("HT", [D, NTOK])
    scr("QT", [D, NTOK], BF16)
    scr("KTs", [256, NTOK], BF16)
    scr("Vs", [NB, 4, 128, NSC, 64], BF16)
    scr("AT", [D, NTOK], BF16)
    for n in ["RT", "KT", "VT", "GT", "SIG0", "SIG1", "AA0", "AA1", "YT0", "YT1"]:
        scr(n, [D, NTOK])

    with ExitStack() as es:
        P = Prog(nc, es)
        G = {}

        def gsb(name, shape, dt=F32):
            return es.enter_context(nc.sbuf_tensor(name, list(shape), dt))

        G["ident"] = gsb("ident", [128, 128])
        G["ones_f"] = gsb("ones_f", [128, 128])
        G["bones"] = gsb("bones", [128, 128])
        G["ones_b"] = gsb("ones_b", [128, 128], BF16)
        G["rot"] = gsb("rot", [128, 128])
        G["vec"] = gsb("vec", [128, NV])
        G["vec128_ka"] = gsb("ka128", [128, KC])
        small = gsb("small", [128, 4])
        G["eps_t"] = small[:, 0:1]
        G["tiny_t"] = small[:, 1:2]
        G["gneps_t"] = small[:, 2:3]
        G["sil"] = gsb("sil", [128, KC, 3])
        G["mod"] = [gsb("mod%d" % L, [128, 48, 3]) for L in range(DEPTH)]
        G["sc1"] = [gsb("sc1_%d" % L, [128, KC, 3]) for L in range(DEPTH)]
        G["sc2"] = [gsb("sc2_%d" % L, [128, KC, 3]) for L in range(DEPTH)]
        for (t, n) in [("ident", "c_ident"), ("ones_f", "c_ones"), ("bones", "c_bones"), ("rot", "c_rot"), ("vec", "c_vec"),
                       ("vec128_ka", "c_ka128")]:
            P.dma("sp", G[t][:], T[n], [], ["consts"])
        P.dma("sp", small[:], T["c_small"], [], ["consts"])
        P.dma("pool", G["ones_b"][:], T["c_ones"], [], ["consts"])
        P.barrier()

        stage_prologue(P, G, T)
        for L in range(n_layers):
            last = L == DEPTH - 1
            kind = L % 3
            j = L // 3
            wc = not last
            if kind == 0:
                stage_attn_qkv(P, G, T, L, j)
                stage_attn_core(P, G, T, wc, bg_layers=(list(range(1, n_layers)) if L == 0 else ()))
                stage_mix_mlp(P, G, T, L, T["attn_w_o"][j], "full", wc)
            elif kind == 1:
                stage_norm_to_HT(P, G, T, L, True)
                stage_rwkv_prep(P, G, T)
                stage_rwkv_scan(P, G, T)
                stage_rwkv_finish(P, G, T, wc)
                stage_mix_mlp(P, G, T, L, T["rwkv_w_o"][j], "full", wc)
            else:
                stage_norm_to_HT(P, G, T, L, wc)
                stage_pool(P, G, T, wc)
                stage_mix_mlp(P, G, T, L, T["pool_w"][j], "pool", wc)
        stage_epilogue(P, G, T)
        P.barrier()
        print("program: %d instructions, %d waits" % (P.n_inst, P.n_wait))
    return nc


def _pm(v, nch=KC):
    return np.ascontiguousarray(np.asarray(v, np.float32).reshape(nch, 128).T)


def host_consts(inp):
    c = {}
    vec = np.zeros((128, NV), np.float32)
    for L in range(DEPTH):
        vec[:, VO["bmod"] + L * 48:VO["bmod"] + (L + 1) * 48] = _pm(inp["b_mod"][L], 48)
        vec[:, VO["g1"] + L * 8:VO["g1"] + (L + 1) * 8] = _pm(inp["norm1_g"][L])
        vec[:, VO["g2"] + L * 8:VO["g2"] + (L + 1) * 8] = _pm(inp["norm2_g"][L])
    for j in range(2):
        vec[:, VO["qg"] + j] = np.tile(inp["attn_q_gain"][j], 2)
        vec[:, VO["kg"] + j] = np.tile(inp["attn_k_gain"][j], 2)
    vec[:, VO["pscale"]:VO["pscale"] + 8] = _pm(inp["pool_scale"][0])
    for m in range(6):
        vec[:, VO["mu"] + m * 8:VO["mu"] + (m + 1) * 8] = _pm(inp["rwkv_mu"][0, m])
    for d in range(2):
        vec[:, VO["w0"] + d * 8:VO["w0"] + (d + 1) * 8] = _pm(inp["rwkv_w0"][0, d])
        vec[:, VO["a0"] + d * 8:VO["a0"] + (d + 1) * 8] = _pm(inp["rwkv_a0"][0, d])
    vec[:, VO["r_k"]:VO["r_k"] + 8] = _pm(inp["rwkv_r_k"][0])
    vec[:, VO["ln_g"]:VO["ln_g"] + 8] = _pm(inp["rwkv_ln_g"][0])
    vec[:, VO["ln_b"]:VO["ln_b"] + 8] = _pm(inp["rwkv_ln_b"][0])
    c["c_vec"] = vec
    c["c_ka128"] = _pm(inp["rwkv_k_a"][0])
    v64 = np.zeros((64, 2, 16), np.float32)
    v64[:, 0, :] = np.asarray(inp["rwkv_k_k"][0], np.float32).reshape(16, 64).T
    v64[:, 1, :] = np.asarray(inp["rwkv_k_a"][0], np.float32).reshape(16, 64).T
    c["c_v64"] = v64
    c["c_ident"] = np.eye(128, dtype=np.float32)
    c["c_ones"] = np.ones((128, 128), np.float32)
    bo = np.zeros((128, 128), np.float32)
    bo[:64, :64] = 1.0
    bo[64:, 64:] = 1.0
    c["c_bones"] = bo
    rot = np.zeros((128, 128), np.float32)
    for hh in range(2):
        for jj in range(32):
            rot[hh * 64 + jj + 32, hh * 64 + jj] = -1.0
            rot[hh * 64 + jj, hh * 64 + jj + 32] = 1.0
    c["c_rot"] = rot
    nf = 16
    inv = (10000.0 ** (-np.arange(nf, dtype=np.float32) / nf)).astype(np.float32)
    t = np.arange(LAT)
    rows = (t // 64).astype(np.float32)
    cols = (t % 64).astype(np.float32)
    ang = np.concatenate([rows[:, None] * inv[None, :], cols[:, None] * inv[None, :]], axis=1).astype(np.float32)
    cosf = np.cos(ang).astype(np.float32).T
    sinf = np.sin(ang).astype(np.float32).T
    c["c_cos"] = np.ascontiguousarray(np.tile(cosf, (4, 1)))
    c["c_sin"] = np.ascontiguousarray(np.tile(sinf, (4, 1)))
    s = np.arange(128)[:, None]
    tt_ = np.arange(128)[None, :]
    mk = np.zeros((128, 2, 256), np.float32)
    mk[:, 0, 0:128] = (tt_ > s)
    mk[:, 0, 128:256] = (tt_ >= s)
    mk[:, 1, 0:128] = (tt_ < s)
    mk[:, 1, 128:256] = (tt_ <= s)
    c["c_mask"] = mk
    mkT = np.zeros((128, 2, 128), np.float32)
    mkT[:, 0, :] = (tt_ < s)
    mkT[:, 1, :] = (tt_ > s)
    c["c_maskT"] = mkT

    def rcnt(Tn):
        out = np.zeros((4, Tn), np.float32)
        tpos = np.arange(Tn)
        for gi, win in enumerate((2, 4, 8, 16)):
            lo = np.clip(tpos - win // 2, 0, Tn)
            hi = np.clip(tpos + win // 2, 0, Tn)
            out[gi] = 1.0 / (hi - lo).astype(np.float32)
        return out
    c["c_rcnt"] = np.ascontiguousarray(np.broadcast_to(rcnt(LAT)[None], (128, 4, LAT))).astype(np.float32)
    c["c_rcntc"] = np.ascontiguousarray(np.broadcast_to(rcnt(CTXL)[None], (128, 4, CTXL))).astype(np.float32)
    sm = np.zeros((128, 4), np.float32)
    sm[:, 0] = EPS
    sm[:, 1] = 1e-30
    sm[:, 2] = 64 * 1e-5
    c["c_small"] = sm
    return c


def make_in_maps(inp, cores):
    consts = host_consts(inp)
    shared = {n: np.ascontiguousarray(np.asarray(inp[n], np.float32)) for n in INPUT_NAMES}
    maps = []
    for i in cores:
        m = dict(shared)
        m.update(consts)
        m["x2"] = np.ascontiguousarray(inp["x"][NB * i:NB * (i + 1)], dtype=np.float32)
        m["ctx2"] = np.ascontiguousarray(inp["ctx"][NB * i:NB * (i + 1)], dtype=np.float32)
        cv = np.stack([inp["c"][NB * i], inp["c"][NB * i + 1], inp["c_ctx"]], axis=0).astype(np.float32)
        m["cvT"] = np.ascontiguousarray(cv.reshape(3, KC, 128).transpose(2, 1, 0))
        maps.append(m)
    return maps


_NC_CACHE = {}


def kernel(**inputs):
    inp = {k: np.asarray(v) for k, v in inputs.items()}
    if "nc" not in _NC_CACHE:
        _NC_CACHE["nc"] = build_program()
    nc = _NC_CACHE["nc"]
    in_maps = make_in_maps(inp, list(range(NCORES)))
    res = run_bass_kernel_spmd(nc, in_maps, core_ids=list(range(NCORES)))
    out = np.concatenate([np.asarray(r["y"]) for r in res.results], axis=0)
    return out.astype(np.float32)
```

```python
from contextlib import ExitStack
import numpy as np
import concourse.bass as bass
import concourse.mybir as mybir
from concourse.bass_utils import run_bass_kernel_spmd

F32 = mybir.dt.float32
BF16 = mybir.dt.bfloat16
AF = mybir.ActivationFunctionType
ALU = mybir.AluOpType
AX = mybir.AxisListType

NCORES = 8
NB = 2
LAT = 4096
CTXL = 256
SEG = LAT + CTXL
NTOK = NB * SEG
D = 1024
KC = 8
DFF = 4096
NSC = SEG // 128
EPS = 1e-6
DEPTH = 4
DEC_C = float(np.exp(-0.5))


class Prog:
    N_DMA_SLOTS = 8
    EPOCH = 30000

    def __init__(self, nc, es):
        self.nc = nc
        self.es = es
        self.eng = {"pe": nc.tensor, "act": nc.scalar, "dve": nc.vector, "pool": nc.gpsimd, "sp": nc.sync}
        self.sem = {}
        self.cnt = {}
        self.cur = {}
        self.epoch = {}
        for e in ["pe", "act", "dve", "pool"]:
            self.epoch[e] = 0
            self._new_epoch(e)
        self.dma_slots = {}
        for q in ["sp", "pool", "act"]:
            lst = []
            for i in range(self.N_DMA_SLOTS):
                name = "d_%s%d" % (q, i)
                self.sem[name] = es.enter_context(nc.semaphore(name))
                self.cnt[name] = 0
                lst.append(name)
            self.dma_slots[q] = lst
        self.dma_rr = {"sp": 0, "pool": 0, "act": 0}
        self.waited = {}
        self.last_w = {}
        self.readers = {}
        self.n_inst = 0
        self.n_wait = 0
        self.ps_tiles = [es.enter_context(nc.psum_tensor("psb%d" % i, [128, 512], F32)) for i in range(8)]
        self.ps_rr = 0

    def _new_epoch(self, e):
        name = "%s#%d" % (e, self.epoch[e])
        self.epoch[e] += 1
        self.sem[name] = self.es.enter_context(self.nc.semaphore("s_" + name.replace("#", "_")))
        self.cnt[name] = 0
        self.cur[e] = name

    def next_ps(self):
        i = self.ps_rr % 8
        self.ps_rr += 1
        return self.ps_tiles[i], ("ps", i)

    def _deps(self, me, reads, writes):
        deps = {}

        def add(t, s):
            if deps.get(t, 0) < s:
                deps[t] = s

        for r in reads:
            if r in self.last_w:
                add(*self.last_w[r])
        for w in writes:
            if w in self.last_w:
                add(*self.last_w[w])
            for t, s in self.readers.get(w, {}).items():
                if t.split("#")[0] == me:
                    continue
                add(t, s)
        return deps

    def _wait(self, me, deps):
        e = self.eng[me]
        for t, s in deps.items():
            if me == "pe" and t.split("#")[0] == "pe":
                continue
            if self.waited.get((me, t), 0) >= s:
                continue
            e.wait_ge(self.sem[t], s)
            self.waited[(me, t)] = s
            self.n_wait += 1

    def _commit(self, tok, reads, writes):
        t, s = tok
        for r in reads:
            d = self.readers.setdefault(r, {})
            if d.get(t, 0) < s:
                d[t] = s
        for w in writes:
            self.last_w[w] = tok
            self.readers[w] = {}

    def op(self, me, fn, reads=(), writes=()):
        self._wait(me, self._deps(me, reads, writes))
        inst = fn(self.eng[me])
        name = self.cur[me]
        self.cnt[name] += 1
        inst.then_inc(self.sem[name], 1)
        self._commit((name, self.cnt[name]), reads, writes)
        if self.cnt[name] >= self.EPOCH:
            self._new_epoch(me)
        self.n_inst += 1
        return inst

    def dma(self, q, out, in_, reads=(), writes=(), **kw):
        deps = self._deps("dma", reads, writes)
        slot = self.dma_slots[q][self.dma_rr[q] % self.N_DMA_SLOTS]
        self.dma_rr[q] += 1
        if self.cnt[slot] > 0 and deps.get(slot, 0) < self.cnt[slot]:
            deps[slot] = self.cnt[slot]
        if self.cnt[slot] >= 16 * 2100:
            raise RuntimeError("dma slot counter too large")
        self._wait(q, deps)
        inst = self.eng[q].dma_start(out=out, in_=in_, **kw)
        self.cnt[slot] += 16
        inst.then_inc(self.sem[slot], 16)
        self._commit((slot, self.cnt[slot]), reads, writes)
        self.n_inst += 1
        return inst

    def barrier(self):
        for me in ["sp", "pool", "act", "dve", "pe"]:
            deps = {t: c for t, c in self.cnt.items() if c > 0}
            if me == "pe":
                deps = {t: c for t, c in deps.items() if t.split("#")[0] != "pe"}
            self._wait(me, deps)
        self.last_w = {}
        self.readers = {}

    def mm(self, out, lhsT, rhs, start, stop, reads, writes):
        return self.op("pe", lambda e: e.matmul(out, lhsT=lhsT, rhs=rhs, start=start, stop=stop), reads, writes)

    def tr(self, out, in_, ident, reads, writes):
        return self.op("pe", lambda e: e.transpose(out=out, in_=in_, identity=ident), reads, writes)

    def act(self, out, in_, func, reads, writes, **kw):
        return self.op("act", lambda e: e.activation(out=out, in_=in_, func=func, **kw), reads, writes)

    def tt(self, eng, out, in0, in1, op, reads, writes):
        return self.op(eng, lambda e: e.tensor_tensor(out=out, in0=in0, in1=in1, op=op), reads, writes)

    def ts(self, eng, out, in0, s1, s2, op0, op1, reads, writes):
        if op1 is None:
            return self.op(eng, lambda e: e.tensor_scalar(out=out, in0=in0, scalar1=s1, scalar2=None, op0=op0),
                           reads, writes)
        return self.op(eng, lambda e: e.tensor_scalar(out=out, in0=in0, scalar1=s1, scalar2=s2, op0=op0, op1=op1),
                       reads, writes)

    def stt(self, out, in0, scalar, in1, op0, op1, reads, writes):
        return self.op("dve", lambda e: e.scalar_tensor_tensor(out=out, in0=in0, scalar=scalar, in1=in1,
                                                               op0=op0, op1=op1), reads, writes)

    def copy(self, eng, out, in_, reads, writes):
        if eng == "act":
            return self.act(out, in_, AF.Copy, reads, writes)
        return self.op(eng, lambda e: e.tensor_copy(out=out, in_=in_), reads, writes)


class Stage:
    _n = [0]

    def __init__(self, P):
        self.P = P
        self.es = ExitStack()
        Stage._n[0] += 1
        self.tag = "s%d_" % Stage._n[0]

    def __enter__(self):
        self.es.__enter__()
        return self

    def sb(self, name, shape, dt=F32):
        return self.es.enter_context(self.P.nc.sbuf_tensor(self.tag + name, list(shape), dt))

    def __exit__(self, *a):
        self.P.barrier()
        return self.es.__exit__(*a)


def token_blocks(with_ctx=True):
    out = []
    for b in range(NB):
        if with_ctx:
            out.append((b * SEG, CTXL, b, True, 0))
        for j in range(LAT // 512):
            out.append((b * SEG + CTXL + j * 512, 512, b, False, j * 512))
    return out


VO = {}
_o = 0
for _n, _w in [("bmod", 4 * 48), ("g1", 32), ("g2", 32), ("qg", 2), ("kg", 2), ("pscale", 8), ("mu", 48),
               ("w0", 16), ("a0", 16), ("r_k", 8), ("ln_g", 8), ("ln_b", 8)]:
    VO[_n] = _o
    _o += _w
NV = _o


def emit_norm(P, G, x_t, xkey, nt, L, which, n, h_t, hkey, tmp, sq, rstd, sqkeys=None):
    sc = G["sc1"] if which == 1 else G["sc2"]
    shm = 0 if which == 1 else 3
    sqk = sqkeys if sqkeys is not None else ["sq"] * KC
    xks = xkey if isinstance(xkey, list) else [xkey] * KC
    P.act(sq[:, 0:KC, :nt], x_t[:, :, :nt], AF.Square, list(set(xks)), list(set(sqk)))
    ps, pk = P.next_ps()
    for c in range(KC):
        P.mm(ps[:, :nt], G["ones_b"][:], sq[:, c, :nt], c == 0, c == KC - 1, [sqk[c]], [pk])
    if rstd is None:
        rs_t, rs_k = P.next_ps()
        P.act(rs_t[:, :nt], ps[:, :nt], AF.Ln, [pk], [rs_k], bias=G["eps_t"][:, 0:1], scale=1.0 / D)
        P.act(rs_t[:, :nt], rs_t[:, :nt], AF.Exp, [rs_k], [rs_k], scale=-0.5)
        for c in range(KC):
            tp, tk = P.next_ps()
            if tp is rs_t:
                tp, tk = P.next_ps()
            P.tt("dve", tp[:, :nt], x_t[:, c, :nt], rs_t[:, :nt], ALU.mult, [xks[c], rs_k], [tk])
            P.act(h_t[:, c, :nt], tp[:, :nt], AF.Identity, [tk], [hkey],
                  scale=sc[L][:, c, n:n + 1], bias=G["mod"][L][:, shm * 8 + c, n:n + 1])
        return
    P.act(rstd[:, :nt], ps[:, :nt], AF.Ln, [pk], ["rstd"], bias=G["eps_t"][:, 0:1], scale=1.0 / D)
    P.act(rstd[:, :nt], rstd[:, :nt], AF.Exp, ["rstd"], ["rstd"], scale=-0.5)
    for c in range(KC):
        tk = ("ntmp", c % 2)
        P.tt("dve" if c % 2 == 0 else "pool", tmp[c % 2][:, :nt], x_t[:, c, :nt], rstd[:, :nt], ALU.mult,
             [xks[c], "rstd"], [tk])
        P.act(h_t[:, c, :nt], tmp[c % 2][:, :nt], AF.Identity, [tk], [hkey],
              scale=sc[L][:, c, n:n + 1], bias=G["mod"][L][:, shm * 8 + c, n:n + 1])


def load_w(P, w_sb, w_dram, kcs, key, q="pool"):
    wv = w_dram.rearrange("(kc p) o -> p kc o", p=128)
    for kc in range(kcs):
        P.dma(q, w_sb[:, kc, :], wv[:, kc, :], [], [key])


def mod_steps(P, G, T, S, layers):
    wt = [S.sb("wmod%d" % i, [128, KC, 512]) for i in range(2)]
    sil = G["sil"]
    items = [(L, blk) for L in layers for blk in range(12)]

    def load(i):
        L, blk = items[i]
        wv = T["w_mod"][L].rearrange("(kc p) o -> p kc o", p=128)
        P.dma("sp", wt[i % 2][:], wv[:, :, blk * 512:(blk + 1) * 512], [], [("wmod", i % 2)])

    if items:
        load(0)
        yield
    for i, (L, blk) in enumerate(items):
        if i + 1 < len(items):
            load(i + 1)
        w = wt[i % 2]
        wk = ("wmod", i % 2)
        for mo in range(4):
            j = blk * 4 + mo
            ps, pk = P.next_ps()
            for kc in range(KC):
                P.mm(ps[:, 0:3], w[:, kc, mo * 128:(mo + 1) * 128], sil[:, kc, :], kc == 0, kc == KC - 1,
                     [wk, "sil"], [pk])
            P.ts("dve", G["mod"][L][:, j, :], ps[:, 0:3], G["vec"][:, VO["bmod"] + L * 48 + j:VO["bmod"] + L * 48 + j + 1],
                 None, ALU.add, None, [pk], [("mod", L)])
        if blk == 11:
            for n in range(3):
                P.stt(G["sc1"][L][:, :, n], G["mod"][L][:, 8:16, n], 1.0, G["vec"][:, VO["g1"] + L * 8:VO["g1"] + L * 8 + 8],
                      ALU.add, ALU.mult, [("mod", L)], [("mod", L)])
                P.stt(G["sc2"][L][:, :, n], G["mod"][L][:, 32:40, n], 1.0, G["vec"][:, VO["g2"] + L * 8:VO["g2"] + L * 8 + 8],
                      ALU.add, ALU.mult, [("mod", L)], [("mod", L)])
        yield


def stage_prologue(P, G, T):
    nc = P.nc
    with Stage(P) as S:
        cv = S.sb("cv", [128, KC, 3])
        P.dma("sp", cv[:], T["cvT"], [], ["cv"])
        P.act(G["sil"][:], cv[:], AF.Silu, ["cv"], ["sil"])
        for _ in mod_steps(P, G, T, S, [0]):
            pass
    with Stage(P) as S:
        xtm = [S.sb("xtm%d" % i, [128, 4, D]) for i in range(2)]
        xT = [S.sb("xT%d" % i, [128, KC, 512]) for i in range(2)]
        XTv = T["XT"].rearrange("(c p) t -> p c t", p=128)
        for bi, (tok0, nt, b, is_ctx, pos0) in enumerate(token_blocks()):
            xm = xtm[bi % 2]
            xk = ("xtm", bi % 2)
            src = T["ctx2"][b] if is_ctx else T["x2"][b, pos0:pos0 + nt]
            nts = nt // 128
            P.dma("sp", xm[:, 0:nts, :], src.rearrange("(ts p) d -> p ts d", p=128), [], [xk])
            xo = xT[bi % 2]
            ok = ("xT", bi % 2)
            for c in range(KC):
                ps, pk = P.next_ps()
                for ts_ in range(nts):
                    P.tr(ps[:, ts_ * 128:(ts_ + 1) * 128], xm[:, ts_, c * 128:(c + 1) * 128], G["ident"][:], [xk], [pk])
                P.copy("act" if c % 2 == 0 else "dve", xo[:, c, :nt], ps[:, :nt], [pk], [ok])
            P.dma("act", XTv[:, :, tok0:tok0 + nt], xo[:, :, :nt], [ok], ["XT"])


def stage_epilogue(P, G, T):
    with Stage(P) as S:
        xT = [S.sb("xT%d" % i, [128, KC, 512]) for i in range(2)]
        yo = [S.sb("yo%d" % i, [128, 4, D]) for i in range(2)]
        XTv = T["XT"].rearrange("(c p) t -> p c t", p=128)
        for bi, (tok0, nt, b, is_ctx, pos0) in enumerate(token_blocks(with_ctx=False)):
            xi = xT[bi % 2]
            xk = ("xT", bi % 2)
            P.dma("sp", xi[:], XTv[:, :, tok0:tok0 + nt], ["XT"], [xk])
            y = yo[bi % 2]
            yk = ("yo", bi % 2)
            for ts_ in range(4):
                for half in range(2):
                    ps, pk = P.next_ps()
                    for cc in range(4):
                        c = half * 4 + cc
                        P.tr(ps[:, cc * 128:(cc + 1) * 128], xi[:, c, ts_ * 128:(ts_ + 1) * 128], G["ident"][:], [xk], [pk])
                    P.copy("act" if half == 0 else "dve", y[:, ts_, half * 512:(half + 1) * 512], ps[:, :], [pk], [yk])
            P.dma("act", T["y"][b, pos0:pos0 + nt].rearrange("(ts p) d -> p ts d", p=128), y[:], [yk], ["yout"])


def stage_attn_qkv(P, G, T, L, j):
    with Stage(P) as S:
        w = S.sb("wqkv", [128, KC, 1536], BF16)
        load_w(P, w, T["attn_w_qkv"][j], KC, "wqkv")
        cosT = S.sb("cosT", [128, LAT])
        sinT = S.sb("sinT", [128, LAT])
        P.dma("sp", cosT[:], T["c_cos"], [], ["cos"])
        P.dma("sp", sinT[:], T["c_sin"], [], ["sin"])
        xs = [S.sb("x%d" % i, [128, KC, 512]) for i in range(2)]
        h = S.sb("h", [128, KC, 512], BF16)
        sq = S.sb("sq", [128, KC, 512], BF16)
        rstd = S.sb("rstd", [128, 512])
        tmp = [S.sb("ntmp%d" % i, [128, 512]) for i in range(2)]
        qs = [S.sb("qs%d" % i, [128, 512]) for i in range(2)]
        q2 = [S.sb("q2%d" % i, [128, 512]) for i in range(2)]
        rs = [S.sb("rs%d" % i, [128, 512]) for i in range(2)]
        qn = [S.sb("qn%d" % i, [128, 512]) for i in range(2)]
        t1 = [S.sb("t1%d" % i, [128, 512]) for i in range(2)]
        t2 = [S.sb("t2%d" % i, [128, 512]) for i in range(2)]
        qo = [S.sb("qo%d" % i, [128, 512], BF16) for i in range(4)]
        vo = [S.sb("vo%d" % i, [128, 256], BF16) for i in range(2)]
        XTv = T["XT"].rearrange("(c p) t -> p c t", p=128)
        blks = token_blocks()
        P.dma("sp", xs[0][:, :, :blks[0][1]], XTv[:, :, blks[0][0]:blks[0][0] + blks[0][1]], ["XT"], [("x", 0)])
        it = 0
        vi = 0
        for bi, (tok0, nt, b, is_ctx, pos0) in enumerate(blks):
            if bi + 1 < len(blks):
                t0n, ntn = blks[bi + 1][0], blks[bi + 1][1]
                P.dma("sp", xs[(bi + 1) % 2][:, :, :ntn], XTv[:, :, t0n:t0n + ntn], ["XT"], [("x", (bi + 1) % 2)])
            n = 2 if is_ctx else b
            emit_norm(P, G, xs[bi % 2], ("x", bi % 2), nt, L, 1, n, h, "h", tmp, sq, rstd)
            def qk_chain(mo, i2, i3):
                isq = mo < 8
                ps, pk = P.next_ps()
                for kc in range(KC):
                    P.mm(ps[:, :nt], w[:, kc, mo * 128:(mo + 1) * 128], h[:, kc, :nt], kc == 0, kc == KC - 1, ["wqkv", "h"], [pk])
                P.copy("act", qs[i2][:, :nt], ps[:, :nt], [pk], [("qs", i2)])
                yield
                P.tt("dve", q2[i2][:, :nt], ps[:, :nt], qs[i2][:, :nt], ALU.mult, [pk, ("qs", i2)], [("q2", i2)])
                yield
                ps2, pk2 = P.next_ps()
                P.mm(ps2[:, :nt], G["bones"][:], q2[i2][:, :nt], True, True, [("q2", i2)], [pk2])
                P.act(rs[i2][:, :nt], ps2[:, :nt], AF.Ln, [pk2], [("rs", i2)], bias=G["eps_t"][:, 0:1], scale=1.0 / 64)
                yield
                P.act(rs[i2][:, :nt], rs[i2][:, :nt], AF.Exp, [("rs", i2)], [("rs", i2)], scale=-0.5)
                yield
                gcol = (VO["qg"] if isq else VO["kg"]) + j
                oq = qo[i3]
                okk = ("qo", i3)
                if is_ctx:
                    P.stt(oq[:, :nt], qs[i2][:, :nt], G["vec"][:, gcol:gcol + 1], rs[i2][:, :nt], ALU.mult, ALU.mult,
                          [("qs", i2), ("rs", i2)], [okk])
                else:
                    P.stt(qn[i2][:, :nt], qs[i2][:, :nt], G["vec"][:, gcol:gcol + 1], rs[i2][:, :nt], ALU.mult, ALU.mult,
                          [("qs", i2), ("rs", i2)], [("qn", i2)])
                    yield
                    ps3, pk3 = P.next_ps()
                    P.mm(ps3[:, :nt], G["rot"][:], qn[i2][:, :nt], True, True, [("qn", i2)], [pk3])
                    P.tt("pool", t1[i2][:, :nt], qn[i2][:, :nt], cosT[:, pos0:pos0 + nt], ALU.mult, [("qn", i2), "cos"], [("t1", i2)])
                    yield
                    P.tt("dve", t2[i2][:, :nt], ps3[:, :nt], sinT[:, pos0:pos0 + nt], ALU.mult, [pk3, "sin"], [("t2", i2)])
                    yield
                    P.tt("pool", oq[:, :nt], t1[i2][:, :nt], t2[i2][:, :nt], ALU.add, [("t1", i2), ("t2", i2)], [okk])
                yield
                if isq:
                    P.dma("sp", T["QT"][mo * 128:(mo + 1) * 128, tok0:tok0 + nt], oq[:, :nt], [okk], ["QT"])
                else:
                    P.dma("sp", T["KTs"][(mo - 8) * 128:(mo - 7) * 128, tok0:tok0 + nt], oq[:, :nt], [okk], ["KTs"])
                yield

            for mo0 in range(0, 10, 2):
                alive = [qk_chain(mo0 + i, i, (it + i) % 4) for i in range(2)]
                it += 2
                while alive:
                    nxt = []
                    for g_ in alive:
                        try:
                            next(g_)
                            nxt.append(g_)
                        except StopIteration:
                            pass
                    alive = nxt
            for ts_ in range(nt // 128):
                ps, pk = P.next_ps()
                for kc in range(KC):
                    P.mm(ps[:, 0:256], h[:, kc, ts_ * 128:(ts_ + 1) * 128], w[:, kc, 1280:1536], kc == 0, kc == KC - 1,
                         ["wqkv", "h"], [pk])
                v = vo[vi % 2]
                vk = ("vo", vi % 2)
                vi += 1
                P.copy("act", v[:, :], ps[:, 0:256], [pk], [vk])
                sc = (tok0 - b * SEG) // 128 + ts_
                P.dma("sp", T["Vs"][b, :, :, sc, :].rearrange("g s d -> s g d"), v[:, :].rearrange("s (g d) -> s g d", g=4),
                      [vk], ["Vs"])


def stage_attn_core(P, G, T, with_ctx_out, bg_layers=()):
    LOOK = 3
    with Stage(P) as S:
        kTs = [[S.sb("kT%d_%d" % (i, hh), [128, SEG], BF16) for hh in range(2)] for i in range(2)]
        for i in range(2):
            P.op("pool", lambda e: e.memset(kTs[i][0][64:128, :], 0.0), [], [("kT", i)])
            P.op("pool", lambda e: e.memset(kTs[i][1][0:64, :], 0.0), [], [("kT", i)])
        Vall = S.sb("Vall", [128, NSC, 4, 128], BF16)
        qt = [S.sb("qt%d" % i, [128, 512], BF16) for i in range(3)]
        pt = [S.sb("pt%d" % i, [128, 512], BF16) for i in range(6)]
        rl = [S.sb("rl%d" % i, [128, 512]) for i in range(2)]
        on = [S.sb("on%d" % i, [64, 512], BF16) for i in range(2)]
        P.op("pool", lambda e: e.memset(Vall[:, :, :, 64:128], 1.0), [], ["Vones"])
        pob = [[(P.ps_tiles[0], ("ps", 0)), (P.ps_tiles[1], ("ps", 1))], [(P.ps_tiles[2], ("ps", 2)), (P.ps_tiles[3], ("ps", 3))]]
        sb_ = [(P.ps_tiles[i], ("ps", i)) for i in range(4, 8)]
        work = []
        for b in range(NB):
            for g in range(4):
                for pair in range(2):
                    for (tok0, nt, bb, is_ctx, pos0) in token_blocks(with_ctx=with_ctx_out):
                        if bb == b:
                            work.append((b, g, pair, tok0, nt, is_ctx))
        kcur = {}
        cnt = {"s": 0, "p": 0, "o": 0, "k": 0}

        def load_q(wi):
            b, g, pair, tok0, nt, is_ctx = work[wi]
            qc = 2 * g + pair
            P.dma("sp", qt[wi % 3][:, :nt], T["QT"][qc * 128:(qc + 1) * 128, tok0:tok0 + nt], ["QT"], [("qt", wi % 3)])

        def load_kv(wi):
            b, g, pair, tok0, nt, is_ctx = work[wi]
            if (b, g) not in kcur:
                ki = cnt["k"] % 2
                cnt["k"] += 1
                kcur[(b, g)] = ki
                for hf in range(2):
                    P.dma("sp", kTs[ki][hf][hf * 64:(hf + 1) * 64, :], T["KTs"][g * 64:(g + 1) * 64, b * SEG:(b + 1) * SEG], ["KTs"], [("kT", ki)])

        bg = mod_steps(P, G, T, S, list(bg_layers)) if bg_layers else None
        load_kv(0)
        load_q(0)
        for wi, (b, g, pair, tok0, nt, is_ctx) in enumerate(work):
            if bg is not None and wi % 3 == 2:
                if next(bg, "done") == "done":
                    bg = None
            if ("v", b) not in kcur:
                kcur[("v", b)] = 1
                for g2 in range(4):
                    P.dma("sp", Vall[:, :, g2, 0:64], T["Vs"][b, g2], ["Vs"], ["Vall"])
            if wi + 1 < len(work):
                load_kv(wi + 1)
                load_q(wi + 1)
            ki = kcur[(b, g)]
            kT = kTs[ki]
            q = qt[wi % 3]
            qk = ("qt", wi % 3)
            qc = 2 * g + pair
            nsc = (CTXL // 128) if is_ctx else NSC
            po = pob[wi % 2]
            items = [(sc, hh) for sc in range(nsc) for hh in range(2)]
            sbank = {}

            def emit_S(i):
                sc, hh = items[i]
                ps, pk = sb_[cnt["s"] % 4]
                cnt["s"] += 1
                P.mm(ps[:, :nt], kT[hh][:, sc * 128:(sc + 1) * 128], q[:, :nt], True, True, [("kT", ki), qk], [pk])
                sbank[i] = (ps, pk)

            for i in range(min(LOOK, len(items))):
                emit_S(i)
            for i, (sc, hh) in enumerate(items):
                if i + LOOK < len(items):
                    emit_S(i + LOOK)
                ps, pk = sbank.pop(i)
                p_ = pt[cnt["p"] % 6]
                ptk = ("pt", cnt["p"] % 6)
                cnt["p"] += 1
                P.act(p_[:, :nt], ps[:, :nt], AF.Exp, [pk], [ptk], scale=0.125)
                P.mm(po[hh][0][:, :nt], Vall[:, sc, g, :], p_[:, :nt], sc == 0, sc == nsc - 1,
                     ["Vall", "Vones", ptk], [po[hh][1]])
            for hh in range(2):
                oi = cnt["o"] % 2
                cnt["o"] += 1
                r_ = rl[oi]
                o_ = on[oi]
                rk, ok = ("rl", oi), ("on", oi)
                P.op("dve", lambda e: e.reciprocal(out=r_[64:128, :nt], in_=po[hh][0][64:128, :nt]), [po[hh][1]], [rk])
                P.tt("dve", o_[:, :nt], po[hh][0][0:64, :nt], r_[64:128, :nt], ALU.mult, [po[hh][1], rk], [ok])
                hq = qc * 2 + hh
                P.dma("sp", T["AT"][hq * 64:(hq + 1) * 64, tok0:tok0 + nt], o_[:, :nt], [ok], ["AT"])
        if bg is not None:
            for _ in bg:
                pass


def stage_mixout(P, G, T, L, w_dram, kind, with_ctx):
    with Stage(P) as S:
        if kind == "full":
            w = S.sb("wo", [128, KC, D], BF16)
            load_w(P, w, w_dram, KC, "wo")
        else:
            w = S.sb("wo", [128, 8, 256], BF16)
            for g in range(4):
                for k2 in range(2):
                    P.dma("pool", w[:, g * 2 + k2, :], w_dram[g, k2 * 128:(k2 + 1) * 128, :], [], ["wo"])
            gs = S.sb("gs", [128, KC, 3])
            for n in range(3):
                P.tt("dve", gs[:, :, n], G["mod"][L][:, 16:24, n], G["vec"][:, VO["pscale"]:VO["pscale"] + 8], ALU.mult, ["mod"], ["gs"])
        xs = [S.sb("x%d" % i, [128, KC, 512]) for i in range(2)]
        at = [S.sb("a%d" % i, [128, KC, 512], BF16) for i in range(2)]
        XTv = T["XT"].rearrange("(c p) t -> p c t", p=128)
        ATv = T["AT"].rearrange("(c p) t -> p c t", p=128)
        for bi, (tok0, nt, b, is_ctx, pos0) in enumerate(token_blocks(with_ctx=with_ctx)):
            x = xs[bi % 2]
            a = at[bi % 2]
            xk, ak = ("x", bi % 2), ("a", bi % 2)
            P.dma("sp", x[:, :, :nt], XTv[:, :, tok0:tok0 + nt], ["XT"], [xk])
            P.dma("sp", a[:, :, :nt], ATv[:, :, tok0:tok0 + nt], ["AT"], [ak])
            n = 2 if is_ctx else b
            for mo in range(KC):
                ps, pk = P.next_ps()
                if kind == "full":
                    for kc in range(KC):
                        P.mm(ps[:, :nt], w[:, kc, mo * 128:(mo + 1) * 128], a[:, kc, :nt], kc == 0, kc == KC - 1, ["wo", ak], [pk])
                    gate = G["mod"][L][:, 16 + mo, n:n + 1]
                    gk = "mod"
                else:
                    g = mo // 2
                    for k2 in range(2):
                        P.mm(ps[:, :nt], w[:, g * 2 + k2, (mo % 2) * 128:(mo % 2 + 1) * 128], a[:, g * 2 + k2, :nt], k2 == 0, k2 == 1,
                             ["wo", ak], [pk])
                    gate = gs[:, mo, n:n + 1]
                    gk = "gs"
                P.stt(x[:, mo, :nt], ps[:, :nt], gate, x[:, mo, :nt], ALU.mult, ALU.add, [pk, xk, gk], [xk])
            P.dma("sp", XTv[:, :, tok0:tok0 + nt], x[:, :, :nt], [xk], ["XT"])


def stage_mix_mlp(P, G, T, L, w_dram, kind, with_ctx):
    with Stage(P) as S:
        if kind == "full":
            w = S.sb("wo", [128, KC, D], BF16)
            load_w(P, w, w_dram, KC, "wo")
        else:
            w = S.sb("wo", [128, 8, 256], BF16)
            for g in range(4):
                for k2 in range(2):
                    P.dma("pool", w[:, g * 2 + k2, :], w_dram[g, k2 * 128:(k2 + 1) * 128, :], [], ["wo"])
            gs = S.sb("gs", [128, KC, 3])
            for n in range(3):
                P.tt("dve", gs[:, :, n], G["mod"][L][:, 16:24, n], G["vec"][:, VO["pscale"]:VO["pscale"] + 8], ALU.mult, ["mod"], ["gs"])
        w1 = S.sb("w1", [128, KC, DFF], BF16)
        w2 = S.sb("w2", [128, 32, D], BF16)
        load_w(P, w1, T["mlp_w_in"][L], KC, "w1")
        load_w(P, w2, T["mlp_w_out"][L], 32, "w2")
        x = S.sb("x", [128, KC, 512])
        h = S.sb("h", [128, KC, 512], BF16)
        a = h
        u = S.sb("u", [128, 32, 512], BF16)
        XTv = T["XT"].rearrange("(c p) t -> p c t", p=128)
        ATv = T["AT"].rearrange("(c p) t -> p c t", p=128)
        ri = 0
        for bi, (tok0, nt, b, is_ctx, pos0) in enumerate(token_blocks(with_ctx=with_ctx)):
            for c in range(KC):
                P.dma("sp", x[:, c, :nt], XTv[:, c, tok0:tok0 + nt], [("XT", c)], [("x", c)])
            P.dma("sp", a[:, :, :nt], ATv[:, :, tok0:tok0 + nt], ["AT"], ["h"])
            n = 2 if is_ctx else b
            for mo in range(KC):
                ps, pk = P.next_ps()
                if kind == "full":
                    for kc in range(KC):
                        P.mm(ps[:, :nt], w[:, kc, mo * 128:(mo + 1) * 128], a[:, kc, :nt], kc == 0, kc == KC - 1, ["wo", "h"], [pk])
                    gate = G["mod"][L][:, 16 + mo, n:n + 1]
                    gk = "mod"
                else:
                    g = mo // 2
                    for k2 in range(2):
                        P.mm(ps[:, :nt], w[:, g * 2 + k2, (mo % 2) * 128:(mo % 2 + 1) * 128], a[:, g * 2 + k2, :nt], k2 == 0, k2 == 1,
                             ["wo", "h"], [pk])
                    gate = gs[:, mo, n:n + 1]
                    gk = "gs"
                P.stt(x[:, mo, :nt], ps[:, :nt], gate, x[:, mo, :nt], ALU.mult, ALU.add, [pk, ("x", mo), gk], [("x", mo)])
            emit_norm(P, G, x, [("x", c) for c in range(KC)], nt, L, 2, n, h, "h", None, u, None, sqkeys=[("u", c) for c in range(KC)])
            for mo in range(32):
                ps, pk = P.next_ps()
                for kc in range(KC):
                    P.mm(ps[:, :nt], w1[:, kc, mo * 128:(mo + 1) * 128], h[:, kc, :nt], kc == 0, kc == KC - 1, ["w1", "h"], [pk])
                P.act(ps[:, :nt], ps[:, :nt], AF.Relu, [pk], [pk])
                P.act(u[:, mo, :nt], ps[:, :nt], AF.Square, [pk], [("u", mo)])
            for mo in range(KC):
                ps, pk = P.next_ps()
                for kc in range(32):
                    P.mm(ps[:, :nt], w2[:, kc, mo * 128:(mo + 1) * 128], u[:, kc, :nt], kc == 0, kc == 31, ["w2", ("u", kc)], [pk])
                P.stt(x[:, mo, :nt], ps[:, :nt], G["mod"][L][:, 40 + mo, n:n + 1], x[:, mo, :nt], ALU.mult, ALU.add,
                      [pk, ("x", mo), ("mod", L)], [("x", mo)])
                P.dma("act", XTv[:, mo, tok0:tok0 + nt], x[:, mo, :nt], [("x", mo)], [("XT", mo)])


def stage_mlp(P, G, T, L, with_ctx):
    with Stage(P) as S:
        w1 = S.sb("w1", [128, KC, DFF], BF16)
        w2 = S.sb("w2", [128, 32, D], BF16)
        load_w(P, w1, T["mlp_w_in"][L], KC, "w1")
        load_w(P, w2, T["mlp_w_out"][L], 32, "w2")
        x = S.sb("x", [128, KC, 512])
        h = S.sb("h", [128, KC, 512], BF16)
        u = S.sb("u", [128, 32, 512], BF16)
        sq = S.sb("sq", [128, KC, 512], BF16)
        rstd = S.sb("rstd", [128, 512])
        tmp = [S.sb("ntmp%d" % i, [128, 512]) for i in range(2)]
        rr = tmp
        XTv = T["XT"].rearrange("(c p) t -> p c t", p=128)
        ri = 0
        for bi, (tok0, nt, b, is_ctx, pos0) in enumerate(token_blocks(with_ctx=with_ctx)):
            P.dma("sp", x[:, :, :nt], XTv[:, :, tok0:tok0 + nt], ["XT"], ["x"])
            n = 2 if is_ctx else b
            emit_norm(P, G, x, "x", nt, L, 2, n, h, "h", tmp, sq, rstd)
            for mo in range(32):
                ps, pk = P.next_ps()
                for kc in range(KC):
                    P.mm(ps[:, :nt], w1[:, kc, mo * 128:(mo + 1) * 128], h[:, kc, :nt], kc == 0, kc == KC - 1, ["w1", "h"], [pk])
                r_ = rr[ri % 2]
                rk = ("ntmp", ri % 2)
                ri += 1
                P.act(r_[:, :nt], ps[:, :nt], AF.Relu, [pk], [rk])
                P.tt("pool" if mo % 4 != 3 else "dve", u[:, mo, :nt], r_[:, :nt], r_[:, :nt], ALU.mult, [rk], [("u", mo)])
            for mo in range(KC):
                ps, pk = P.next_ps()
                for kc in range(32):
                    P.mm(ps[:, :nt], w2[:, kc, mo * 128:(mo + 1) * 128], u[:, kc, :nt], kc == 0, kc == 31, ["w2", ("u", kc)], [pk])
                P.stt(x[:, mo, :nt], ps[:, :nt], G["mod"][L][:, 40 + mo, n:n + 1], x[:, mo, :nt], ALU.mult, ALU.add,
                      [pk, "x", ("mod", L)], ["x"])
            P.dma("sp", XTv[:, :, tok0:tok0 + nt], x[:, :, :nt], ["x"], ["XT"])


def stage_norm_to_HT(P, G, T, L, with_ctx):
    with Stage(P) as S:
        xs = [S.sb("x%d" % i, [128, KC, 512]) for i in range(2)]
        hs = [S.sb("h%d" % i, [128, KC, 512]) for i in range(2)]
        sq = S.sb("sq", [128, KC, 512], BF16)
        rstd = S.sb("rstd", [128, 512])
        tmp = [S.sb("ntmp%d" % i, [128, 512]) for i in range(2)]
        XTv = T["XT"].rearrange("(c p) t -> p c t", p=128)
        HTv = T["HT"].rearrange("(c p) t -> p c t", p=128)
        for bi, (tok0, nt, b, is_ctx, pos0) in enumerate(token_blocks(with_ctx=with_ctx)):
            x, h = xs[bi % 2], hs[bi % 2]
            xk, hk = ("x", bi % 2), ("h", bi % 2)
            P.dma("sp", x[:, :, :nt], XTv[:, :, tok0:tok0 + nt], ["XT"], [xk])
            emit_norm(P, G, x, xk, nt, L, 1, 2 if is_ctx else b, h, hk, tmp, sq, rstd)
            P.dma("act", HTv[:, :, tok0:tok0 + nt], h[:, :, :nt], [hk], ["HT"])


def stage_pool(P, G, T, with_ctx):
    with Stage(P) as S:
        PADL = 8
        hb = [S.sb("hb%d" % i, [128, LAT + 24]) for i in range(2)]
        d2 = S.sb("d2", [128, LAT + 24])
        d4 = S.sb("d4", [128, LAT + 24])
        rc = S.sb("rc", [128, 4, LAT])
        rcc = S.sb("rcc", [128, 4, CTXL])
        pm = [S.sb("pm%d" % i, [128, LAT]) for i in range(2)]
        po = [S.sb("po%d" % i, [128, LAT], BF16) for i in range(2)]
        P.dma("sp", rc[:], T["c_rcnt"], [], ["rc"])
        P.dma("sp", rcc[:], T["c_rcntc"], [], ["rc"])
        for i in range(2):
            P.op("pool", lambda e: e.memset(hb[i][:], 0.0), [], [("hb", i)])
        it = 0
        segs = []
        for b in range(NB):
            if with_ctx:
                segs.append((b * SEG, CTXL, True))
            segs.append((b * SEG + CTXL, LAT, False))
        for (tok0, Tn, is_ctx) in segs:
            for c in range(KC):
                gi = c // 2
                win = (2, 4, 8, 16)[gi]
                i2 = it % 2
                it += 1
                hbt = hb[i2]
                hk = ("hb", i2)
                if Tn < LAT:
                    P.op("pool", lambda e: e.memset(hbt[:, PADL + Tn:PADL + Tn + 16], 0.0), [], [hk])
                P.dma("sp", hbt[:, PADL:PADL + Tn], T["HT"][c * 128:(c + 1) * 128, tok0:tok0 + Tn], ["HT"], [hk])
                W_ = Tn + 16
                src = hbt
                sk = hk
                cur = 1
                bufs = [(d2, "d2"), (d4, "d4")]
                bi_ = 0
                while cur < win:
                    dst, dk = bufs[bi_ % 2]
                    bi_ += 1
                    P.tt("dve" if cur in (1, 4) else "pool", dst[:, 0:W_ - cur], src[:, 0:W_ - cur], src[:, cur:W_], ALU.add, [sk], [dk])
                    src, sk = dst, dk
                    cur *= 2
                off = PADL - win // 2
                rct = rcc if is_ctx else rc
                pmt, pot = pm[i2], po[i2]
                P.tt("dve", pmt[:, :Tn], src[:, off:off + Tn], rct[:, gi, :Tn], ALU.mult, [sk, "rc"], [("pm", i2)])
                P.tt("pool", pot[:, :Tn], pmt[:, :Tn], hbt[:, PADL:PADL + Tn], ALU.subtract, [("pm", i2), hk], [("po", i2)])
                P.dma("sp", T["AT"][c * 128:(c + 1) * 128, tok0:tok0 + Tn], pot[:, :Tn], [("po", i2)], ["AT"])


def stage_rwkv_prep(P, G, T, with_ctx=True):
    with Stage(P) as S:
        wr = [S.sb("wrkv%d" % i, [128, KC, D], BF16) for i in range(3)]
        for i in range(3):
            load_w(P, wr[i], T["rwkv_w_rkv"][0, i], KC, "wrkv")
        wg1 = S.sb("wg1", [128, KC, 160], BF16)
        load_w(P, wg1, T["rwkv_g1"][0], KC, "wl")
        wg2a = S.sb("wg2a", [128, D], BF16)
        wg2b = S.sb("wg2b", [32, D], BF16)
        P.dma("pool", wg2a[:], T["rwkv_g2"][0, 0:128, :], [], ["wl"])
        P.dma("pool", wg2b[:], T["rwkv_g2"][0, 128:160, :], [], ["wl"])
        w1 = S.sb("w1", [128, KC, 2, 128], BF16)
        for d in range(2):
            for kc in range(KC):
                P.dma("pool", w1[:, kc, d, 0:64], T["rwkv_w1"][0, d, kc * 128:(kc + 1) * 128, :], [], ["wl"])
                P.dma("pool", w1[:, kc, d, 64:128], T["rwkv_a1"][0, d, kc * 128:(kc + 1) * 128, :], [], ["wl"])
        w2 = S.sb("w2", [64, 2, D], BF16)
        a2 = S.sb("a2", [64, 2, D], BF16)
        for d in range(2):
            P.dma("pool", w2[:, d, :], T["rwkv_w2"][0, d], [], ["wl"])
            P.dma("pool", a2[:, d, :], T["rwkv_a2"][0, d], [], ["wl"])
        hb = S.sb("hb", [128, KC, 514])
        xx = S.sb("xx", [128, KC, 512])
        xms = [S.sb("xm%d" % i, [128, 6, KC, 512], BF16) for i in range(1)]
        lo = S.sb("lo", [128, 2, 512], BF16)
        la = S.sb("la", [128, 2, 512], BF16)
        lg = S.sb("lg", [128, 2, 512], BF16)
        ev = [S.sb("ev%d" % i, [128, 512]) for i in range(4)]
        HTv = T["HT"].rearrange("(c p) t -> p c t", p=128)
        ei = 0
        for bi, (tok0, nt, b, is_ctx, pos0) in enumerate(token_blocks(with_ctx=with_ctx)):
            xm = xms[0]
            xb = 0
            seg0 = b * SEG if is_ctx else b * SEG + CTXL
            segn = CTXL if is_ctx else LAT
            first = tok0 == seg0
            last = tok0 + nt == seg0 + segn
            lo_ = 0 if first else -1
            hi_ = 0 if last else 1
            if first:
                P.op("pool", lambda e: e.memset(hb[:, :, 0:1], 0.0), [], ["hb"])
            if last:
                P.op("pool", lambda e: e.memset(hb[:, :, nt + 1:nt + 2], 0.0), [], ["hb"])
            P.dma("sp", hb[:, :, 1 + lo_:1 + nt + hi_], HTv[:, :, tok0 + lo_:tok0 + nt + hi_], ["HT"], ["hb"])
            P.tt("pool", xx[:, :, :nt], hb[:, :, 0:nt], hb[:, :, 2:nt + 2], ALU.add, ["hb"], ["xx"])
            P.stt(xx[:, :, :nt], xx[:, :, :nt], 0.5, hb[:, :, 1:nt + 1], ALU.mult, ALU.subtract, ["xx", "hb"], ["xx"])
            for m in range(6):
                for c in range(KC):
                    P.stt(xm[:, m, c, :nt], xx[:, c, :nt], G["vec"][:, VO["mu"] + m * 8 + c:VO["mu"] + m * 8 + c + 1], hb[:, c, 1:nt + 1],
                          ALU.mult, ALU.add, ["xx", "hb"], [("xm", xb, m)])
            for i, (m, dst) in enumerate([(0, "RT"), (2, "KT"), (3, "VT")]):
                for mo in range(KC):
                    ps, pk = P.next_ps()
                    for kc in range(KC):
                        P.mm(ps[:, :nt], wr[i][:, kc, mo * 128:(mo + 1) * 128], xm[:, m, kc, :nt], kc == 0, kc == KC - 1,
                             ["wrkv", ("xm", xb, m)], [pk])
                    e_ = ev[ei % 4]
                    ek = ("ev", ei % 4)
                    ei += 1
                    P.copy("act" if mo % 2 == 0 else "dve", e_[:, :nt], ps[:, :nt], [pk], [ek])
                    P.dma("act", T[dst][mo * 128:(mo + 1) * 128, tok0:tok0 + nt], e_[:, :nt], [ek], [dst])
            for part, (o0, osz) in enumerate([(0, 128), (128, 32)]):
                ps, pk = P.next_ps()
                for kc in range(KC):
                    P.mm(ps[0:osz, :nt], wg1[:, kc, o0:o0 + osz], xm[:, 5, kc, :nt], kc == 0, kc == KC - 1, ["wl", ("xm", xb, 5)], [pk])
                P.act(lg[0:osz, part, :nt], ps[0:osz, :nt], AF.Sigmoid, [pk], ["lg"])
            for mo in range(KC):
                ps, pk = P.next_ps()
                P.mm(ps[:, :nt], wg2a[:, mo * 128:(mo + 1) * 128], lg[:, 0, :nt], True, False, ["wl", "lg"], [pk])
                P.mm(ps[:, :nt], wg2b[:, mo * 128:(mo + 1) * 128], lg[0:32, 1, :nt], False, True, ["wl", "lg"], [pk])
                e_ = ev[ei % 4]
                ek = ("ev", ei % 4)
                ei += 1
                P.copy("act" if mo % 2 == 0 else "dve", e_[:, :nt], ps[:, :nt], [pk], [ek])
                P.dma("act", T["GT"][mo * 128:(mo + 1) * 128, tok0:tok0 + nt], e_[:, :nt], [ek], ["GT"])
            for d in range(2):
                ps, pk = P.next_ps()
                for kc in range(KC):
                    P.mm(ps[0:64, :nt], w1[:, kc, d, 0:64], xm[:, 1, kc, :nt], kc == 0, kc == KC - 1, ["wl", ("xm", xb, 1)], [pk])
                P.act(lo[0:64, d, :nt], ps[0:64, :nt], AF.Tanh, [pk], ["lo"])
                ps, pk = P.next_ps()
                for kc in range(KC):
                    P.mm(ps[0:64, :nt], w1[:, kc, d, 64:128], xm[:, 4, kc, :nt], kc == 0, kc == KC - 1, ["wl", ("xm", xb, 4)], [pk])
                P.copy("dve", la[0:64, d, :nt], ps[0:64, :nt], [pk], ["la"])
                for (wsb, src, skey, vofs, dst) in [(w2, lo, "lo", VO["w0"], "SIG%d" % d), (a2, la, "la", VO["a0"], "AA%d" % d)]:
                    for mo in range(KC):
                        ps, pk = P.next_ps()
                        P.mm(ps[:, :nt], wsb[:, d, mo * 128:(mo + 1) * 128], src[0:64, d, :nt], True, True, ["wl", skey], [pk])
                        e_ = ev[ei % 4]
                        ek = ("ev", ei % 4)
                        ei += 1
                        P.act(e_[:, :nt], ps[:, :nt], AF.Sigmoid, [pk], [ek], bias=G["vec"][:, vofs + d * 8 + mo:vofs + d * 8 + mo + 1], scale=1.0)
                        P.dma("act", T[dst][mo * 128:(mo + 1) * 128, tok0:tok0 + nt], e_[:, :nt], [ek], [dst])


F32R = mybir.dt.float32r
SCAN_MODE = "f32"
SCAN_BF16 = True
SCAN_GH = 8


def _mo(ap):
    return ap.bitcast(F32R) if SCAN_MODE == "f32r" else ap


def stage_rwkv_scan(P, G, T):
    CH = 128
    NH = 16
    GH = SCAN_GH
    with Stage(P) as S:
        def big(name):
            return S.sb(name, [64, NH, CH])
        _base = [big("%s0" % n) for n in ("r_t", "k_t", "v_t", "s_t", "a_t")]
        ld = [_base, [_base[0], _base[1], big("v_t1"), _base[3], _base[4]]]

        def ldkey(li, ti):
            return ("ld", li if ti == 2 else 0, ti)
        tA, tB, tC, tD, tE, tF, tG = big("tA"), big("tB"), big("tC"), big("tD"), big("tE"), big("tF"), big("tG")
        tH = big("tH")
        MD = BF16 if SCAN_BF16 else F32
        QR = S.sb("QR", [64, NH, 2 * CH], MD)
        Hst = S.sb("Hst", [64, NH, 64])
        if SCAN_BF16:
            HstM = S.sb("HstM", [64, NH, 64], MD)
            kinvM = S.sb("kinvM", [64, NH, CH], MD)
            binvM = S.sb("binvM", [64, NH, CH], MD)
        else:
            HstM = Hst
        kkv = S.sb("kkv", [64, 3, NH])
        P.dma("sp", kkv[:, 0:2, :], T["c_v64"], [], ["kkv"])
        P.ts("dve", kkv[:, 2, :], kkv[:, 1, :], -1.0, 1.0, ALU.mult, ALU.add, ["kkv"], ["kkv"])
        tot = S.sb("tot", [64, NH])
        ones_s = S.sb("ones_s", [64, CH])
        P.op("pool", lambda e: e.memset(ones_s[:], 1.0), [], ["ones_s"])
        mk = S.sb("mk", [128, 2, 2 * CH])
        mkT = S.sb("mkT", [128, 2, CH])
        P.dma("sp", mk[:], T["c_mask"], [], ["mk"])
        P.dma("sp", mkT[:], T["c_maskT"], [], ["mk"])

        def hb(name, shape):
            return [S.sb("%s%d" % (name, i), shape) for i in range(GH)]
        def hbm(name, shape):
            return [S.sb("%s%d" % (name, i), shape, MD) for i in range(GH)]
        vT, kdT, bdT = hbm("vT", [128, 64]), hbm("kdT", [128, 64]), hbm("bdT", [128, 64])
        Nn, ABb, AK = hb("Nn", [128, CH]), hbm("ABb", [128, CH]), hbm("AK", [128, 2 * CH])
        X = [hb("Xa", [128, CH]), hb("Xb", [128, CH])]
        XT_ = [hb("XTa", [128, CH]), hb("XTb", [128, CH])]
        Rr = [hb("Ra", [128, 64]), hb("Rb", [128, 64])]
        negU = hbm("negU", [128, 64])
        yt = hb("yt", [64, CH])
        cdec = DEC_C
        evi = [0]

        def ev_eng():
            evi[0] += 1
            return "dve" if evi[0] % 3 == 0 else "act"

        def bc(ap2):
            return ap2.unsqueeze(2).broadcast_to([64, NH, CH])

        def view(Tn):
            return T[Tn].rearrange("(h k) t -> k h t", k=64)

        seq = []
        for b in range(NB):
            for d in range(2):
                order = [0, 1] + list(range(2, NSC)) if d == 0 else [1, 0] + list(range(NSC - 1, 1, -1))
                for oi_, sc in enumerate(order):
                    seq.append((b, d, sc, oi_ == 0))

        def issue_loads(si):
            b, d, sc, first = seq[si]
            tok0 = b * SEG + sc * CH
            li = si % 2
            for ti, nm in enumerate(["RT", "KT", "VT", "SIG%d" % d, "AA%d" % d]):
                P.dma("sp", ld[li][ti][:], view(nm)[:, :, tok0:tok0 + CH], [nm], [ldkey(li, ti)])

        tI = big("tI")
        wcs = [S.sb("wc%d" % i, [64, NH]) for i in range(2)]
        kiK, biK = "kinvM", "binvM"
        assert SCAN_BF16

        def prep_early(si):
            b, d, sc, first = seq[si]
            li = si % 2
            r_t, k_t, v_t, s_t, a_t = ld[li]
            kr, kk_, kv, ks, ka = [ldkey(li, ti) for ti in range(5)]
            lastc = CH - 1 if d == 0 else 0
            P.tt("pool", tA[:], k_t[:], bc(kkv[:, 0, :]), ALU.mult, [kk_, "kkv"], ["tA"])
            yield
            P.act(tH[:], tA[:], AF.Square, ["tA"], ["tH"])
            yield
            for q4 in range(4):
                ps, pk = P.next_ps()
                P.mm(ps[0:64, :], G["ones_f"][0:64, 0:64], tH[:, q4 * 4:(q4 + 1) * 4, :], True, True, ["tH"], [pk])
                P.act(tH[:, q4 * 4:(q4 + 1) * 4, :], ps[0:64, :], AF.Ln, [pk], ["tH"], bias=G["tiny_t"][0:64, 0:1], scale=1.0)
                yield
            P.act(tH[:], tH[:], AF.Exp, ["tH"], ["tH"], scale=-0.5)
            yield
            P.tt("dve", tA[:], tA[:], tH[:], ALU.mult, ["tA", "tH"], ["tA"])
            P.tt("pool", tC[:], a_t[:], bc(kkv[:, 1, :]), ALU.mult, [ka, "kkv"], ["tC"])
            yield
            P.tt("pool", tC[:], tC[:], bc(kkv[:, 2, :]), ALU.add, ["tC", "kkv"], ["tC"])
            yield
            P.tt("pool", tC[:], tC[:], k_t[:], ALU.mult, ["tC", kk_], ["tC"])
            P.tt("dve", tD[:], tA[:], a_t[:], ALU.mult, ["tA", ka], ["tD"])
            yield
            for h in range(NH):
                P.op("dve", lambda e: e.tensor_tensor_scan(out=tI[:, h, :], data0=ones_s[:], data1=s_t[:, h, :], initial=0.0,
                                                           op0=ALU.mult, op1=ALU.add), ["ones_s", ks], ["tI"])
                if h % 4 == 3:
                    yield
            if d == 1:
                P.copy("pool", tot[:], tI[:, :, CH - 1], ["tI"], ["tot"])
                P.tt("pool", tF[:], s_t[:], tI[:], ALU.subtract, [ks, "tI"], ["tF"])
                yield
                P.tt("pool", tI[:], tF[:], bc(tot[:]), ALU.add, ["tF", "tot"], ["tI"])
                yield
            P.tt("pool", tF[:], tI[:], s_t[:], ALU.subtract, ["tI", ks], ["tF"])
            P.act(tG[:], tI[:], AF.Exp, ["tI"], ["tG"], scale=-cdec)
            yield
            P.act(tF[:], tF[:], AF.Exp, ["tF"], ["tF"], scale=-cdec)
            P.act(tI[:], tI[:], AF.Exp, ["tI"], ["tI"], scale=cdec)
            P.copy("pool", wcs[li][:], tG[:, :, lastc], ["tG"], [("wc", li)])
            yield
            P.tt("dve", tC[:], tC[:], tI[:], ALU.mult, ["tC", "tI"], ["tC"])
            P.tt("pool", tD[:], tD[:], tI[:], ALU.mult, ["tD", "tI"], ["tD"])
            yield

        def prep_late(si):
            b, d, sc, first = seq[si]
            li = si % 2
            r_t = ld[li][0]
            kr = ldkey(li, 0)
            if first:
                P.op("pool", lambda e: e.memset(Hst[:], 0.0), [], [("Hst", h_) for h_ in range(NH)])
                P.op("pool", lambda e: e.memset(HstM[:], 0.0), [], [("HstM", h_) for h_ in range(NH)])
            P.tt("dve", QR[:, :, 0:CH], tA[:], tF[:], ALU.mult, ["tA", "tF"], ["QR"])
            P.tt("pool", QR[:, :, CH:2 * CH], r_t[:], tG[:], ALU.mult, [kr, "tG"], ["QR"])
            P.copy("act", binvM[:], tD[:], ["tD"], ["binvM"])
            P.copy("act", kinvM[:], tC[:], ["tC"], ["kinvM"])
            P.tt("dve", tB[:], tC[:], bc(wcs[li][:]), ALU.mult, ["tC", ("wc", li)], ["tB"])
            P.tt("pool", tE[:], tD[:], bc(wcs[li][:]), ALU.mult, ["tD", ("wc", li)], ["tE"])

        def run_all(gens):
            alive = list(gens)
            while alive:
                nxt = []
                for g_ in alive:
                    try:
                        next(g_)
                        nxt.append(g_)
                    except StopIteration:
                        pass
                alive = nxt

        issue_loads(0)
        run_all([prep_early(0)])
        for si, (b, d, sc, first) in enumerate(seq):
            prep_late(si)
            if si + 1 < len(seq):
                issue_loads(si + 1)
                pe_gen = prep_early(si + 1)
            else:
                pe_gen = None
            li = si % 2
            v_t = ld[li][2]
            kv = ldkey(li, 2)
            tok0 = b * SEG + sc * CH
            lastc = CH - 1 if d == 0 else 0
            kdec, bdec = tB, tE
            wc_t, wck = wcs[li], ("wc", li)

            def head(h, i2, d=d, tok0=tok0, v_t=v_t, kv=kv, kdec=kdec, bdec=bdec, wc_t=wc_t, wck=wck):
                def K(n):
                    return (n, i2)
                for (src, skey, dst, dkey) in [(v_t, kv, vT, "vT"), (kdec, "tB", kdT, "kdT"), (bdec, "tE", bdT, "bdT")]:
                    ps, pk = P.next_ps()
                    P.tr(ps[:, 0:64], src[:, h, :], G["ident"][0:64, 0:64], [skey], [pk])
                    P.copy(ev_eng(), dst[i2][:], ps[:, 0:64], [pk], [K(dkey)])
                yield
                ps, pk = P.next_ps()
                P.mm(ps[:, 0:2 * CH], binvM[:, h, :], QR[:, h, :], True, True, [biK, "QR"], [pk])
                P.tt("dve", Nn[i2][:], ps[:, 0:CH], mk[:, d, 0:CH], ALU.mult, [pk, "mk"], [K("Nn")])
                P.tt("dve", ABb[i2][:], ps[:, CH:2 * CH], mk[:, d, CH:2 * CH], ALU.mult, [pk, "mk"], [K("ABb")])
                ps, pk = P.next_ps()
                P.mm(ps[:, 0:2 * CH], kinvM[:, h, :], QR[:, h, :], True, True, [kiK, "QR"], [pk])
                P.tt("dve", AK[i2][:], ps[:, 0:2 * CH], mk[:, d, :], ALU.mult, [pk, "mk"], [K("AK")])
                ps, pk = P.next_ps()
                P.mm(ps[:, 0:CH], QR[:, h, 0:CH], binvM[:, h, :], True, True, [biK, "QR"], [pk])
                P.tt("dve", XT_[0][i2][:], ps[:, 0:CH], mkT[:, d, :], ALU.mult, [pk, "mk"], [K("XT0")])
                yield
                ps, pk = P.next_ps()
                P.mm(ps[:, 0:64], QR[:, h, 0:CH], HstM[:, h, :], True, False, ["QR", ("HstM", h)], [pk])
                P.mm(ps[:, 0:64], AK[i2][:, 0:CH], vT[i2][:], False, True, [K("AK"), K("vT")], [pk])
                P.copy(ev_eng(), Rr[0][i2][:], ps[:, 0:64], [pk], [K("R0")])
                yield
                ps, pk = P.next_ps()
                P.mm(ps[:, 0:64], Nn[i2][:], Rr[0][i2][:], True, True, [K("Nn"), K("R0")], [pk])
                P.tt("dve", Rr[1][i2][:], Rr[0][i2][:], ps[:, 0:64], ALU.subtract, [pk, K("R0")], [K("R1")])
                rc = 1
                Xc, XTc = Nn[i2][:], XT_[0][i2][:]
                xck, xtck = K("Nn"), K("XT0")
                for step in range(1, 7):
                    nx = step % 2
                    lastst = step == 6
                    ps, pk = P.next_ps()
                    P.mm(ps[:, 0:CH], XTc, Xc, True, True, [xck, xtck], [pk])
                    P.copy(ev_eng(), X[nx][i2][:], ps[:, 0:CH], [pk], [K("X%d" % nx)])
                    yield
                    if not lastst:
                        ps, pk = P.next_ps()
                        P.tr(ps[:, 0:CH], X[nx][i2][:], G["ident"][:], [K("X%d" % nx)], [pk])
                        P.copy(ev_eng(), XT_[nx][i2][:], ps[:, 0:CH], [pk], [K("XT%d" % nx)])
                    ps, pk = P.next_ps()
                    P.mm(ps[:, 0:64], X[nx][i2][:], Rr[rc][i2][:], True, True, [K("X%d" % nx), K("R%d" % rc)], [pk])
                    if not lastst:
                        P.tt("dve", Rr[1 - rc][i2][:], ps[:, 0:64], Rr[rc][i2][:], ALU.add, [pk, K("R%d" % rc)], [K("R%d" % (1 - rc))])
                    else:
                        P.stt(negU[i2][:], ps[:, 0:64], -1.0, Rr[rc][i2][:], ALU.mult, ALU.subtract, [pk, K("R%d" % rc)], [K("negU")])
                    rc = 1 - rc
                    Xc, XTc = X[nx][i2][:], XT_[nx][i2][:]
                    xck, xtck = K("X%d" % nx), K("XT%d" % nx)
                    yield
                ps, pk = P.next_ps()
                P.mm(ps[0:64, 0:CH], HstM[:, h, :], QR[:, h, CH:2 * CH], True, False, ["QR", ("HstM", h)], [pk])
                P.mm(ps[0:64, 0:CH], vT[i2][:], AK[i2][:, CH:2 * CH], False, False, [K("AK"), K("vT")], [pk])
                P.mm(ps[0:64, 0:CH], negU[i2][:], ABb[i2][:], False, True, [K("ABb"), K("negU")], [pk])
                P.copy(ev_eng(), yt[i2][:], ps[0:64, 0:CH], [pk], [K("yt")])
                P.dma("sp", T["YT%d" % d][h * 64:(h + 1) * 64, tok0:tok0 + CH], yt[i2][:], [K("yt")], ["YT%d" % d])
                ps, pk = P.next_ps()
                P.mm(ps[0:64, 0:64], kdT[i2][:], vT[i2][:], True, False, [K("kdT"), K("vT")], [pk])
                P.mm(ps[0:64, 0:64], bdT[i2][:], negU[i2][:], False, True, [K("bdT"), K("negU")], [pk])
                P.stt(Hst[:, h, :], Hst[:, h, :], wc_t[:, h:h + 1], ps[0:64, 0:64], ALU.mult, ALU.add,
                      [("Hst", h), wck, pk], [("Hst", h)])
                if MD is not F32:
                    P.copy("pool", HstM[:, h, :], Hst[:, h, :], [("Hst", h)], [("HstM", h)])
                yield

            for h0 in range(0, NH, GH):
                gens = [head(h0 + i, i) for i in range(GH)]
                if pe_gen is not None:
                    gens.append(pe_gen)
                alive = list(gens)
                while alive:
                    nxt = []
                    for g_ in alive:
                        try:
                            next(g_)
                            if g_ is pe_gen and len(alive) == 1 and h0 + GH < NH:
                                nxt.append(g_)
                                break
                            nxt.append(g_)
                        except StopIteration:
                            if g_ is pe_gen:
                                pe_gen = None
                    if len(nxt) == 1 and nxt[0] is pe_gen and h0 + GH < NH:
                        break
                    alive = nxt
            if pe_gen is not None:
                run_all([pe_gen])


def stage_rwkv_finish(P, G, T, with_ctx=True):
    NI = 3
    with Stage(P) as S:
        def t5(name, dt=F32):
            return [S.sb("%s%d" % (name, i), [128, 512], dt) for i in range(NI)]
        y0, y1, rr, kk_, vv, gg, a0, a1 = t5("y0"), t5("y1"), t5("r"), t5("k"), t5("v"), t5("g"), t5("a0"), t5("a1")
        ysq, mean, var, tmp, bon = t5("ysq"), t5("mean"), t5("var"), t5("tmp"), t5("bon")
        ao = t5("ao", BF16)

        def body(i2, tok0, nt, c):
            def K(n):
                return (n, i2)
            sl = (slice(c * 128, (c + 1) * 128), slice(tok0, tok0 + nt))
            for (tl, nm) in [(y0, "YT0"), (y1, "YT1"), (rr, "RT"), (kk_, "KT"), (vv, "VT"), (gg, "GT"), (a0, "AA0"), (a1, "AA1")]:
                P.dma("sp", tl[i2][:, :nt], T[nm][sl[0], sl[1]], [nm], [K(nm)])
            yield
            y = y0[i2]
            P.tt("pool", y[:, :nt], y0[i2][:, :nt], y1[i2][:, :nt], ALU.add, [K("YT0"), K("YT1")], [K("YT0")])
            P.tt("pool", tmp[i2][:, :nt], a0[i2][:, :nt], a1[i2][:, :nt], ALU.add, [K("AA0"), K("AA1")], [K("tmp")])
            yield
            ps, pk = P.next_ps()
            P.mm(ps[:, :nt], G["bones"][:], y[:, :nt], True, True, [K("YT0")], [pk])
            P.act(mean[i2][:, :nt], ps[:, :nt], AF.Copy, [pk], [K("mean")], scale=1.0 / 64)
            P.ts("dve", tmp[i2][:, :nt], tmp[i2][:, :nt], -2.0, G["vec128_ka"][:, c:c + 1], ALU.add, ALU.mult, [K("tmp")], [K("tmp")])
            yield
            P.tt("dve", y[:, :nt], y[:, :nt], mean[i2][:, :nt], ALU.subtract, [K("YT0"), K("mean")], [K("YT0")])
            P.act(ysq[i2][:, :nt], y[:, :nt], AF.Square, [K("YT0")], [K("ysq")])
            P.stt(tmp[i2][:, :nt], tmp[i2][:, :nt], 2.0, kk_[i2][:, :nt], ALU.add, ALU.mult, [K("tmp"), K("KT")], [K("tmp")])
            yield
            ps, pk = P.next_ps()
            P.mm(ps[:, :nt], G["bones"][:], ysq[i2][:, :nt], True, True, [K("ysq")], [pk])
            P.act(var[i2][:, :nt], ps[:, :nt], AF.Ln, [pk], [K("var")], bias=G["gneps_t"][:, 0:1], scale=1.0 / 64)
            P.stt(tmp[i2][:, :nt], tmp[i2][:, :nt], G["vec"][:, VO["r_k"] + c:VO["r_k"] + c + 1], rr[i2][:, :nt], ALU.mult, ALU.mult,
                  [K("tmp"), K("RT")], [K("tmp")])
            yield
            P.act(var[i2][:, :nt], var[i2][:, :nt], AF.Exp, [K("var")], [K("var")], scale=-0.5)
            ps, pk = P.next_ps()
            P.mm(ps[:, :nt], G["bones"][:], tmp[i2][:, :nt], True, True, [K("tmp")], [pk])
            P.tt("dve", bon[i2][:, :nt], ps[:, :nt], vv[i2][:, :nt], ALU.mult, [pk, K("VT")], [K("bon")])
            yield
            P.tt("dve", y[:, :nt], y[:, :nt], var[i2][:, :nt], ALU.mult, [K("YT0"), K("var")], [K("YT0")])
            yield
            P.act(y[:, :nt], y[:, :nt], AF.Identity, [K("YT0")], [K("YT0")],
                  scale=G["vec"][:, VO["ln_g"] + c:VO["ln_g"] + c + 1], bias=G["vec"][:, VO["ln_b"] + c:VO["ln_b"] + c + 1])
            yield
            P.tt("pool", y[:, :nt], y[:, :nt], bon[i2][:, :nt], ALU.add, [K("YT0"), K("bon")], [K("YT0")])
            yield
            P.tt("pool", ao[i2][:, :nt], y[:, :nt], gg[i2][:, :nt], ALU.mult, [K("YT0"), K("GT")], [K("ao")])
            P.dma("act", T["AT"][sl[0], sl[1]], ao[i2][:, :nt], [K("ao")], ["AT"])
            yield

        its = [(tok0, nt, c) for (tok0, nt, b, is_ctx, pos0) in token_blocks(with_ctx=with_ctx) for c in range(KC)]
        for g0 in range(0, len(its), NI):
            alive = [body(i, *its[g0 + i]) for i in range(min(NI, len(its) - g0))]
            while alive:
                nxt = []
                for g_ in alive:
                    try:
                        next(g_)
                        nxt.append(g_)
                    except StopIteration:
                        pass
                alive = nxt


INPUT_NAMES = ["w_mod", "mlp_w_in", "mlp_w_out", "attn_w_qkv", "attn_w_o", "rwkv_w_rkv", "rwkv_w1", "rwkv_w2",
               "rwkv_a1", "rwkv_a2", "rwkv_g1", "rwkv_g2", "rwkv_w_o", "pool_w"]
INPUT_SHAPES = {"w_mod": [4, 1024, 6144], "mlp_w_in": [4, 1024, 4096], "mlp_w_out": [4, 4096, 1024],
                "attn_w_qkv": [2, 1024, 1536], "attn_w_o": [2, 1024, 1024], "rwkv_w_rkv": [1, 3, 1024, 1024],
                "rwkv_w1": [1, 2, 1024, 64], "rwkv_w2": [1, 2, 64, 1024], "rwkv_a1": [1, 2, 1024, 64],
                "rwkv_a2": [1, 2, 64, 1024], "rwkv_g1": [1, 1024, 160], "rwkv_g2": [1, 160, 1024],
                "rwkv_w_o": [1, 1024, 1024], "pool_w": [1, 4, 256, 256]}
CONST_SHAPES = {"cvT": [128, KC, 3], "c_vec": [128, NV], "c_ident": [128, 128], "c_ones": [128, 128], "c_bones": [128, 128],
                "c_rot": [128, 128], "c_cos": [128, LAT], "c_sin": [128, LAT], "c_mask": [128, 2, 256], "c_maskT": [128, 2, 128],
                "c_rcnt": [128, 4, LAT], "c_rcntc": [128, 4, CTXL], "c_v64": [64, 2, 16], "c_ka128": [128, KC],
                "c_small": [128, 4]}


def build_program(n_layers=DEPTH, dbg=None):
    nc = bass.Bass("TRN2", target_bir_lowering=False)
    T = {}
    T["x2"] = nc.dram_tensor("x2", [NB, LAT, D], F32, kind="ExternalInput").ap()
    T["ctx2"] = nc.dram_tensor("ctx2", [NB, CTXL, D], F32, kind="ExternalInput").ap()
    for n in INPUT_NAMES:
        T[n] = nc.dram_tensor(n, INPUT_SHAPES[n], F32, kind="ExternalInput").ap()
    for n, s in CONST_SHAPES.items():
        T[n] = nc.dram_tensor(n, s, F32, kind="ExternalInput").ap()
    T["y"] = nc.dram_tensor("y", [NB, LAT, D], F32, kind="ExternalOutput").ap()

    def scr(name, shape, dt=F32):
        kind = "ExternalOutput" if (dbg and name in dbg) else "Internal"
        T[name] = nc.dram_tensor(name, shape, dt, kind=kind).ap()

    scr("XT", [D, NTOK])
    scr("HT", [D, NTOK])
    scr("QT", [D, NTOK], BF16)
    scr("KTs", [256, NTOK], BF16)
    scr("Vs", [NB, 4, 128, NSC, 64], BF16)
    scr("AT", [D, NTOK], BF16)
    for n in ["RT", "KT", "VT", "GT", "SIG0", "SIG1", "AA0", "AA1", "YT0", "YT1"]:
        scr(n, [D, NTOK])

    with ExitStack() as es:
        P = Prog(nc, es)
        G = {}

        def gsb(name, shape, dt=F32):
            return es.enter_context(nc.sbuf_tensor(name, list(shape), dt))

        G["ident"] = gsb("ident", [128, 128])
        G["ones_f"] = gsb("ones_f", [128, 128])
        G["bones"] = gsb("bones", [128, 128])
        G["ones_b"] = gsb("ones_b", [128, 128], BF16)
        G["rot"] = gsb("rot", [128, 128])
        G["vec"] = gsb("vec", [128, NV])
        G["vec128_ka"] = gsb("ka128", [128, KC])
        small = gsb("small", [128, 4])
        G["eps_t"] = small[:, 0:1]
        G["tiny_t"] = small[:, 1:2]
        G["gneps_t"] = small[:, 2:3]
        G["sil"] = gsb("sil", [128, KC, 3])
        G["mod"] = [gsb("mod%d" % L, [128, 48, 3]) for L in range(DEPTH)]
        G["sc1"] = [gsb("sc1_%d" % L, [128, KC, 3]) for L in range(DEPTH)]
        G["sc2"] = [gsb("sc2_%d" % L, [128, KC, 3]) for L in range(DEPTH)]
        for (t, n) in [("ident", "c_ident"), ("ones_f", "c_ones"), ("bones", "c_bones"), ("rot", "c_rot"), ("vec", "c_vec"),
                       ("vec128_ka", "c_ka128")]:
            P.dma("sp", G[t][:], T[n], [], ["consts"])
        P.dma("sp", small[:], T["c_small"], [], ["consts"])
        P.dma("pool", G["ones_b"][:], T["c_ones"], [], ["consts"])
        P.barrier()

        stage_prologue(P, G, T)
        for L in range(n_layers):
            last = L == DEPTH - 1
            kind = L % 3
            j = L // 3
            wc = not last
            if kind == 0:
                stage_attn_qkv(P, G, T, L, j)
                stage_attn_core(P, G, T, wc, bg_layers=(list(range(1, n_layers)) if L == 0 else ()))
                stage_mix_mlp(P, G, T, L, T["attn_w_o"][j], "full", wc)
            elif kind == 1:
                stage_norm_to_HT(P, G, T, L, True)
                stage_rwkv_prep(P, G, T)
                stage_rwkv_scan(P, G, T)
                stage_rwkv_finish(P, G, T, wc)
                stage_mix_mlp(P, G, T, L, T["rwkv_w_o"][j], "full", wc)
            else:
                stage_norm_to_HT(P, G, T, L, wc)
                stage_pool(P, G, T, wc)
                stage_mix_mlp(P, G, T, L, T["pool_w"][j], "pool", wc)
        stage_epilogue(P, G, T)
        P.barrier()
        print("program: %d instructions, %d waits" % (P.n_inst, P.n_wait))
    return nc


def _pm(v, nch=KC):
    return np.ascontiguousarray(np.asarray(v, np.float32).reshape(nch, 128).T)


def host_consts(inp):
    c = {}
    vec = np.zeros((128, NV), np.float32)
    for L in range(DEPTH):
        vec[:, VO["bmod"] + L * 48:VO["bmod"] + (L + 1) * 48] = _pm(inp["b_mod"][L], 48)
        vec[:, VO["g1"] + L * 8:VO["g1"] + (L + 1) * 8] = _pm(inp["norm1_g"][L])
        vec[:, VO["g2"] + L * 8:VO["g2"] + (L + 1) * 8] = _pm(inp["norm2_g"][L])
    for j in range(2):
        vec[:, VO["qg"] + j] = np.tile(inp["attn_q_gain"][j], 2)
        vec[:, VO["kg"] + j] = np.tile(inp["attn_k_gain"][j], 2)
    vec[:, VO["pscale"]:VO["pscale"] + 8] = _pm(inp["pool_scale"][0])
    for m in range(6):
        vec[:, VO["mu"] + m * 8:VO["mu"] + (m + 1) * 8] = _pm(inp["rwkv_mu"][0, m])
    for d in range(2):
        vec[:, VO["w0"] + d * 8:VO["w0"] + (d + 1) * 8] = _pm(inp["rwkv_w0"][0, d])
        vec[:, VO["a0"] + d * 8:VO["a0"] + (d + 1) * 8] = _pm(inp["rwkv_a0"][0, d])
    vec[:, VO["r_k"]:VO["r_k"] + 8] = _pm(inp["rwkv_r_k"][0])
    vec[:, VO["ln_g"]:VO["ln_g"] + 8] = _pm(inp["rwkv_ln_g"][0])
    vec[:, VO["ln_b"]:VO["ln_b"] + 8] = _pm(inp["rwkv_ln_b"][0])
    c["c_vec"] = vec
    c["c_ka128"] = _pm(inp["rwkv_k_a"][0])
    v64 = np.zeros((64, 2, 16), np.float32)
    v64[:, 0, :] = np.asarray(inp["rwkv_k_k"][0], np.float32).reshape(16, 64).T
    v64[:, 1, :] = np.asarray(inp["rwkv_k_a"][0], np.float32).reshape(16, 64).T
    c["c_v64"] = v64
    c["c_ident"] = np.eye(128, dtype=np.float32)
    c["c_ones"] = np.ones((128, 128), np.float32)
    bo = np.zeros((128, 128), np.float32)
    bo[:64, :64] = 1.0
    bo[64:, 64:] = 1.0
    c["c_bones"] = bo
    rot = np.zeros((128, 128), np.float32)
    for hh in range(2):
        for jj in range(32):
            rot[hh * 64 + jj + 32, hh * 64 + jj] = -1.0
            rot[hh * 64 + jj, hh * 64 + jj + 32] = 1.0
    c["c_rot"] = rot
    nf = 16
    inv = (10000.0 ** (-np.arange(nf, dtype=np.float32) / nf)).astype(np.float32)
    t = np.arange(LAT)
    rows = (t // 64).astype(np.float32)
    cols = (t % 64).astype(np.float32)
    ang = np.concatenate([rows[:, None] * inv[None, :], cols[:, None] * inv[None, :]], axis=1).astype(np.float32)
    cosf = np.cos(ang).astype(np.float32).T
    sinf = np.sin(ang).astype(np.float32).T
    c["c_cos"] = np.ascontiguousarray(np.tile(cosf, (4, 1)))
    c["c_sin"] = np.ascontiguousarray(np.tile(sinf, (4, 1)))
    s = np.arange(128)[:, None]
    tt_ = np.arange(128)[None, :]
    mk = np.zeros((128, 2, 256), np.float32)
    mk[:, 0, 0:128] = (tt_ > s)
    mk[:, 0, 128:256] = (tt_ >= s)
    mk[:, 1, 0:128] = (tt_ < s)
    mk[:, 1, 128:256] = (tt_ <= s)
    c["c_mask"] = mk
    mkT = np.zeros((128, 2, 128), np.float32)
    mkT[:, 0, :] = (tt_ < s)
    mkT[:, 1, :] = (tt_ > s)
    c["c_maskT"] = mkT

    def rcnt(Tn):
        out = np.zeros((4, Tn), np.float32)
        tpos = np.arange(Tn)
        for gi, win in enumerate((2, 4, 8, 16)):
            lo = np.clip(tpos - win // 2, 0, Tn)
            hi = np.clip(tpos + win // 2, 0, Tn)
            out[gi] = 1.0 / (hi - lo).astype(np.float32)
        return out
    c["c_rcnt"] = np.ascontiguousarray(np.broadcast_to(rcnt(LAT)[None], (128, 4, LAT))).astype(np.float32)
    c["c_rcntc"] = np.ascontiguousarray(np.broadcast_to(rcnt(CTXL)[None], (128, 4, CTXL))).astype(np.float32)
    sm = np.zeros((128, 4), np.float32)
    sm[:, 0] = EPS
    sm[:, 1] = 1e-30
    sm[:, 2] = 64 * 1e-5
    c["c_small"] = sm
    return c


def make_in_maps(inp, cores):
    consts = host_consts(inp)
    shared = {n: np.ascontiguousarray(np.asarray(inp[n], np.float32)) for n in INPUT_NAMES}
    maps = []
    for i in cores:
        m = dict(shared)
        m.update(consts)
        m["x2"] = np.ascontiguousarray(inp["x"][NB * i:NB * (i + 1)], dtype=np.float32)
        m["ctx2"] = np.ascontiguousarray(inp["ctx"][NB * i:NB * (i + 1)], dtype=np.float32)
        cv = np.stack([inp["c"][NB * i], inp["c"][NB * i + 1], inp["c_ctx"]], axis=0).astype(np.float32)
        m["cvT"] = np.ascontiguousarray(cv.reshape(3, KC, 128).transpose(2, 1, 0))
        maps.append(m)
    return maps


_NC_CACHE = {}


def kernel(**inputs):
    inp = {k: np.asarray(v) for k, v in inputs.items()}
    if "nc" not in _NC_CACHE:
        _NC_CACHE["nc"] = build_program()
    nc = _NC_CACHE["nc"]
    in_maps = make_in_maps(inp, list(range(NCORES)))
    res = run_bass_kernel_spmd(nc, in_maps, core_ids=list(range(NCORES)))
    out = np.concatenate([np.asarray(r["y"]) for r in res.results], axis=0)
    return out.astype(np.float32)
```
